# Optimizing a Trainium2 kernel written in Bass

```python
import jax, jax.numpy as jnp
from jax import lax
import numpy as np

D_MODEL = 4096
BATCH = 4
SEQ = 4096
DEPTH = 2

N_MIXERS = 2
N_META = 16
GLA_HEADS = 4
GLA_DK = D_MODEL // 2 // GLA_HEADS
GLA_DV = D_MODEL // GLA_HEADS
GLA_QK = GLA_HEADS * GLA_DK
GLA_V = GLA_HEADS * GLA_DV
GLA_GATE_RANK = 16
GLA_GATE_TAU = 16.0
GLA_IN = 2 * GLA_QK + 2 * GLA_V + GLA_GATE_RANK
CHUNK = 64
CONV_WIDTH = 3
D_FF = -(-8 * D_MODEL // (3 * 256)) * 256
N_GLA = (DEPTH + 1) // 2
N_CONV = DEPTH // 2
EPS = 1e-6

kernel_name = "hybrid_gla_shortconv_meta_trunk"


def rms_norm(x, g):
    xf = x.astype(jnp.float32)
    y = xf * lax.rsqrt(jnp.mean(xf * xf, axis=-1, keepdims=True) + EPS)
    return (y * g.astype(jnp.float32)).astype(x.dtype)


def gla_mixer(h, w_in, w_a2, b_a, head_norm, w_out):
    B, L, _ = h.shape
    proj = h @ w_in
    q, k, v, r, a_low = jnp.split(proj, np.cumsum([GLA_QK, GLA_QK, GLA_V, GLA_V]).tolist(), axis=-1)
    log_a = jax.nn.log_sigmoid((a_low @ w_a2 + b_a).astype(jnp.float32)) / GLA_GATE_TAU
    q = q.astype(jnp.float32) * (GLA_DK ** -0.5)
    pad = CHUNK - N_META

    def to_chunks(t, d):
        t = jnp.pad(t.astype(jnp.float32), ((0, 0), (pad, 0), (0, 0)))
        n = t.shape[1] // CHUNK
        return t.reshape(B, n, CHUNK, GLA_HEADS, d).transpose(1, 0, 3, 2, 4)

    qc = to_chunks(q, GLA_DK)
    kc = to_chunks(k, GLA_DK)
    vc = to_chunks(v, GLA_DV)
    gc = to_chunks(log_a, GLA_DK)
    causal = jnp.tril(jnp.ones((CHUNK, CHUNK), dtype=bool))[None, None, :, :, None]

    def step(S, inp):
        qi, ki, vi, gi = inp
        b = jnp.cumsum(gi, axis=2)
        b_last = b[:, :, -1:, :]
        o_inter = jnp.einsum('bhik,bhkv->bhiv', qi * jnp.exp(b), S)
        diff = jnp.where(causal, b[:, :, :, None, :] - b[:, :, None, :, :], -jnp.inf)
        scores = jnp.sum(qi[:, :, :, None, :] * ki[:, :, None, :, :] * jnp.exp(diff), axis=-1)
        o_intra = jnp.einsum('bhij,bhjv->bhiv', scores, vi)
        S_new = jnp.exp(b_last[:, :, 0, :])[..., None] * S + jnp.einsum('bhjk,bhjv->bhkv', ki * jnp.exp(b_last - b), vi)
        return S_new, o_inter + o_intra

    S0 = jnp.zeros((B, GLA_HEADS, GLA_DK, GLA_DV), jnp.float32)
    _, o = lax.scan(step, S0, (qc, kc, vc, gc))
    o = o.transpose(1, 0, 3, 2, 4).reshape(B, -1, GLA_HEADS, GLA_DV)[:, pad:]
    o = o * lax.rsqrt(jnp.mean(o * o, axis=-1, keepdims=True) + EPS) * head_norm.astype(jnp.float32)
    o = o * jax.nn.silu(r.astype(jnp.float32)).reshape(B, L, GLA_HEADS, GLA_DV)
    return o.reshape(B, L, GLA_V).astype(h.dtype) @ w_out


def conv_mixer(h, w_in, conv_w, w_out):
    L = h.shape[1]
    bg, cg, u = jnp.split(h @ w_in, 3, axis=-1)
    z = cg * u
    zp = jnp.pad(z, ((0, 0), (CONV_WIDTH - 1, 0), (0, 0)))
    conv = conv_w[CONV_WIDTH - 1] * z
    for j in range(CONV_WIDTH - 1):
        conv = conv + conv_w[j] * zp[:, j:j + L]
    return (bg * conv) @ w_out


def swiglu(h, w_gate, w_up, w_down):
    return (jax.nn.silu(h @ w_gate) * (h @ w_up)) @ w_down


def setup_inputs(seed: int = 0) -> dict:
    key = jax.random.key(seed)
    ks = jax.random.split(key, 20)
    nrm = jax.random.normal
    f = jnp.float32
    return {
        "x": nrm(ks[0], (BATCH, SEQ, D_MODEL), f),
        "meta": nrm(ks[1], (N_META, D_MODEL), f),
        "norm_mix": 1.0 + 0.01 * nrm(ks[2], (DEPTH, D_MODEL), f),
        "norm_ffn": 1.0 + 0.01 * nrm(ks[3], (DEPTH, D_MODEL), f),
        "gla_w_in": nrm(ks[4], (N_GLA, D_MODEL, GLA_IN), f) * D_MODEL ** -0.5,
        "gla_w_a2": nrm(ks[5], (N_GLA, GLA_GATE_RANK, GLA_QK), f) * GLA_GATE_RANK ** -0.5,
        "gla_b_a": 0.01 * nrm(ks[6], (N_GLA, GLA_QK), f),
        "gla_head_norm": 1.0 + 0.01 * nrm(ks[7], (N_GLA, GLA_DV), f),
        "gla_w_out": nrm(ks[8], (N_GLA, GLA_V, D_MODEL), f) * GLA_V ** -0.5,
        "conv_w_in": nrm(ks[9], (N_CONV, D_MODEL, 3 * D_MODEL), f) * D_MODEL ** -0.5,
        "conv_w": nrm(ks[10], (N_CONV, CONV_WIDTH, D_MODEL), f) * CONV_WIDTH ** -0.5,
        "conv_w_out": nrm(ks[11], (N_CONV, D_MODEL, D_MODEL), f) * D_MODEL ** -0.5,
        "ffn_w_gate": nrm(ks[12], (DEPTH, D_MODEL, D_FF), f) * D_MODEL ** -0.5,
        "ffn_w_up": nrm(ks[13], (DEPTH, D_MODEL, D_FF), f) * D_MODEL ** -0.5,
        "ffn_w_down": nrm(ks[14], (DEPTH, D_FF, D_MODEL), f) * D_FF ** -0.5,
        "norm_final": 1.0 + 0.01 * nrm(ks[15], (D_MODEL,), f),
    }


def reference(x, meta, norm_mix, norm_ffn, gla_w_in, gla_w_a2, gla_b_a, gla_head_norm, gla_w_out,
              conv_w_in, conv_w, conv_w_out, ffn_w_gate, ffn_w_up, ffn_w_down, norm_final):
    B = x.shape[0]
    h = jnp.concatenate([jnp.broadcast_to(meta.astype(x.dtype)[None], (B, N_META, D_MODEL)), x], axis=1)
    for i in range(DEPTH):
        hn = rms_norm(h, norm_mix[i])
        j = i // N_MIXERS
        if i % N_MIXERS == 0:
            h = h + gla_mixer(hn, gla_w_in[j], gla_w_a2[j], gla_b_a[j], gla_head_norm[j], gla_w_out[j])
        else:
            h = h + conv_mixer(hn, conv_w_in[j], conv_w[j], conv_w_out[j])
        h = h + swiglu(rms_norm(h, norm_ffn[i]), ffn_w_gate[i], ffn_w_up[i], ffn_w_down[i])
    return rms_norm(h, norm_final)[:, N_META:]
```

```python
import numpy as np
from contextlib import ExitStack

import concourse.bass as bass
import concourse.mybir as mybir
from concourse.bass_utils import run_bass_kernel_spmd

F32 = mybir.dt.float32
BF16 = mybir.dt.bfloat16
AF = mybir.ActivationFunctionType
ALU = mybir.AluOpType

D = 4096
KC = D // 128
T = 2112
TP = 2048
CH = 64
NCH = T // CH
NCHP = TP // CH
CS = 128
NCT = TP // CS + 1 + (T - CH) // CS
DFF = 11008
FC = DFF // 128
H = 4
DK = 512
DV = 1024
QK = H * DK
VV = H * DV
GIN = 2 * QK + 2 * VV + 16
EPS = 1e-6
NOUT = 2048


def tok_tiles(n):
    r = []
    t = 0
    while t < n:
        s = min(128, n - t)
        r.append((t, s))
        t += s
    return r


def tok_blocks(n):
    r = []
    t = 0
    while t < n:
        s = min(512, n - t)
        r.append((t, s))
        t += s
    return r


class Sem:
    def __init__(self, h, name):
        self.h = h
        self.n = 0
        self.name = name


class Slot:
    def __init__(self, ap=None):
        self.ap = ap
        self.ready = []
        self.free = []


class Prog:
    ENG = ("sp", "act", "pool", "pe", "dve")

    def __init__(self, nc, es):
        self.nc = nc
        self.es = es
        self.q = {e: [] for e in self.ENG}
        self.waited = {e: {} for e in self.ENG}
        self.nsem = 0
        self.cnt = {e: 0 for e in self.ENG}
        self.auto = {}

    def sem(self, name):
        h = self.es.enter_context(self.nc.semaphore(name))
        self.nsem += 1
        return Sem(h, name)

    def wait(self, e, deps):
        best = {}
        for (s, v) in deps:
            if v > best.get(s.name, (None, 0))[1]:
                best[s.name] = (s, v)
        w = self.waited[e]
        for name, (s, v) in best.items():
            if w.get(name, 0) >= v:
                continue
            w[name] = v
            self.q[e].append(("w", s.h, v))

    def op(self, e, method, *args, inc="auto", deps=None, **kw):
        if deps:
            self.wait(e, deps)
        val = None
        amt = 0
        if inc == "auto":
            inc = self.auto.get(e) if method != "dma_start" else None
        if inc is not None:
            amt = 16 if method == "dma_start" else 1
            inc.n += amt
            val = (inc, inc.n)
        self.q[e].append(("o", method, args, kw, inc.h if inc is not None else None, amt))
        self.cnt[e] += 1
        return val

    def run(self, e, eng):
        for it in self.q[e]:
            if it[0] == "w":
                eng.wait_ge(it[1], it[2])
            else:
                _, method, args, kw, sh, amt = it
                ins = getattr(eng, method)(*args, **kw)
                if sh is not None:
                    ins.then_inc(sh, amt)


class K:
    pass


def build(stages, ext_in=(), ext_out=(), debug_T=None):
    nc = bass.Bass("TRN2", target_bir_lowering=False)
    es = ExitStack()
    P = Prog(nc, es)
    k = K()
    k.nc, k.P = nc, P

    def dram(name, shape, dt=F32, inp=False, out=False):
        if inp or name in ext_in:
            kind = "ExternalInput"
        elif out or name in ext_out:
            kind = "ExternalOutput"
        else:
            kind = "Internal"
        return nc.dram_tensor(name, list(shape), dt, kind=kind).ap()

    g = {}
    g["xo"] = dram("xo", [T, D], inp=True)
    g["xp"] = dram("xp", [TP, D], inp=True)
    g["w_in0"] = dram("w_in0", [D, GIN], inp=True)
    g["w_a2"] = dram("w_a2", [16, QK], inp=True)
    g["b_a_t"] = dram("b_a_t", [128, 16], inp=True)
    g["hnorm"] = dram("hnorm", [DV], inp=True)
    g["w_out0"] = dram("w_out0", [VV, D], inp=True)
    g["cw_in"] = dram("cw_in", [D, 3 * D], inp=True)
    g["cw_t"] = dram("cw_t", [128, 3 * KC], inp=True)
    g["cw_out"] = dram("cw_out", [D, D], inp=True)
    for l in range(2):
        g[f"wg{l}"] = dram(f"wg{l}", [D, DFF], inp=True)
        g[f"wu{l}"] = dram(f"wu{l}", [D, DFF], inp=True)
        g[f"wd{l}"] = dram(f"wd{l}", [DFF, D], inp=True)
    g["nmix"] = dram("nmix", [2, D], inp=True)
    g["nffn"] = dram("nffn", [2, D], inp=True)
    g["nfin"] = dram("nfin", [D], inp=True)
    g["ident"] = dram("ident", [128, 128], inp=True)
    g["tril"] = dram("tril", [CS, CS], inp=True)
    g["h"] = dram("h", [T, D])
    g["qT"] = dram("qT", [QK, T], BF16)
    g["kT"] = dram("kT", [QK, T], BF16)
    g["kh"] = dram("kh", [TP + T, QK], BF16)
    g["v"] = dram("v", [TP + T, VV], BF16)
    g["sr"] = dram("sr", [T, VV], BF16)
    g["ogT"] = dram("ogT", [VV, T], BF16)
    g["yT"] = dram("yT", [D, T], BF16)
    g["hidT"] = dram("hidT", [DFF, T], BF16)
    g["out"] = dram("out", [NOUT, D], out=True)
    k.g = g

    A_t = nc.alloc_sbuf_tensor("A", [128, KC * T], BF16)
    WR_t = nc.alloc_sbuf_tensor("WR", [128, 32768], BF16)
    k.A = A_t[:, :]
    k.WR = WR_t[:, :]
    k.A3 = k.A.rearrange("p (c t) -> p c t", t=T)
    k.identb = nc.alloc_sbuf_tensor("identb", [128, 128], BF16)[:, :]
    k.tril = nc.alloc_sbuf_tensor("trilm", [CS, CS], F32)[:, :]
    k.decay = nc.alloc_sbuf_tensor("decay", [128, 16 * NCT], F32)[:, :]
    k.nba = nc.alloc_sbuf_tensor("nba", [128, 16], F32)[:, :]
    k.cw = nc.alloc_sbuf_tensor("cw", [128, 3 * KC], F32)[:, :]
    k.ss = nc.alloc_sbuf_tensor("ss", [128, 20], F32)[:, :]
    k.sd = nc.alloc_sbuf_tensor("sd", [128, 20], F32)[:, :]
    k.rstd = nc.alloc_sbuf_tensor("rstd", [128, 20], F32)[:, :]
    k.epsb = nc.alloc_sbuf_tensor("epsb", [128, 1], F32)[:, :]
    k.oneb = nc.alloc_sbuf_tensor("oneb", [128, 1], F32)[:, :]
    xr_cols = (nc.sbuf_bytes_remaining - 64) // 2
    xr_cols = (xr_cols // 16) * 16
    XR_t = nc.alloc_sbuf_tensor("XR", [128, xr_cols], BF16)
    k.XR = XR_t[:, :]
    k.xr_cols = xr_cols

    k.banks = []
    psall = nc.alloc_psum_tensor("psall", [128, 4096], F32)
    k.psall = psall[:, :]
    for b in range(8):
        s = Slot(k.psall[:, b * 512:(b + 1) * 512])
        s.idx = b
        k.banks.append(s)
    k.bank_i = 0

    k.S_pe = P.sem("S_pe")
    k.S_dve = P.sem("S_dve")
    k.S_act = P.sem("S_act")
    k.S_pool = P.sem("S_pool")
    k.s_w = [P.sem(f"s_w{i}") for i in range(4)]
    k.s_ld = [P.sem(f"s_ld{i}") for i in range(6)]
    k.s_st = [P.sem(f"s_st{i}") for i in range(6)]
    k.s_a = [P.sem(f"s_a{i}") for i in range(8)]
    k.A_kc = None
    k.s_misc = P.sem("s_misc")
    k.s_gain = P.sem("s_gain")
    k.s_cp = P.sem("s_cp")
    P.auto = {"act": k.S_act, "dve": k.S_dve, "pool": k.S_pool}

    k.Aslot = Slot(k.A)
    k.Wslot = [Slot(), Slot()]
    k.dram_ready = {}

    from_stages(k, stages)

    with nc.Block() as block:
        @block.sync
        def _(e):
            P.run("sp", e)

        @block.scalar
        def _(e):
            P.run("act", e)

        @block.gpsimd
        def _(e):
            P.run("pool", e)

        @block.tensor
        def _(e):
            P.run("pe", e)

        @block.vector
        def _(e):
            P.run("dve", e)
    es.close()
    return nc


def k_tiles(k, ntok):
    if getattr(k, "skip_halo", False) and ntok == T:
        return [(CH + 128 * i, 128) for i in range((T - CH) // 128)]
    return tok_tiles(ntok)


def k_blocks(k, ntok):
    if getattr(k, "skip_halo", False) and ntok == T:
        return [(CH + 512 * i, 512) for i in range((T - CH) // 512)]
    return tok_blocks(ntok)


def look_free(k, n=4):
    i = k.bank_i
    if i % n != 0:
        return []
    deps = []
    for j in range(n):
        deps += k.banks[(i + j) % 8].free
    return deps


def next_bank(k):
    b = k.banks[k.bank_i % 8]
    k.bank_i += 1
    return b


def wr_f32(k, c0, n):
    return k.WR[:, c0:c0 + 2 * n].bitcast(F32)


def xr_f32(k, c0, n):
    return k.XR[:, c0:c0 + 2 * n].bitcast(F32)


def st_setup(k):
    P, g = k.P, k.g
    tmp = wr_f32(k, 0, 128)
    v1 = P.op("sp", "dma_start", out=tmp, in_=g["ident"], inc=k.s_misc)
    v2 = P.op("sp", "dma_start", out=k.tril, in_=g["tril"], inc=k.s_misc)
    v3 = P.op("sp", "dma_start", out=k.nba, in_=g["b_a_t"], inc=k.s_misc)
    v4 = P.op("sp", "dma_start", out=k.cw, in_=g["cw_t"], inc=k.s_misc)
    P.op("dve", "tensor_copy", out=k.identb, in_=tmp, deps=[v4])
    P.op("dve", "tensor_scalar", out=k.nba, in0=k.nba, scalar1=-1.0, scalar2=None, op0=ALU.mult, deps=[v3])
    P.op("dve", "memset", k.epsb, EPS)
    P.op("dve", "memset", k.oneb, 1.0)
    vv = P.op("dve", "memset", k.decay, 1.0, inc=k.S_dve)
    k.setup_done = [vv]
    k.Wslot[0].free = [vv]
    k.Wslot[1].free = [vv]


def st_norm_T(k, src, gain_row, ntok, src_deps=()):
    P = k.P
    xb = [Slot(wr_f32(k, 0, 4096)), Slot(wr_f32(k, 8192, 4096))]
    gbc = wr_f32(k, 16384, 4096)
    hnb = [Slot(k.WR[:, 24576:28672]), Slot(k.WR[:, 28672:32768])]
    wfree = k.Wslot[0].free + k.Wslot[1].free + list(k.setup_done)
    for s in xb + hnb:
        s.free = list(wfree)
    gv = P.op("sp", "dma_start", out=gbc, in_=gain_row.partition_broadcast(128), inc=k.s_gain,
              deps=wfree)
    tiles = k_tiles(k, ntok)
    a_ready = []
    st = {"last_pe": None, "ev_i": 0}

    def back_half(tt, t0, ts, dv):
        hb = hnb[tt % 2]
        pv = None
        for q in range(4):
            bank = next_bank(k)
            pb = bank.ap.bitcast(BF16)
            P.wait("pe", [dv] + bank.free)
            for kk in range(8):
                kc = q * 8 + kk
                pv = P.op("pe", "transpose", out=pb[:, kk * 128:kk * 128 + ts],
                          in_=hb.ap[:ts, kc * 128:(kc + 1) * 128], identity=k.identb[:ts, :ts],
                          inc=(k.S_pe if kk == 7 else None))
            eng = "act" if st["ev_i"] % 2 == 0 else "dve"
            st["ev_i"] += 1
            src_v = pb.rearrange("p (a b) -> p a b", b=128)[:, :, :ts]
            dst_v = k.A3[:, q * 8:(q + 1) * 8, t0:t0 + ts]
            if eng == "act":
                ev = P.op("act", "activation", out=dst_v, in_=src_v, func=AF.Copy,
                          deps=[pv] + k.Aslot.free)
            else:
                ev = P.op("dve", "tensor_copy", out=dst_v, in_=src_v, deps=[pv] + k.Aslot.free)
            bank.free = [ev]
            a_ready.append(ev)
        hb.free = [pv]
        st["last_pe"] = pv

    prev = None
    for tt, (t0, ts) in enumerate(tiles):
        b = tt % 2
        x, hb = xb[b], hnb[b]
        lv = P.op("sp", "dma_start", out=x.ap[:ts], in_=src[t0:t0 + ts, :], inc=k.s_ld[b],
                  deps=list(x.free) + list(src_deps))
        sq = P.op("act", "activation", out=hb.ap[:ts], in_=x.ap[:ts], func=AF.Square,
                  accum_out=k.ss[:ts, tt:tt + 1], deps=[lv] + hb.free)
        av = P.op("act", "activation", out=k.sd[:ts, tt:tt + 1], in_=k.ss[:ts, tt:tt + 1], func=AF.Sqrt,
                  scale=1.0 / D, bias=k.epsb[:ts], deps=[sq])
        rv = P.op("dve", "reciprocal", out=k.rstd[:ts, tt:tt + 1], in_=k.sd[:ts, tt:tt + 1], deps=[av, gv])
        dv = P.op("dve", "scalar_tensor_tensor", out=hb.ap[:ts], in0=x.ap[:ts],
                  scalar=k.rstd[:ts, tt:tt + 1], in1=gbc[:ts], op0=ALU.mult, op1=ALU.mult, deps=[rv])
        x.free = [dv]
        if prev is not None:
            back_half(*prev)
        prev = (tt, t0, ts, dv)
    back_half(*prev)
    last_pe = st["last_pe"]
    k.Aslot.ready = a_ready[-2:]
    k.A_kc = None
    k.Aslot.free = []
    k.Wslot[0].free = [last_pe, a_ready[-1], a_ready[-2]]
    k.Wslot[1].free = [last_pe, a_ready[-1], a_ready[-2]]


def a_tok_deps(k, t0, ts):
    if k.A_kc is None:
        return []
    return [v for (a, b, v) in k.A_kc if a < t0 + ts and b > t0]


def load_A(k, src, kcn, src_deps):
    P = k.P
    view = src.rearrange("(c p) t -> p c t", p=128)
    k.A_kc = []
    for i, (t0, ts) in enumerate(tok_blocks(T)):
        v = None
        for c0 in range(0, kcn, 16):
            c1 = min(kcn, c0 + 16)
            v = P.op("sp", "dma_start", out=k.A3[:, c0:c1, t0:t0 + ts], in_=view[:, c0:c1, t0:t0 + ts],
                     inc=k.s_a[i], deps=list(k.Aslot.free) + list(src_deps))
        k.A_kc.append((t0, t0 + ts, v))
    k.Aslot.ready = []
    k.Aslot.free = []


def gemm_A(k, kcn, wsrc, ncols_total, ntok, epilogue, col_tile=512):
    P = k.P
    wv = wsrc.rearrange("(c p) n -> p c n", p=128)
    tiles = k_tiles(k, ntok)
    last = None
    for nt in range(ncols_total // col_tile):
        c0 = nt * col_tile
        ws = k.Wslot[nt % 2]
        wb = k.WR[:, (nt % 2) * 16384:(nt % 2) * 16384 + kcn * col_tile].rearrange("p (c n) -> p c n", n=col_tile)
        step = 8
        for kc0 in range(0, kcn, step):
            kc1 = min(kcn, kc0 + step)
            wl = P.op("pool", "dma_start", out=wb[:, kc0:kc1, :], in_=wv[:, kc0:kc1, c0:c0 + col_tile],
                      inc=k.s_w[nt % 2], deps=ws.free)
        ws.ready = [wl]
        for tt, (t0, ts) in enumerate(tiles):
            la = look_free(k)
            bank = next_bank(k)
            P.wait("pe", k.Aslot.ready + ws.ready + la + bank.free + a_tok_deps(k, t0, ts))
            for kc in range(kcn):
                pv = P.op("pe", "matmul", bank.ap[:ts, :col_tile], k.A3[:, kc, t0:t0 + ts], wb[:, kc, :],
                          start=(kc == 0), stop=(kc == kcn - 1),
                          inc=(k.S_pe if kc == kcn - 1 else None))
            epilogue(nt, c0, tt, t0, ts, bank, pv)
            last = pv
        ws.free = [last]
    k.Aslot.free = [last]
    return last


def gemm_B(k, kcn, jobs, ntok, ep_block, ep_row=None, wbase=0, wslots=2, mrows=128, pre_block=None,
           ksplit=1, blocks=None):
    P = k.P
    if blocks is None:
        blocks = k_blocks(k, ntok)
    ng = len(jobs[0])
    kcp = kcn // ksplit
    assert kcp * ksplit == kcn
    last = None
    wsl = [Slot() for _ in range(wslots)]
    for s in wsl:
        s.free = k.Wslot[0].free + k.Wslot[1].free
    psz = ng * kcp * 128
    for j, job in enumerate(jobs):
        pieces = []
        for p in range(ksplit):
            pi = j * ksplit + p
            ws = wsl[pi % wslots]
            base = wbase + (pi % wslots) * psz
            wb = k.WR[:, base:base + psz].rearrange("p (g c n) -> p g c n", g=ng, n=128)
            wl = None
            for gi, (wv, c0) in enumerate(job):
                for kc0 in range(0, kcp, 8):
                    kc1 = min(kcp, kc0 + 8)
                    wl = P.op("pool", "dma_start", out=wb[:, gi, kc0:kc1, :mrows],
                              in_=wv[:, p * kcp + kc0:p * kcp + kc1, c0:c0 + mrows],
                              inc=k.s_w[pi % wslots], deps=ws.free)
            ws.ready = [wl]
            pieces.append((ws, wb))
        for bi, (t0, ts) in enumerate(blocks):
            xb = pre_block(j, bi, t0, ts) if pre_block is not None else []
            la = []
            banks = []
            for _ in range(ng):
                la += look_free(k)
                banks.append(next_bank(k))
            for b_ in banks:
                la += b_.free
            for p in range(ksplit):
                ws, wb = pieces[p]
                P.wait("pe", ws.ready)
                for gi in range(ng):
                    if p == 0:
                        P.wait("pe", k.Aslot.ready + (la if gi == 0 else []) + banks[gi].free + a_tok_deps(k, t0, ts))
                    for kk in range(kcp):
                        kc = p * kcp + kk
                        pv = P.op("pe", "matmul", banks[gi].ap[:mrows, :ts], wb[:, gi, kk, :mrows],
                                  k.A3[:, kc, t0:t0 + ts], start=(kc == 0), stop=(kc == kcn - 1),
                                  inc=(k.S_pe if (kc == kcn - 1 and gi == ng - 1) else None))
            ep_block(j, bi, t0, ts, xb + banks, pv)
            last = pv
        for ws, _ in pieces:
            ws.free = [last]
        if ep_row is not None:
            ep_row(j)
    k.Aslot.free = [last]
    k.Wslot[0].free = [last]
    k.Wslot[1].free = [last]
    return last


def st_resid_gemm(k, name, a_src, kcn, wsrc, h_src, h_dst, a_deps=None, from_A=False):
    P, g = k.P, k.g
    if not from_A:
        load_A(k, a_src, kcn, a_deps or [])
    nslot = 3
    hb = [Slot(k.XR[:, i * 1024:(i + 1) * 1024].bitcast(F32)) for i in range(nslot)]
    for s in hb:
        s.free = list(k.xr_free)
    st_vals = [None] * nslot
    cnt = [0]
    hdeps = list(k.dram_ready.get("h", []))

    def ep(nt, c0, tt, t0, ts, bank, pv):
        i = cnt[0] % nslot
        cnt[0] += 1
        s = hb[i]
        lv = P.op("sp", "dma_start", out=s.ap[:ts], in_=h_src[t0:t0 + ts, c0:c0 + 512], inc=k.s_ld[i],
                  deps=s.free + hdeps)
        dv = P.op("dve", "tensor_tensor", out=s.ap[:ts], in0=bank.ap[:ts, :], in1=s.ap[:ts], op=ALU.add,
                  inc=k.S_dve, deps=[pv, lv])
        bank.free = [dv]
        sv = P.op("sp", "dma_start", out=h_dst[t0:t0 + ts, c0:c0 + 512], in_=s.ap[:ts], inc=k.s_st[i],
                  deps=[dv])
        s.free = [sv]
        st_vals[i] = sv

    gemm_A(k, kcn, wsrc, D, T, ep)
    k.dram_ready["h"] = [v for v in st_vals if v is not None]
    k.xr_free = [v for v in st_vals if v is not None]


def st_ffn(k, l):
    P, g = k.P, k.g
    st_norm_T(k, g["h"], g["nffn"][l], T, src_deps=k.dram_ready.get("h", []))
    wgv = g[f"wg{l}"].rearrange("(c p) n -> p c n", p=128)
    wuv = g[f"wu{l}"].rearrange("(c p) n -> p c n", p=128)
    jobs = [[(wgv, j * 128), (wuv, j * 128)] for j in range(FC)]
    tmp = [Slot(wr_f32(k, 16384 + i * 1024, 512)) for i in range(2)]
    rows = [Slot(k.WR[:, 20480 + i * 2304:20480 + i * 2304 + T]) for i in range(2)]
    base_free = k.Wslot[0].free + k.Wslot[1].free
    for s in tmp + rows:
        s.free = list(base_free)
    cnt = [0]
    row_st = [None, None]
    hid_deps = list(k.dram_ready.get("hidT_free", []))

    def epb(j, bi, t0, ts, banks, pv):
        i = cnt[0] % 2
        cnt[0] += 1
        tm = tmp[i]
        row = rows[j % 2]
        av = P.op("act", "activation", out=tm.ap[:, :ts], in_=banks[0].ap[:, :ts], func=AF.Silu, inc=k.S_act,
                  deps=[pv] + tm.free)
        dv = P.op("dve", "tensor_tensor", out=row.ap[:, t0:t0 + ts], in0=banks[1].ap[:, :ts], in1=tm.ap[:, :ts],
                  op=ALU.mult, inc=k.S_dve, deps=[av] + row.free)
        tm.free = [dv]
        banks[0].free = [dv]
        banks[1].free = [dv]
        row.last = dv

    def epr(j):
        row = rows[j % 2]
        sv = P.op("sp", "dma_start", out=g["hidT"][j * 128:(j + 1) * 128, :], in_=row.ap, inc=k.s_st[j % 2],
                  deps=[row.last] + hid_deps)
        row.free = [sv]
        row_st[j % 2] = sv

    last = gemm_B(k, KC, jobs, T, epb, epr, wbase=0, wslots=2)
    k.dram_ready["hidT"] = [v for v in row_st if v is not None]
    k.Wslot[0].free = [last] + k.dram_ready["hidT"]
    k.Wslot[1].free = [last] + k.dram_ready["hidT"]
    parts = [(0, 29), (29, 29), (58, 28)]
    for (c0, cn) in parts:
        st_resid_gemm(k, f"down{l}", g["hidT"][c0 * 128:(c0 + cn) * 128, :], cn,
                      g[f"wd{l}"][c0 * 128:(c0 + cn) * 128, :], g["h"], g["h"],
                      a_deps=k.dram_ready["hidT"])
    k.dram_ready["hidT_free"] = [k.Aslot.free[0]]


def st_conv(k):
    P, g = k.P, k.g
    st_norm_T(k, g["h"], g["nmix"][1], T, src_deps=k.dram_ready.get("h", []))
    wv = g["cw_in"].rearrange("(c p) n -> p c n", p=128)
    jobs = [[(wv, j * 128), (wv, D + j * 128), (wv, 2 * D + j * 128)] for j in range(KC)]
    o = 18432
    z = k.WR[:, o:o + 2 * (T + 2)].bitcast(F32)
    o += 2 * (T + 2) + 12
    bgr = k.WR[:, o:o + 2 * T].bitcast(F32)
    o += 2 * T
    cr = k.WR[:, o:o + 2 * T].bitcast(F32)
    o += 2 * T
    assert o <= 32768, o
    tmp = [Slot(k.XR[:, T + i * 1024:T + (i + 1) * 1024].bitcast(F32)) for i in range(1)]
    assert T + 1024 <= k.xr_cols
    yrow = [Slot(k.XR[:, 0:T])]
    base_free = k.Wslot[0].free + k.Wslot[1].free
    for s in tmp:
        s.free = list(base_free)
    yrow[0].free = list(k.xr_free)
    zs = Slot(z)
    zs.free = list(base_free)
    cnt = [0]
    st = {"zlast": None, "sv": None, "alast": None}
    mz = P.op("dve", "memset", z[:, 0:2], 0.0, deps=base_free)

    def epb(j, bi, t0, ts, banks, pv):
        i = 0
        cnt[0] += 1
        tm = tmp[i]
        P.op("act", "activation", out=tm.ap[:, :ts], in_=banks[1].ap[:, :ts], func=AF.Copy,
             deps=[pv] + tm.free + zs.free)
        av = P.op("act", "activation", out=bgr[:, t0:t0 + ts], in_=banks[0].ap[:, :ts], func=AF.Copy)
        st["alast"] = av
        dv = P.op("dve", "tensor_tensor", out=z[:, 2 + t0:2 + t0 + ts], in0=banks[2].ap[:, :ts], in1=tm.ap[:, :ts],
                  op=ALU.mult, inc=k.S_dve, deps=[av] + zs.free)
        tm.free = [dv]
        for b in banks:
            b.free = [dv]
        st["zlast"] = dv

    def epr(j):
        y = yrow[0]
        c1 = P.op("dve", "tensor_scalar", out=cr, in0=z[:, 2:2 + T], scalar1=k.cw[:, 2 * KC + j:2 * KC + j + 1],
                  scalar2=None, op0=ALU.mult, deps=y.free + [st["zlast"]])
        c2 = P.op("dve", "scalar_tensor_tensor", out=cr, in0=z[:, 1:1 + T], scalar=k.cw[:, KC + j:KC + j + 1],
                  in1=cr, op0=ALU.mult, op1=ALU.add, deps=[c1])
        c3 = P.op("dve", "scalar_tensor_tensor", out=cr, in0=z[:, 0:T], scalar=k.cw[:, j:j + 1],
                  in1=cr, op0=ALU.mult, op1=ALU.add, deps=[c2])
        dv = P.op("dve", "tensor_tensor", out=y.ap, in0=cr, in1=bgr, op=ALU.mult, deps=[c3, st["alast"]])
        zs.free = [dv]
        sv = P.op("sp", "dma_start", out=g["yT"][j * 128:(j + 1) * 128, :], in_=y.ap, inc=k.s_st[2], deps=[dv])
        y.free = [sv]
        st["sv"] = sv

    last = gemm_B(k, KC, jobs, T, epb, epr, wbase=0, wslots=3, ksplit=2)
    k.dram_ready["yT"] = [st["sv"]]
    k.skip_halo = True
    k.xr_free = [st["sv"]]
    k.Wslot[0].free = [last, st["sv"]]
    k.Wslot[1].free = [last, st["sv"]]
    st_resid_gemm(k, "cw_out", g["yT"], KC, g["cw_out"], g["h"], g["h"], a_deps=k.dram_ready["yT"])


def st_final(k):
    P, g = k.P, k.g
    xb = [Slot(wr_f32(k, 0, 4096)), Slot(wr_f32(k, 8192, 4096))]
    gbc = wr_f32(k, 16384, 4096)
    junk = k.WR[:, 24576:28672]
    wfree = k.Wslot[0].free + k.Wslot[1].free
    for s in xb:
        s.free = list(wfree)
    gv = P.op("sp", "dma_start", out=gbc, in_=g["nfin"].partition_broadcast(128), inc=k.s_gain, deps=wfree)
    hdeps = list(k.dram_ready.get("h", []))
    svs = [None, None]
    for tt in range(NOUT // 128):
        t0 = CH + tt * 128
        b = tt % 2
        x = xb[b]
        lv = P.op("sp", "dma_start", out=x.ap, in_=g["h"][t0:t0 + 128, :], inc=k.s_ld[b], deps=x.free + hdeps)
        sq = P.op("act", "activation", out=junk, in_=x.ap, func=AF.Square, accum_out=k.ss[:, tt:tt + 1],
                  deps=[lv] + wfree)
        av = P.op("act", "activation", out=k.sd[:, tt:tt + 1], in_=k.ss[:, tt:tt + 1], func=AF.Sqrt,
                  scale=1.0 / D, bias=k.epsb, deps=[sq])
        rv = P.op("dve", "reciprocal", out=k.rstd[:, tt:tt + 1], in_=k.sd[:, tt:tt + 1], deps=[av, gv])
        dv = P.op("dve", "scalar_tensor_tensor", out=x.ap, in0=x.ap, scalar=k.rstd[:, tt:tt + 1], in1=gbc,
                  op0=ALU.mult, op1=ALU.mult, deps=[rv])
        sv = P.op("sp", "dma_start", out=g["out"][tt * 128:(tt + 1) * 128, :], in_=x.ap, inc=k.s_st[b], deps=[dv])
        x.free = [sv]
        svs[b] = sv
    P.wait("sp", [v for v in svs if v is not None])


def st_copy_h(k):
    P, g = k.P, k.g
    vals = []
    for i in range(0, T, 264):
        v = P.op("sp", "dma_start", out=g["h"][i:i + 264, :], in_=g["xo"][i:i + 264, :], inc=k.s_cp)
        vals.append(v)
    k.dram_ready["h"] = [vals[-1]]


def from_stages(k, stages):
    k.xr_free = []
    st_setup(k)
    k.xr_free = list(k.setup_done)
    for s in stages:
        if s == "copy_h":
            st_copy_h(k)
        elif s == "gla":
            import_gla(k)
        elif s == "ffn0":
            st_ffn(k, 0)
        elif s == "conv":
            st_conv(k)
        elif s == "ffn1":
            st_ffn(k, 1)
        elif s == "final":
            st_final(k)
        else:
            raise ValueError(s)


def import_gla(k):
    st_gla(k)


def st_gla_inproj(k, src, ntok, row0, ch0, own):
    P, g = k.P, k.g
    st_norm_T(k, src, g["nmix"][0], ntok)
    wv = g["w_in0"].rearrange("(c p) n -> p c n", p=128)
    base_free = k.Wslot[0].free + k.Wslot[1].free
    alowT = k.WR[:16, 16384:16384 + T]
    wa2b = k.WR[:16, 18496:18496 + QK]
    wa2f = k.WR[:16, 20544:20544 + 2 * QK].bitcast(F32)
    o = 24640
    tl = k.WR[:, o:o + 1024].bitcast(F32); o += 1024
    tB = k.WR[:, o:o + 1024].bitcast(F32); o += 1024
    teq = k.WR[:, o:o + 1024].bitcast(F32); o += 1024
    tek = k.WR[:, o:o + 1024].bitcast(F32); o += 1024
    tk = k.WR[:, o:o + 1024].bitcast(F32); o += 1024
    tkh = k.WR[:, o:o + 512]; o += 512
    qb = [Slot(k.WR[:, o + i * 512:o + (i + 1) * 512]) for i in range(2)]; o += 1024
    kb = [Slot(k.WR[:, o + i * 512:o + (i + 1) * 512]) for i in range(2)]; o += 1024
    assert o <= 32768, o
    khs = [Slot(k.XR[:, i * 512:(i + 1) * 512].rearrange("p (a d) -> p a d", d=128)) for i in range(2)]
    mask = k.XR[:, 1024:1536]
    for s_ in qb + kb:
        s_.free = list(base_free)
    for s_ in khs:
        s_.free = list(k.xr_free)
    decay3 = k.decay.rearrange("p (j c) -> p j c", c=NCT)
    if own:
        qk_blocks = [(0, CH)] + [(CH + 512 * i, 512) for i in range(4)]
    else:
        qk_blocks = tok_blocks(ntok)

    def chunk_of(t0):
        if not own:
            return CS, t0 // CS
        if t0 == 0:
            return CH, TP // CS
        return CS, TP // CS + 1 + (t0 - CH) // CS
    lv = P.op("sp", "dma_start", out=wa2f, in_=g["w_a2"], inc=k.s_misc, deps=base_free)
    wa_v = P.op("dve", "tensor_copy", out=wa2b, in_=wa2f, deps=[lv] + base_free)
    m1 = P.op("dve", "memset", mask, 1.0, deps=list(k.xr_free))
    m2 = P.op("dve", "memset", mask.rearrange("p (c i) -> p c i", i=CS)[:, :, 0:1], 0.0, deps=[m1])
    st = {"tfree": list(base_free) + [m2], "tkh_free": list(base_free), "pending": None, "cnt": 0,
          "kst": [], "alow": None}

    def ep_alow(j, bi, t0, ts, banks, pv):
        av = P.op("act", "activation", out=alowT[:, t0:t0 + ts], in_=banks[0].ap[:16, :ts], func=AF.Copy,
                  deps=[pv] + base_free)
        banks[0].free = [av]
        st["alow"] = av

    gemm_B(k, KC, [[(wv, 2 * QK + 2 * VV)]], ntok, ep_alow, None, wbase=0, wslots=2, mrows=16)
    alow_ready = [st["alow"], wa_v]

    def pre_block(j, bi, t0, ts):
        bx = next_bank(k)
        P.wait("pe", alow_ready + bx.free)
        P.op("pe", "matmul", bx.ap[:, :ts], wa2b[:, j * 128:(j + 1) * 128], alowT[:, t0:t0 + ts],
             start=True, stop=True, inc=None)
        return [bx]

    def flush_pending():
        if st["pending"] is not None:
            st["pending"]()
            st["pending"] = None

    def epb(j, bi, t0, ts, banks, pv):
        flush_pending()
        if own:
            bx, bq, bk = banks
        else:
            bx, bk = banks
            bq = None
        csz, cb = chunk_of(t0)
        nchb = ts // csz
        e1 = P.op("act", "activation", out=tl[:, :ts], in_=bx.ap[:, :ts], func=AF.Exp, scale=-1.0,
                  bias=k.nba[:, j:j + 1], deps=[pv] + st["tfree"])
        bx.free = [e1]
        l1 = P.op("act", "activation", out=tl[:, :ts], in_=tl[:, :ts], func=AF.Ln, scale=1.0, bias=k.oneb,
                  deps=[e1])
        sc = P.op("dve", "tensor_tensor_scan", out=tB[:, :ts], data0=mask[:, :ts], data1=tl[:, :ts], initial=0.0,
                  op0=ALU.mult, op1=ALU.add, deps=[l1] + st["tfree"])
        eqv = P.op("act", "activation", out=teq[:, :ts], in_=tB[:, :ts], func=AF.Exp, scale=-1.0 / 16, deps=[sc])
        ekv = P.op("act", "activation", out=tek[:, :ts], in_=tB[:, :ts], func=AF.Exp, scale=1.0 / 16)
        dcv = P.op("act", "activation", out=decay3[:, j, cb:cb + nchb],
                   in_=tB[:, :ts].rearrange("p (c i) -> p c i", i=csz)[:, :, csz - 1], func=AF.Exp, scale=-1.0 / 16)
        i = st["cnt"] % 2
        st["cnt"] += 1
        lastd = None
        if own:
            q_ = qb[i]
            qv = P.op("dve", "scalar_tensor_tensor", out=q_.ap[:, :ts], in0=bq.ap[:, :ts], scalar=float(DK) ** -0.5,
                      in1=teq[:, :ts], op0=ALU.mult, op1=ALU.mult, deps=[eqv] + q_.free)
            bq.free = [qv]
            sv = P.op("sp", "dma_start", out=g["qT"][j * 128:(j + 1) * 128, t0:t0 + ts], in_=q_.ap[:, :ts],
                      inc=k.s_st[i], deps=[qv])
            q_.free = [sv]
            st["kst"].append(sv)
        tkv = P.op("dve", "tensor_tensor", out=tk[:, :ts], in0=bk.ap[:, :ts], in1=tek[:, :ts], op=ALU.mult,
                   deps=[ekv])
        bk.free = [tkv]
        if own:
            k_ = kb[i]
            kv = P.op("act", "activation", out=k_.ap[:, :ts], in_=tk[:, :ts], func=AF.Copy, deps=[tkv] + k_.free)
            sv = P.op("sp", "dma_start", out=g["kT"][j * 128:(j + 1) * 128, t0:t0 + ts], in_=k_.ap[:, :ts],
                      inc=k.s_st[2 + i], deps=[kv])
            k_.free = [sv]
            st["kst"].append(sv)
            lastd = kv
        khv = P.op("dve", "tensor_tensor", out=tkh[:, :ts].rearrange("p (c i) -> p c i", i=csz),
                   in0=tk[:, :ts].rearrange("p (c i) -> p c i", i=csz),
                   in1=decay3[:, j, cb:cb + nchb].unsqueeze(2).broadcast_to([128, nchb, csz]),
                   op=ALU.mult, deps=[tkv, dcv] + st["tkh_free"])
        st["tfree"] = [khv] + ([lastd] if lastd is not None else [])

        def pend(j=j, t0=t0, ts=ts, khv=khv, i=i):
            bank = next_bank(k)
            pb = bank.ap.bitcast(BF16)
            tls = tok_tiles(ts)
            P.wait("pe", [khv] + bank.free)
            for ti, (a0, asz) in enumerate(tls):
                pv2 = P.op("pe", "transpose", out=pb[:asz, ti * 128:(ti + 1) * 128], in_=tkh[:, a0:a0 + asz],
                           identity=k.identb, inc=(k.S_pe if ti == len(tls) - 1 else None))
            st["tkh_free"] = [pv2]
            hs = khs[i]
            nt_ = len(tls)
            asz = tls[0][1]
            ev = P.op("act", "activation", out=hs.ap[:asz, :nt_, :],
                      in_=pb[:asz, :nt_ * 128].rearrange("p (a d) -> p a d", d=128), func=AF.Copy,
                      deps=[pv2] + hs.free)
            bank.free = [ev]
            r0 = row0 + t0
            if asz == 128:
                dst = g["kh"][r0:r0 + ts, j * 128:(j + 1) * 128].rearrange("(a p) d -> p a d", p=128)
            else:
                dst = g["kh"][r0:r0 + ts, j * 128:(j + 1) * 128].rearrange("(a p) d -> p a d", p=asz)
            sv = P.op("sp", "dma_start", out=dst, in_=hs.ap[:asz, :nt_, :], inc=k.s_st[4 + i], deps=[ev])
            hs.free = [sv]
            st["kst"].append(sv)

        st["pending"] = pend

    if own:
        jobs = [[(wv, j * 128), (wv, QK + j * 128)] for j in range(16)]
    else:
        jobs = [[(wv, QK + j * 128)] for j in range(16)]
    k.Aslot.free = []
    last = gemm_B(k, KC, jobs, ntok, epb, None, wbase=0, wslots=2, pre_block=pre_block, blocks=qk_blocks)
    flush_pending()
    tail = st["tfree"] + st["tkh_free"] + [qb[0].free, qb[1].free, kb[0].free, kb[1].free][0:0]
    fin = [last] + st["tfree"] + st["tkh_free"]
    for s_ in qb + kb + khs:
        fin += s_.free
    k.Wslot[0].free = list(fin)
    k.Wslot[1].free = list(fin)
    k.xr_free = list(fin)
    k.dram_ready["qk"] = k.dram_ready.get("qk", []) + [v for v in fin if v[0].name.startswith("s_st")]

    vst = [Slot(k.XR[:, i * 512:(i + 1) * 512]) for i in range(3)]
    for s_ in vst:
        s_.free = list(k.xr_free)
    cnt = [0]
    vals = []

    def mk_ep(dst, func, r0):
        def ep(nt, c0, tt, t0, ts, bank, pv):
            i = cnt[0] % 3
            cnt[0] += 1
            s_ = vst[i]
            av = P.op("act", "activation", out=s_.ap[:ts], in_=bank.ap[:ts, :], func=func, deps=[pv] + s_.free)
            bank.free = [av]
            sv = P.op("sp", "dma_start", out=dst[r0 + t0:r0 + t0 + ts, c0:c0 + 512], in_=s_.ap[:ts],
                      inc=k.s_st[i], deps=[av])
            s_.free = [sv]
            vals.append(sv)
        return ep

    k.Aslot.free = []
    gemm_A(k, KC, g["w_in0"][:, 2 * QK:2 * QK + VV], VV, ntok, mk_ep(g["v"], AF.Copy, row0))
    if own:
        k.Aslot.free = []
        gemm_A(k, KC, g["w_in0"][:, 2 * QK + VV:2 * QK + 2 * VV], VV, ntok, mk_ep(g["sr"], AF.Silu, 0))
    fin2 = []
    for s_ in vst:
        fin2 += s_.free
    k.xr_free = fin2
    k.dram_ready["qk"] = k.dram_ready.get("qk", []) + fin2


def st_gla_scan(k):
    P, g = k.P, k.g
    base_free = k.Wslot[0].free + k.Wslot[1].free + k.Aslot.free + k.xr_free
    ddeps = list(k.dram_ready["qk"])
    decay3 = k.decay.rearrange("p (j c) -> p j c", c=NCT)
    Sf_flat = k.WR.bitcast(F32)
    Sf = Sf_flat.rearrange("p (h c v) -> p h c v", h=H, c=4)
    Sb_flat = k.A[:, 0:16384]
    Sb = Sb_flat.rearrange("p (h c v) -> p h c v", h=H, c=4)
    o = 16384
    slots = []
    for i in range(2):
        d = {}
        d["qT"] = k.A[:, o:o + 16 * CS].rearrange("p (j t) -> p j t", t=CS); o += 16 * CS
        d["kT"] = k.A[:, o:o + 16 * CS].rearrange("p (j t) -> p j t", t=CS); o += 16 * CS
        d["kh"] = k.A[:, o:o + QK]; o += QK
        d["v"] = k.A[:, o:o + VV]; o += VV
        d["sr"] = k.A[:, o:o + VV]; o += VV
        d["free"] = list(base_free)
        slots.append(d)
    hnbc = k.A[:, o:o + 2 * DV].bitcast(F32); o += 2 * DV
    AT = [[k.A[:, o + (i * 4 + h) * CS:o + (i * 4 + h + 1) * CS] for h in range(H)] for i in range(2)]
    o += 8 * CS
    og = [k.A[:, o + i * VV:o + (i + 1) * VV] for i in range(2)]; o += 2 * VV
    ogT = [Slot(k.A[:, o + i * 4096:o + (i + 1) * 4096].rearrange("p (j t) -> p j t", t=CS)) for i in range(2)]
    o += 8192
    sqj = k.A[:, o:o + DV]; o += DV
    assert o <= KC * T, o
    for s_ in ogT:
        s_.free = list(base_free)
    z1 = P.op("dve", "memset", Sf_flat, 0.0, deps=base_free)
    z2 = P.op("pool", "memset", Sb_flat, 0.0, deps=base_free)
    hv = P.op("sp", "dma_start", out=hnbc, in_=g["hnorm"].partition_broadcast(CS), inc=k.s_gain, deps=base_free)
    sb_ready = [[z2] for _ in range(H)]
    sf_last = [[z1] for _ in range(H)]
    og_free = [list(base_free), list(base_free)]
    at_free = [list(base_free), list(base_free)]
    o_reads = [[] for _ in range(H)]
    sb_a = [None] * H
    sb_p = [None] * H
    st_vals = []
    n_own = 0
    chunks = [(CS * c, CS, False, 0) for c in range(TP // CS)]
    chunks.append((TP, CH, True, 0))
    chunks += [(TP + CH + CS * i, CS, True, CH + CS * i) for i in range((T - CH) // CS)]
    assert len(chunks) == NCT
    n_pre = TP // CS

    def emit_loads(c):
        r0, cs, own, tl0 = chunks[c]
        sl = slots[c % 2]
        sem_a = k.s_ld[(c % 2) * 2]
        sem_b = k.s_ld[(c % 2) * 2 + 1]
        P.op("sp", "dma_start", out=sl["kh"][:cs], in_=g["kh"][r0:r0 + cs, :], inc=sem_a, deps=sl["free"] + ddeps)
        ld_a = P.op("sp", "dma_start", out=sl["v"][:cs], in_=g["v"][r0:r0 + cs, :], inc=sem_a)
        ld_b = None
        if own:
            P.op("sp", "dma_start", out=sl["qT"][:, :, :cs],
                 in_=g["qT"][:, tl0:tl0 + cs].rearrange("(j p) t -> p j t", p=128), inc=sem_b)
            P.op("sp", "dma_start", out=sl["kT"][:, :, :cs],
                 in_=g["kT"][:, tl0:tl0 + cs].rearrange("(j p) t -> p j t", p=128), inc=sem_b)
            ld_b = P.op("sp", "dma_start", out=sl["sr"][:cs], in_=g["sr"][tl0:tl0 + cs, :], inc=sem_b)
        return ld_a, ld_b

    nxt = emit_loads(0)
    for c, (r0, cs, own, tl0) in enumerate(chunks):
        sl = slots[c % 2]
        ld_a, ld_b = nxt
        if c + 1 < NCT:
            nxt = emit_loads(c + 1)
        reads = []
        if own:
            oi = n_own % 2
            gv = P.op("pool", "tensor_tensor", out=sl["sr"][:cs].rearrange("p (h v) -> p h v", h=H),
                      in0=sl["sr"][:cs].rearrange("p (h v) -> p h v", h=H),
                      in1=hnbc[:cs].unsqueeze(1).broadcast_to([cs, H, DV]), op=ALU.mult, deps=[ld_b, hv])
            sc_vals = []
            for h in range(H):
                bank = next_bank(k)
                P.wait("pe", [ld_b] + bank.free)
                for kc in range(4):
                    pv = P.op("pe", "matmul", bank.ap[:cs, :cs], sl["kT"][:, 4 * h + kc, :cs],
                              sl["qT"][:, 4 * h + kc, :cs], start=(kc == 0), stop=(kc == 3),
                              inc=(k.S_pe if kc == 3 else None))
                mv = P.op("dve", "tensor_tensor", out=AT[oi][h][:cs, :cs], in0=bank.ap[:cs, :cs],
                          in1=k.tril[:cs, :cs], op=ALU.mult, deps=[pv] + at_free[oi])
                bank.free = [mv]
                sc_vals.append(mv)
            tr_list = []
            for h in range(H):
                if k.bank_i % 2 == 1:
                    k.bank_i += 1
                b0 = next_bank(k)
                b1 = next_bank(k)
                pair = k.psall[:cs, b0.idx * 512:b0.idx * 512 + 1024]
                P.wait("pe", [ld_a, sc_vals[h]] + b0.free + b1.free + sb_ready[h])
                for vh, bnk in enumerate((b0, b1)):
                    for kc in range(4):
                        P.op("pe", "matmul", bnk.ap[:cs, :], sl["qT"][:, 4 * h + kc, :cs],
                             Sb[:, h, kc, vh * 512:(vh + 1) * 512], start=(kc == 0), stop=False, inc=None)
                    pv = P.op("pe", "matmul", bnk.ap[:cs, :], AT[oi][h][:cs, :cs],
                              sl["v"][:cs, h * DV + vh * 512:h * DV + (vh + 1) * 512], start=False, stop=True,
                              inc=(k.S_pe if vh == 1 else None))
                o_reads[h] = [pv]
                col = oi * 4 + h
                sq = P.op("act", "activation", out=sqj[:cs], in_=pair, func=AF.Square,
                          accum_out=k.ss[:cs, col:col + 1], deps=[pv])
                sdv = P.op("act", "activation", out=k.sd[:cs, col:col + 1], in_=k.ss[:cs, col:col + 1],
                           func=AF.Sqrt, scale=1.0 / DV, bias=k.epsb[:cs], deps=[sq])
                rv = P.op("dve", "reciprocal", out=k.rstd[:cs, col:col + 1], in_=k.sd[:cs, col:col + 1], deps=[sdv])
                ov = P.op("dve", "scalar_tensor_tensor", out=og[oi][:cs, h * DV:(h + 1) * DV], in0=pair,
                          scalar=k.rstd[:cs, col:col + 1], in1=sl["sr"][:cs, h * DV:(h + 1) * DV],
                          op0=ALU.mult, op1=ALU.mult, deps=[rv, gv] + og_free[oi])
                b0.free = [ov]
                b1.free = [ov]
                tr_list.append(ov)
                reads.append(ov)
        last_pe_read = None
        sfv = None
        for h in range(H):
            for kc in range(4):
                for vh in range(2):
                    bank = next_bank(k)
                    P.wait("pe", [ld_a] + bank.free)
                    pv = P.op("pe", "matmul", bank.ap[:, :], sl["kh"][:cs, (4 * h + kc) * 128:(4 * h + kc + 1) * 128],
                              sl["v"][:cs, h * DV + vh * 512:h * DV + (vh + 1) * 512], start=True, stop=True,
                              inc=k.S_pe)
                    sfv = P.op("dve", "scalar_tensor_tensor", out=Sf[:, h, kc, vh * 512:(vh + 1) * 512],
                               in0=Sf[:, h, kc, vh * 512:(vh + 1) * 512], scalar=decay3[:, 4 * h + kc, c:c + 1],
                               in1=bank.ap[:, :], op0=ALU.mult, op1=ALU.add, deps=[pv] + sf_last[h])
                    bank.free = [sfv]
                    if c >= n_pre - 1 and c < NCT - 1:
                        cv = P.op("act", "activation", out=Sb[:, h, kc, vh * 512:(vh + 1) * 512],
                                  in_=Sf[:, h, kc, vh * 512:(vh + 1) * 512], func=AF.Copy,
                                  deps=[sfv] + o_reads[h])
                        sb_ready[h] = [cv]
                    last_pe_read = pv
            sf_last[h] = [sfv]
        reads.append(last_pe_read)
        if own:
            oslot = ogT[n_own % 2]
            ev = None
            pv = None
            for h in range(H):
                bank = next_bank(k)
                pb = bank.ap.bitcast(BF16)
                P.wait("pe", [tr_list[h]] + bank.free)
                for kk in range(8):
                    pv = P.op("pe", "transpose", out=pb[:, kk * cs:(kk + 1) * cs],
                              in_=og[oi][:cs, h * DV + kk * 128:h * DV + (kk + 1) * 128], identity=k.identb[:cs, :cs],
                              inc=(k.S_pe if kk == 7 else None))
                ev = P.op("act", "activation", out=oslot.ap[:, h * 8:(h + 1) * 8, :cs],
                          in_=pb[:, :8 * cs].rearrange("p (a t) -> p a t", t=cs), func=AF.Copy,
                          deps=[pv] + oslot.free)
                bank.free = [ev]
            og_free[oi] = [pv]
            at_free[oi] = list(o_reads[H - 1])
            sv = P.op("sp", "dma_start",
                      out=g["ogT"][:, tl0:tl0 + cs].rearrange("(j p) t -> p j t", p=128),
                      in_=oslot.ap[:, :, :cs], inc=k.s_st[n_own % 2], deps=[ev])
            oslot.free = [sv]
            st_vals.append(sv)
            n_own += 1
        sl["free"] = reads
    k.dram_ready["ogT"] = st_vals[-2:]
    fin = st_vals[-2:] + [last_pe_read] + sf_last[H - 1]
    k.Wslot[0].free = list(fin)
    k.Wslot[1].free = list(fin)
    k.Aslot.free = list(fin)
    k.xr_free = list(fin)


def st_gla(k):
    g = k.g
    st_gla_inproj(k, g["xp"], TP, 0, 0, own=False)
    st_gla_inproj(k, g["xo"], T, TP, NCHP, own=True)
    st_gla_scan(k)
    st_resid_gemm(k, "w_out0", g["ogT"], KC, g["w_out0"], g["xo"], g["h"], a_deps=k.dram_ready["ogT"])


ALL_STAGES = ["gla", "ffn0", "conv", "ffn1", "final"]


def shared_inputs(inputs):
    f = np.float32
    sh = {}
    sh["w_in0"] = np.ascontiguousarray(inputs["gla_w_in"][0], dtype=f)
    sh["w_a2"] = np.ascontiguousarray(inputs["gla_w_a2"][0], dtype=f)
    sh["b_a_t"] = np.ascontiguousarray(np.asarray(inputs["gla_b_a"][0], dtype=f).reshape(16, 128).T)
    sh["hnorm"] = np.ascontiguousarray(inputs["gla_head_norm"][0], dtype=f)
    sh["w_out0"] = np.ascontiguousarray(inputs["gla_w_out"][0], dtype=f)
    sh["cw_in"] = np.ascontiguousarray(inputs["conv_w_in"][0], dtype=f)
    sh["cw_t"] = np.ascontiguousarray(
        np.asarray(inputs["conv_w"][0], dtype=f).reshape(3, KC, 128).transpose(2, 0, 1).reshape(128, 3 * KC))
    sh["cw_out"] = np.ascontiguousarray(inputs["conv_w_out"][0], dtype=f)
    for l in range(2):
        sh[f"wg{l}"] = np.ascontiguousarray(inputs["ffn_w_gate"][l], dtype=f)
        sh[f"wu{l}"] = np.ascontiguousarray(inputs["ffn_w_up"][l], dtype=f)
        sh[f"wd{l}"] = np.ascontiguousarray(inputs["ffn_w_down"][l], dtype=f)
    sh["nmix"] = np.ascontiguousarray(inputs["norm_mix"], dtype=f)
    sh["nffn"] = np.ascontiguousarray(inputs["norm_ffn"], dtype=f)
    sh["nfin"] = np.ascontiguousarray(inputs["norm_final"], dtype=f)
    sh["ident"] = np.eye(128, dtype=f)
    sh["tril"] = np.triu(np.ones((CS, CS), dtype=f))
    return sh


def core_tokens(x_b, meta):
    f = np.float32
    seq = np.concatenate([np.zeros((CH - meta.shape[0], D), f), np.asarray(meta, f), np.asarray(x_b, f)], axis=0)
    a = {"xo": np.ascontiguousarray(seq[0:T]), "xp": np.zeros((TP, D), f)}
    b = {"xo": np.ascontiguousarray(seq[TP:TP + T]), "xp": np.ascontiguousarray(seq[0:TP])}
    return a, b


_NC_CACHE = {}


def kernel(**inputs):
    x = np.asarray(inputs["x"])
    B = x.shape[0]
    sh = shared_inputs(inputs)
    in_maps = []
    for b in range(B):
        a, c = core_tokens(x[b], inputs["meta"])
        for t in (a, c):
            m = dict(sh)
            m.update(t)
            in_maps.append(m)
    if "nc" not in _NC_CACHE:
        _NC_CACHE["nc"] = build(ALL_STAGES)
    nc = _NC_CACHE["nc"]
    res = run_bass_kernel_spmd(nc, in_maps, core_ids=list(range(2 * B)))
    out = np.empty((B, 2 * NOUT, D), np.float32)
    for b in range(B):
        for hf in range(2):
            out[b, hf * NOUT:(hf + 1) * NOUT] = res.results[2 * b + hf]["out"]
    return out
```

```python
import numpy as np
from contextlib import ExitStack

import concourse.bass as bass
import concourse.mybir as mybir
from concourse.bass_utils import run_bass_kernel_spmd

F32 = mybir.dt.float32
BF16 = mybir.dt.bfloat16
AF = mybir.ActivationFunctionType
ALU = mybir.AluOpType

D = 4096
KC = D // 128
T = 2112
TP = 2048
CH = 64
NCH = T // CH
NCHP = TP // CH
CS = 128
CSP = 256
NPRE = TP // CSP
NCT = NPRE + 1 + (T - CH) // CS
DFF = 11008
FC = DFF // 128
H = 4
DK = 512
DV = 1024
QK = H * DK
VV = H * DV
GIN = 2 * QK + 2 * VV + 16
EPS = 1e-6
NOUT = 2048


def tok_tiles(n):
    r = []
    t = 0
    while t < n:
        s = min(128, n - t)
        r.append((t, s))
        t += s
    return r


def tok_blocks(n):
    r = []
    t = 0
    while t < n:
        s = min(512, n - t)
        r.append((t, s))
        t += s
    return r


class Sem:
    def __init__(self, h, name):
        self.h = h
        self.n = 0
        self.name = name


class Slot:
    def __init__(self, ap=None):
        self.ap = ap
        self.ready = []
        self.free = []


class Prog:
    ENG = ("sp", "act", "pool", "pe", "dve")

    def __init__(self, nc, es):
        self.nc = nc
        self.es = es
        self.q = {e: [] for e in self.ENG}
        self.waited = {e: {} for e in self.ENG}
        self.nsem = 0
        self.cnt = {e: 0 for e in self.ENG}
        self.auto = {}

    def sem(self, name):
        h = self.es.enter_context(self.nc.semaphore(name))
        self.nsem += 1
        return Sem(h, name)

    def wait(self, e, deps):
        best = {}
        for (s, v) in deps:
            if v > best.get(s.name, (None, 0))[1]:
                best[s.name] = (s, v)
        w = self.waited[e]
        for name, (s, v) in best.items():
            if w.get(name, 0) >= v:
                continue
            w[name] = v
            self.q[e].append(("w", s.h, v))

    def op(self, e, method, *args, inc="auto", deps=None, **kw):
        if deps:
            self.wait(e, deps)
        val = None
        amt = 0
        if inc == "auto":
            inc = self.auto.get(e) if method != "dma_start" else None
        if inc is not None:
            amt = 16 if method == "dma_start" else 1
            inc.n += amt
            val = (inc, inc.n)
        self.q[e].append(("o", method, args, kw, inc.h if inc is not None else None, amt))
        self.cnt[e] += 1
        return val

    def run(self, e, eng):
        for it in self.q[e]:
            if it[0] == "w":
                eng.wait_ge(it[1], it[2])
            else:
                _, method, args, kw, sh, amt = it
                ins = getattr(eng, method)(*args, **kw)
                if sh is not None:
                    ins.then_inc(sh, amt)


class K:
    pass


def build(stages, ext_in=(), ext_out=(), debug_T=None):
    nc = bass.Bass("TRN2", target_bir_lowering=False)
    es = ExitStack()
    P = Prog(nc, es)
    k = K()
    k.nc, k.P = nc, P

    def dram(name, shape, dt=F32, inp=False, out=False):
        if inp or name in ext_in:
            kind = "ExternalInput"
        elif out or name in ext_out:
            kind = "ExternalOutput"
        else:
            kind = "Internal"
        return nc.dram_tensor(name, list(shape), dt, kind=kind).ap()

    g = {}
    g["xo"] = dram("xo", [T, D], inp=True)
    g["xp"] = dram("xp", [TP, D], inp=True)
    g["w_in0"] = dram("w_in0", [D, GIN], inp=True)
    g["w_a2"] = dram("w_a2", [16, QK], inp=True)
    g["b_a_t"] = dram("b_a_t", [128, 16], inp=True)
    g["hnorm"] = dram("hnorm", [DV], inp=True)
    g["w_out0"] = dram("w_out0", [VV, D], inp=True)
    g["cw_in"] = dram("cw_in", [D, 3 * D], inp=True)
    g["cw_t"] = dram("cw_t", [128, 3 * KC], inp=True)
    g["cw_out"] = dram("cw_out", [D, D], inp=True)
    for l in range(2):
        g[f"wg{l}"] = dram(f"wg{l}", [D, DFF], inp=True)
        g[f"wu{l}"] = dram(f"wu{l}", [D, DFF], inp=True)
        g[f"wd{l}"] = dram(f"wd{l}", [DFF, D], inp=True)
    g["nmix"] = dram("nmix", [2, D], inp=True)
    g["nffn"] = dram("nffn", [2, D], inp=True)
    g["nfin"] = dram("nfin", [D], inp=True)
    g["ident"] = dram("ident", [128, 128], inp=True)
    g["tril"] = dram("tril", [CS, CS], inp=True)
    g["h"] = dram("h", [T, D])
    g["qT"] = dram("qT", [QK, T], BF16)
    g["kT"] = dram("kT", [QK, T], BF16)
    g["kh"] = dram("kh", [TP + T, QK], BF16)
    g["v"] = dram("v", [TP + T, VV], BF16)
    g["sr"] = dram("sr", [T, VV], BF16)
    g["ogT"] = dram("ogT", [VV, T], BF16)
    g["yT"] = dram("yT", [D, T], BF16)
    g["hidT"] = dram("hidT", [DFF, T], BF16)
    g["out"] = dram("out", [NOUT, D], out=True)
    k.g = g

    A_t = nc.alloc_sbuf_tensor("A", [128, KC * T], BF16)
    WR_t = nc.alloc_sbuf_tensor("WR", [128, 32768], BF16)
    k.A = A_t[:, :]
    k.WR = WR_t[:, :]
    k.A3 = k.A.rearrange("p (c t) -> p c t", t=T)
    k.identb = nc.alloc_sbuf_tensor("identb", [128, 128], BF16)[:, :]
    k.tril = nc.alloc_sbuf_tensor("trilm", [CS, CS], F32)[:, :]
    k.decay = nc.alloc_sbuf_tensor("decay", [128, 16 * NCT], F32)[:, :]
    k.nba = nc.alloc_sbuf_tensor("nba", [128, 16], F32)[:, :]
    k.cw = nc.alloc_sbuf_tensor("cw", [128, 3 * KC], F32)[:, :]
    k.ss = nc.alloc_sbuf_tensor("ss", [128, 20], F32)[:, :]
    k.sd = nc.alloc_sbuf_tensor("sd", [128, 20], F32)[:, :]
    k.rstd = nc.alloc_sbuf_tensor("rstd", [128, 20], F32)[:, :]
    k.epsb = nc.alloc_sbuf_tensor("epsb", [128, 1], F32)[:, :]
    k.oneb = nc.alloc_sbuf_tensor("oneb", [128, 1], F32)[:, :]
    xr_cols = (nc.sbuf_bytes_remaining - 64) // 2
    xr_cols = (xr_cols // 16) * 16
    XR_t = nc.alloc_sbuf_tensor("XR", [128, xr_cols], BF16)
    k.XR = XR_t[:, :]
    k.xr_cols = xr_cols

    k.banks = []
    psall = nc.alloc_psum_tensor("psall", [128, 4096], F32)
    k.psall = psall[:, :]
    for b in range(8):
        s = Slot(k.psall[:, b * 512:(b + 1) * 512])
        s.idx = b
        k.banks.append(s)
    k.bank_i = 0

    k.S_pe = P.sem("S_pe")
    k.S_dve = P.sem("S_dve")
    k.S_act = P.sem("S_act")
    k.S_pool = P.sem("S_pool")
    k.s_w = [P.sem(f"s_w{i}") for i in range(4)]
    k.s_ld = [P.sem(f"s_ld{i}") for i in range(6)]
    k.s_st = [P.sem(f"s_st{i}") for i in range(6)]
    k.s_a = [P.sem(f"s_a{i}") for i in range(8)]
    k.A_kc = None
    k.s_misc = P.sem("s_misc")
    k.s_gain = P.sem("s_gain")
    k.s_cp = P.sem("s_cp")
    P.auto = {"act": k.S_act, "dve": k.S_dve, "pool": k.S_pool}

    k.Aslot = Slot(k.A)
    k.Wslot = [Slot(), Slot()]
    k.dram_ready = {}

    from_stages(k, stages)

    with nc.Block() as block:
        @block.sync
        def _(e):
            P.run("sp", e)

        @block.scalar
        def _(e):
            P.run("act", e)

        @block.gpsimd
        def _(e):
            P.run("pool", e)

        @block.tensor
        def _(e):
            P.run("pe", e)

        @block.vector
        def _(e):
            P.run("dve", e)
    es.close()
    return nc


def k_tiles(k, ntok):
    if getattr(k, "skip_halo", False) and ntok == T:
        return [(CH + 128 * i, 128) for i in range((T - CH) // 128)]
    return tok_tiles(ntok)


def k_blocks(k, ntok):
    if getattr(k, "skip_halo", False) and ntok == T:
        return [(CH + 512 * i, 512) for i in range((T - CH) // 512)]
    return tok_blocks(ntok)


def look_free(k, n=4):
    i = k.bank_i
    if i % n != 0:
        return []
    deps = []
    for j in range(n):
        deps += k.banks[(i + j) % 8].free
    return deps


def next_bank(k):
    b = k.banks[k.bank_i % 8]
    k.bank_i += 1
    return b


def wr_f32(k, c0, n):
    return k.WR[:, c0:c0 + 2 * n].bitcast(F32)


def xr_f32(k, c0, n):
    return k.XR[:, c0:c0 + 2 * n].bitcast(F32)


def st_setup(k):
    P, g = k.P, k.g
    tmp = wr_f32(k, 0, 128)
    v1 = P.op("sp", "dma_start", out=tmp, in_=g["ident"], inc=k.s_misc)
    v2 = P.op("sp", "dma_start", out=k.tril, in_=g["tril"], inc=k.s_misc)
    v3 = P.op("sp", "dma_start", out=k.nba, in_=g["b_a_t"], inc=k.s_misc)
    v4 = P.op("sp", "dma_start", out=k.cw, in_=g["cw_t"], inc=k.s_misc)
    P.op("dve", "tensor_copy", out=k.identb, in_=tmp, deps=[v4])
    P.op("dve", "tensor_scalar", out=k.nba, in0=k.nba, scalar1=-1.0, scalar2=None, op0=ALU.mult, deps=[v3])
    P.op("dve", "memset", k.epsb, EPS)
    P.op("dve", "memset", k.oneb, 1.0)
    vv = P.op("dve", "memset", k.decay, 1.0, inc=k.S_dve)
    k.setup_done = [vv]
    k.Wslot[0].free = [vv]
    k.Wslot[1].free = [vv]


def st_norm_T(k, src, gain_row, ntok, src_deps=()):
    P = k.P
    xb = [Slot(wr_f32(k, 0, 4096)), Slot(wr_f32(k, 8192, 4096))]
    gbc = wr_f32(k, 16384, 4096)
    hnb = [Slot(k.WR[:, 24576:28672]), Slot(k.WR[:, 28672:32768])]
    wfree = k.Wslot[0].free + k.Wslot[1].free + list(k.setup_done)
    for s in xb + hnb:
        s.free = list(wfree)
    gv = P.op("sp", "dma_start", out=gbc, in_=gain_row.partition_broadcast(128), inc=k.s_gain,
              deps=wfree)
    tiles = k_tiles(k, ntok)
    a_ready = []
    st = {"last_pe": None, "ev_i": 0}

    def back_half(tt, t0, ts, dv):
        hb = hnb[tt % 2]
        pv = None
        for q in range(4):
            bank = next_bank(k)
            pb = bank.ap.bitcast(BF16)
            P.wait("pe", [dv] + bank.free)
            for kk in range(8):
                kc = q * 8 + kk
                pv = P.op("pe", "transpose", out=pb[:, kk * 128:kk * 128 + ts],
                          in_=hb.ap[:ts, kc * 128:(kc + 1) * 128], identity=k.identb[:ts, :ts],
                          inc=(k.S_pe if kk == 7 else None))
            eng = "act" if st["ev_i"] % 2 == 0 else "dve"
            st["ev_i"] += 1
            src_v = pb.rearrange("p (a b) -> p a b", b=128)[:, :, :ts]
            dst_v = k.A3[:, q * 8:(q + 1) * 8, t0:t0 + ts]
            if eng == "act":
                ev = P.op("act", "activation", out=dst_v, in_=src_v, func=AF.Copy,
                          deps=[pv] + k.Aslot.free)
            else:
                ev = P.op("dve", "tensor_copy", out=dst_v, in_=src_v, deps=[pv] + k.Aslot.free)
            bank.free = [ev]
            a_ready.append(ev)
        hb.free = [pv]
        st["last_pe"] = pv

    prev = None
    for tt, (t0, ts) in enumerate(tiles):
        b = tt % 2
        x, hb = xb[b], hnb[b]
        lv = P.op("sp", "dma_start", out=x.ap[:ts], in_=src[t0:t0 + ts, :], inc=k.s_ld[b],
                  deps=list(x.free) + list(src_deps))
        sq = P.op("act", "activation", out=hb.ap[:ts], in_=x.ap[:ts], func=AF.Square,
                  accum_out=k.ss[:ts, tt:tt + 1], deps=[lv] + hb.free)
        av = P.op("act", "activation", out=k.sd[:ts, tt:tt + 1], in_=k.ss[:ts, tt:tt + 1], func=AF.Sqrt,
                  scale=1.0 / D, bias=k.epsb[:ts], deps=[sq])
        rv = P.op("dve", "reciprocal", out=k.rstd[:ts, tt:tt + 1], in_=k.sd[:ts, tt:tt + 1], deps=[av, gv])
        dv = P.op("dve", "scalar_tensor_tensor", out=hb.ap[:ts], in0=x.ap[:ts],
                  scalar=k.rstd[:ts, tt:tt + 1], in1=gbc[:ts], op0=ALU.mult, op1=ALU.mult, deps=[rv])
        x.free = [dv]
        if prev is not None:
            back_half(*prev)
        prev = (tt, t0, ts, dv)
    back_half(*prev)
    last_pe = st["last_pe"]
    k.Aslot.ready = a_ready[-2:]
    k.A_kc = None
    k.Aslot.free = []
    k.Wslot[0].free = [last_pe, a_ready[-1], a_ready[-2]]
    k.Wslot[1].free = [last_pe, a_ready[-1], a_ready[-2]]


def a_tok_deps(k, t0, ts):
    if k.A_kc is None:
        return []
    return [v for (a, b, v) in k.A_kc if a < t0 + ts and b > t0]


def load_A(k, src, kcn, src_deps):
    P = k.P
    view = src.rearrange("(c p) t -> p c t", p=128)
    k.A_kc = []
    for i, (t0, ts) in enumerate(tok_blocks(T)):
        v = None
        for c0 in range(0, kcn, 16):
            c1 = min(kcn, c0 + 16)
            v = P.op("sp", "dma_start", out=k.A3[:, c0:c1, t0:t0 + ts], in_=view[:, c0:c1, t0:t0 + ts],
                     inc=k.s_a[i], deps=list(k.Aslot.free) + list(src_deps))
        k.A_kc.append((t0, t0 + ts, v))
    k.Aslot.ready = []
    k.Aslot.free = []


def gemm_A(k, kcn, wsrc, ncols_total, ntok, epilogue, col_tile=512):
    P = k.P
    wv = wsrc.rearrange("(c p) n -> p c n", p=128)
    tiles = k_tiles(k, ntok)
    last = None
    for nt in range(ncols_total // col_tile):
        c0 = nt * col_tile
        ws = k.Wslot[nt % 2]
        wb = k.WR[:, (nt % 2) * 16384:(nt % 2) * 16384 + kcn * col_tile].rearrange("p (c n) -> p c n", n=col_tile)
        step = 8
        for kc0 in range(0, kcn, step):
            kc1 = min(kcn, kc0 + step)
            wl = P.op("pool", "dma_start", out=wb[:, kc0:kc1, :], in_=wv[:, kc0:kc1, c0:c0 + col_tile],
                      inc=k.s_w[nt % 2], deps=ws.free)
        ws.ready = [wl]
        for tt, (t0, ts) in enumerate(tiles):
            la = look_free(k)
            bank = next_bank(k)
            P.wait("pe", k.Aslot.ready + ws.ready + la + bank.free + a_tok_deps(k, t0, ts))
            for kc in range(kcn):
                pv = P.op("pe", "matmul", bank.ap[:ts, :col_tile], k.A3[:, kc, t0:t0 + ts], wb[:, kc, :],
                          start=(kc == 0), stop=(kc == kcn - 1),
                          inc=(k.S_pe if kc == kcn - 1 else None))
            epilogue(nt, c0, tt, t0, ts, bank, pv)
            last = pv
        ws.free = [last]
    k.Aslot.free = [last]
    return last


def gemm_B(k, kcn, jobs, ntok, ep_block, ep_row=None, wbase=0, wslots=2, mrows=128, pre_block=None,
           ksplit=1, blocks=None):
    P = k.P
    if blocks is None:
        blocks = k_blocks(k, ntok)
    ng = len(jobs[0])
    kcp = kcn // ksplit
    assert kcp * ksplit == kcn
    last = None
    wsl = [Slot() for _ in range(wslots)]
    for s in wsl:
        s.free = k.Wslot[0].free + k.Wslot[1].free
    psz = ng * kcp * 128
    for j, job in enumerate(jobs):
        pieces = []
        for p in range(ksplit):
            pi = j * ksplit + p
            ws = wsl[pi % wslots]
            base = wbase + (pi % wslots) * psz
            wb = k.WR[:, base:base + psz].rearrange("p (g c n) -> p g c n", g=ng, n=128)
            wl = None
            for gi, (wv, c0) in enumerate(job):
                for kc0 in range(0, kcp, 8):
                    kc1 = min(kcp, kc0 + 8)
                    wl = P.op("pool", "dma_start", out=wb[:, gi, kc0:kc1, :mrows],
                              in_=wv[:, p * kcp + kc0:p * kcp + kc1, c0:c0 + mrows],
                              inc=k.s_w[pi % wslots], deps=ws.free)
            ws.ready = [wl]
            pieces.append((ws, wb))
        for bi, (t0, ts) in enumerate(blocks):
            xb = pre_block(j, bi, t0, ts) if pre_block is not None else []
            la = []
            banks = []
            for _ in range(ng):
                la += look_free(k)
                banks.append(next_bank(k))
            for b_ in banks:
                la += b_.free
            for p in range(ksplit):
                ws, wb = pieces[p]
                P.wait("pe", ws.ready)
                for gi in range(ng):
                    if p == 0:
                        P.wait("pe", k.Aslot.ready + (la if gi == 0 else []) + banks[gi].free + a_tok_deps(k, t0, ts))
                    for kk in range(kcp):
                        kc = p * kcp + kk
                        pv = P.op("pe", "matmul", banks[gi].ap[:mrows, :ts], wb[:, gi, kk, :mrows],
                                  k.A3[:, kc, t0:t0 + ts], start=(kc == 0), stop=(kc == kcn - 1),
                                  inc=(k.S_pe if (kc == kcn - 1 and gi == ng - 1) else None))
            ep_block(j, bi, t0, ts, xb + banks, pv)
            last = pv
        for ws, _ in pieces:
            ws.free = [last]
        if ep_row is not None:
            ep_row(j)
    k.Aslot.free = [last]
    k.Wslot[0].free = [last]
    k.Wslot[1].free = [last]
    return last


def st_resid_gemm(k, name, a_src, kcn, wsrc, h_src, h_dst, a_deps=None, from_A=False):
    P, g = k.P, k.g
    if not from_A:
        load_A(k, a_src, kcn, a_deps or [])
    nslot = 3
    hb = [Slot(k.XR[:, i * 1024:(i + 1) * 1024].bitcast(F32)) for i in range(nslot)]
    for s in hb:
        s.free = list(k.xr_free)
    st_vals = [None] * nslot
    cnt = [0]
    hdeps = list(k.dram_ready.get("h", []))

    def ep(nt, c0, tt, t0, ts, bank, pv):
        i = cnt[0] % nslot
        cnt[0] += 1
        s = hb[i]
        lv = P.op("sp", "dma_start", out=s.ap[:ts], in_=h_src[t0:t0 + ts, c0:c0 + 512], inc=k.s_ld[i],
                  deps=s.free + hdeps)
        dv = P.op("dve", "tensor_tensor", out=s.ap[:ts], in0=bank.ap[:ts, :], in1=s.ap[:ts], op=ALU.add,
                  inc=k.S_dve, deps=[pv, lv])
        bank.free = [dv]
        sv = P.op("sp", "dma_start", out=h_dst[t0:t0 + ts, c0:c0 + 512], in_=s.ap[:ts], inc=k.s_st[i],
                  deps=[dv])
        s.free = [sv]
        st_vals[i] = sv

    gemm_A(k, kcn, wsrc, D, T, ep)
    k.dram_ready["h"] = [v for v in st_vals if v is not None]
    k.xr_free = [v for v in st_vals if v is not None]


def st_ffn(k, l):
    P, g = k.P, k.g
    st_norm_T(k, g["h"], g["nffn"][l], T, src_deps=k.dram_ready.get("h", []))
    wgv = g[f"wg{l}"].rearrange("(c p) n -> p c n", p=128)
    wuv = g[f"wu{l}"].rearrange("(c p) n -> p c n", p=128)
    jobs = [[(wgv, j * 128), (wuv, j * 128)] for j in range(FC)]
    tmp = [Slot(wr_f32(k, 16384 + i * 1024, 512)) for i in range(2)]
    rows = [Slot(k.WR[:, 20480 + i * 2304:20480 + i * 2304 + T]) for i in range(2)]
    base_free = k.Wslot[0].free + k.Wslot[1].free
    for s in tmp + rows:
        s.free = list(base_free)
    cnt = [0]
    row_st = [None, None]
    hid_deps = list(k.dram_ready.get("hidT_free", []))

    def epb(j, bi, t0, ts, banks, pv):
        i = cnt[0] % 2
        cnt[0] += 1
        tm = tmp[i]
        row = rows[j % 2]
        av = P.op("act", "activation", out=tm.ap[:, :ts], in_=banks[0].ap[:, :ts], func=AF.Silu, inc=k.S_act,
                  deps=[pv] + tm.free)
        dv = P.op("dve", "tensor_tensor", out=row.ap[:, t0:t0 + ts], in0=banks[1].ap[:, :ts], in1=tm.ap[:, :ts],
                  op=ALU.mult, inc=k.S_dve, deps=[av] + row.free)
        tm.free = [dv]
        banks[0].free = [dv]
        banks[1].free = [dv]
        row.last = dv

    def epr(j):
        row = rows[j % 2]
        sv = P.op("sp", "dma_start", out=g["hidT"][j * 128:(j + 1) * 128, :], in_=row.ap, inc=k.s_st[j % 2],
                  deps=[row.last] + hid_deps)
        row.free = [sv]
        row_st[j % 2] = sv

    last = gemm_B(k, KC, jobs, T, epb, epr, wbase=0, wslots=2)
    k.dram_ready["hidT"] = [v for v in row_st if v is not None]
    k.Wslot[0].free = [last] + k.dram_ready["hidT"]
    k.Wslot[1].free = [last] + k.dram_ready["hidT"]
    parts = [(0, 29), (29, 29), (58, 28)]
    for (c0, cn) in parts:
        st_resid_gemm(k, f"down{l}", g["hidT"][c0 * 128:(c0 + cn) * 128, :], cn,
                      g[f"wd{l}"][c0 * 128:(c0 + cn) * 128, :], g["h"], g["h"],
                      a_deps=k.dram_ready["hidT"])
    k.dram_ready["hidT_free"] = [k.Aslot.free[0]]


def st_conv(k):
    P, g = k.P, k.g
    st_norm_T(k, g["h"], g["nmix"][1], T, src_deps=k.dram_ready.get("h", []))
    wv = g["cw_in"].rearrange("(c p) n -> p c n", p=128)
    jobs = [[(wv, j * 128), (wv, D + j * 128), (wv, 2 * D + j * 128)] for j in range(KC)]
    o = 18432
    z = k.WR[:, o:o + 2 * (T + 2)].bitcast(F32)
    o += 2 * (T + 2) + 12
    bgr = k.WR[:, o:o + 2 * T].bitcast(F32)
    o += 2 * T
    cr = k.WR[:, o:o + 2 * T].bitcast(F32)
    o += 2 * T
    assert o <= 32768, o
    tmp = [Slot(k.XR[:, T + i * 1024:T + (i + 1) * 1024].bitcast(F32)) for i in range(1)]
    assert T + 1024 <= k.xr_cols
    yrow = [Slot(k.XR[:, 0:T])]
    base_free = k.Wslot[0].free + k.Wslot[1].free
    for s in tmp:
        s.free = list(base_free)
    yrow[0].free = list(k.xr_free)
    zs = Slot(z)
    zs.free = list(base_free)
    cnt = [0]
    st = {"zlast": None, "sv": None, "alast": None}
    mz = P.op("dve", "memset", z[:, 0:2], 0.0, deps=base_free)

    def epb(j, bi, t0, ts, banks, pv):
        i = 0
        cnt[0] += 1
        tm = tmp[i]
        P.op("act", "activation", out=tm.ap[:, :ts], in_=banks[1].ap[:, :ts], func=AF.Copy,
             deps=[pv] + tm.free + zs.free)
        av = P.op("act", "activation", out=bgr[:, t0:t0 + ts], in_=banks[0].ap[:, :ts], func=AF.Copy)
        st["alast"] = av
        dv = P.op("dve", "tensor_tensor", out=z[:, 2 + t0:2 + t0 + ts], in0=banks[2].ap[:, :ts], in1=tm.ap[:, :ts],
                  op=ALU.mult, inc=k.S_dve, deps=[av] + zs.free)
        tm.free = [dv]
        for b in banks:
            b.free = [dv]
        st["zlast"] = dv

    def epr(j):
        y = yrow[0]
        c1 = P.op("dve", "tensor_scalar", out=cr, in0=z[:, 2:2 + T], scalar1=k.cw[:, 2 * KC + j:2 * KC + j + 1],
                  scalar2=None, op0=ALU.mult, deps=y.free + [st["zlast"]])
        c2 = P.op("dve", "scalar_tensor_tensor", out=cr, in0=z[:, 1:1 + T], scalar=k.cw[:, KC + j:KC + j + 1],
                  in1=cr, op0=ALU.mult, op1=ALU.add, deps=[c1])
        c3 = P.op("dve", "scalar_tensor_tensor", out=cr, in0=z[:, 0:T], scalar=k.cw[:, j:j + 1],
                  in1=cr, op0=ALU.mult, op1=ALU.add, deps=[c2])
        dv = P.op("dve", "tensor_tensor", out=y.ap, in0=cr, in1=bgr, op=ALU.mult, deps=[c3, st["alast"]])
        zs.free = [dv]
        sv = P.op("sp", "dma_start", out=g["yT"][j * 128:(j + 1) * 128, :], in_=y.ap, inc=k.s_st[2], deps=[dv])
        y.free = [sv]
        st["sv"] = sv

    last = gemm_B(k, KC, jobs, T, epb, epr, wbase=0, wslots=3, ksplit=2)
    k.dram_ready["yT"] = [st["sv"]]
    k.skip_halo = True
    k.xr_free = [st["sv"]]
    k.Wslot[0].free = [last, st["sv"]]
    k.Wslot[1].free = [last, st["sv"]]
    st_resid_gemm(k, "cw_out", g["yT"], KC, g["cw_out"], g["h"], g["h"], a_deps=k.dram_ready["yT"])


def st_final(k):
    P, g = k.P, k.g
    xb = [Slot(wr_f32(k, 0, 4096)), Slot(wr_f32(k, 8192, 4096))]
    gbc = wr_f32(k, 16384, 4096)
    junk = k.WR[:, 24576:28672]
    wfree = k.Wslot[0].free + k.Wslot[1].free
    for s in xb:
        s.free = list(wfree)
    gv = P.op("sp", "dma_start", out=gbc, in_=g["nfin"].partition_broadcast(128), inc=k.s_gain, deps=wfree)
    hdeps = list(k.dram_ready.get("h", []))
    svs = [None, None]
    for tt in range(NOUT // 128):
        t0 = CH + tt * 128
        b = tt % 2
        x = xb[b]
        lv = P.op("sp", "dma_start", out=x.ap, in_=g["h"][t0:t0 + 128, :], inc=k.s_ld[b], deps=x.free + hdeps)
        sq = P.op("act", "activation", out=junk, in_=x.ap, func=AF.Square, accum_out=k.ss[:, tt:tt + 1],
                  deps=[lv] + wfree)
        av = P.op("act", "activation", out=k.sd[:, tt:tt + 1], in_=k.ss[:, tt:tt + 1], func=AF.Sqrt,
                  scale=1.0 / D, bias=k.epsb, deps=[sq])
        rv = P.op("dve", "reciprocal", out=k.rstd[:, tt:tt + 1], in_=k.sd[:, tt:tt + 1], deps=[av, gv])
        dv = P.op("dve", "scalar_tensor_tensor", out=x.ap, in0=x.ap, scalar=k.rstd[:, tt:tt + 1], in1=gbc,
                  op0=ALU.mult, op1=ALU.mult, deps=[rv])
        sv = P.op("sp", "dma_start", out=g["out"][tt * 128:(tt + 1) * 128, :], in_=x.ap, inc=k.s_st[b], deps=[dv])
        x.free = [sv]
        svs[b] = sv
    P.wait("sp", [v for v in svs if v is not None])


def st_copy_h(k):
    P, g = k.P, k.g
    vals = []
    for i in range(0, T, 264):
        v = P.op("sp", "dma_start", out=g["h"][i:i + 264, :], in_=g["xo"][i:i + 264, :], inc=k.s_cp)
        vals.append(v)
    k.dram_ready["h"] = [vals[-1]]


def from_stages(k, stages):
    k.xr_free = []
    st_setup(k)
    k.xr_free = list(k.setup_done)
    for s in stages:
        if s == "copy_h":
            st_copy_h(k)
        elif s == "gla":
            import_gla(k)
        elif s == "ffn0":
            st_ffn(k, 0)
        elif s == "conv":
            st_conv(k)
        elif s == "ffn1":
            st_ffn(k, 1)
        elif s == "final":
            st_final(k)
        else:
            raise ValueError(s)


def import_gla(k):
    st_gla(k)


def st_gla_inproj(k, src, ntok, row0, ch0, own):
    P, g = k.P, k.g
    st_norm_T(k, src, g["nmix"][0], ntok)
    wv = g["w_in0"].rearrange("(c p) n -> p c n", p=128)
    base_free = k.Wslot[0].free + k.Wslot[1].free
    alowT = k.WR[:16, 16384:16384 + T]
    wa2b = k.WR[:16, 18496:18496 + QK]
    wa2f = k.WR[:16, 20544:20544 + 2 * QK].bitcast(F32)
    o = 24640
    tl = k.WR[:, o:o + 1024].bitcast(F32); o += 1024
    tB = k.WR[:, o:o + 1024].bitcast(F32); o += 1024
    teq = k.WR[:, o:o + 1024].bitcast(F32); o += 1024
    tek = k.WR[:, o:o + 1024].bitcast(F32); o += 1024
    tk = k.WR[:, o:o + 1024].bitcast(F32); o += 1024
    tkh = k.WR[:, o:o + 512]; o += 512
    qb = [Slot(k.WR[:, o + i * 512:o + (i + 1) * 512]) for i in range(2)]; o += 1024
    kb = [Slot(k.WR[:, o + i * 512:o + (i + 1) * 512]) for i in range(2)]; o += 1024
    assert o <= 32768, o
    khs = [Slot(k.XR[:, i * 512:(i + 1) * 512].rearrange("p (a d) -> p a d", d=128)) for i in range(2)]
    mask = k.XR[:, 1024:1536]
    for s_ in qb + kb:
        s_.free = list(base_free)
    for s_ in khs:
        s_.free = list(k.xr_free)
    decay3 = k.decay.rearrange("p (j c) -> p j c", c=NCT)
    if own:
        qk_blocks = [(0, CH)] + [(CH + 512 * i, 512) for i in range(4)]
    else:
        qk_blocks = tok_blocks(ntok)

    def chunk_of(t0):
        if not own:
            return CSP, t0 // CSP
        if t0 == 0:
            return CH, NPRE
        return CS, NPRE + 1 + (t0 - CH) // CS
    lv = P.op("sp", "dma_start", out=wa2f, in_=g["w_a2"], inc=k.s_misc, deps=base_free)
    wa_v = P.op("dve", "tensor_copy", out=wa2b, in_=wa2f, deps=[lv] + base_free)
    m1 = P.op("dve", "memset", mask, 1.0, deps=list(k.xr_free))
    m2 = P.op("dve", "memset", mask.rearrange("p (c i) -> p c i", i=(CS if own else CSP))[:, :, 0:1], 0.0,
              deps=[m1])
    st = {"tfree": list(base_free) + [m2], "tkh_free": list(base_free), "pending": None, "cnt": 0,
          "kst": [], "alow": None}

    def ep_alow(j, bi, t0, ts, banks, pv):
        av = P.op("act", "activation", out=alowT[:, t0:t0 + ts], in_=banks[0].ap[:16, :ts], func=AF.Copy,
                  deps=[pv] + base_free)
        banks[0].free = [av]
        st["alow"] = av

    gemm_B(k, KC, [[(wv, 2 * QK + 2 * VV)]], ntok, ep_alow, None, wbase=0, wslots=2, mrows=16)
    alow_ready = [st["alow"], wa_v]

    def pre_block(j, bi, t0, ts):
        bx = next_bank(k)
        P.wait("pe", alow_ready + bx.free)
        P.op("pe", "matmul", bx.ap[:, :ts], wa2b[:, j * 128:(j + 1) * 128], alowT[:, t0:t0 + ts],
             start=True, stop=True, inc=None)
        return [bx]

    def flush_pending():
        if st["pending"] is not None:
            st["pending"]()
            st["pending"] = None

    def epb(j, bi, t0, ts, banks, pv):
        flush_pending()
        if own:
            bx, bq, bk = banks
        else:
            bx, bk = banks
            bq = None
        csz, cb = chunk_of(t0)
        nchb = ts // csz
        e1 = P.op("act", "activation", out=tl[:, :ts], in_=bx.ap[:, :ts], func=AF.Exp, scale=-1.0,
                  bias=k.nba[:, j:j + 1], deps=[pv] + st["tfree"])
        bx.free = [e1]
        l1 = P.op("act", "activation", out=tl[:, :ts], in_=tl[:, :ts], func=AF.Ln, scale=1.0, bias=k.oneb,
                  deps=[e1])
        sc = P.op("dve", "tensor_tensor_scan", out=tB[:, :ts], data0=mask[:, :ts], data1=tl[:, :ts], initial=0.0,
                  op0=ALU.mult, op1=ALU.add, deps=[l1] + st["tfree"])
        eqv = P.op("act", "activation", out=teq[:, :ts], in_=tB[:, :ts], func=AF.Exp, scale=-1.0 / 16, deps=[sc])
        ekv = P.op("act", "activation", out=tek[:, :ts], in_=tB[:, :ts], func=AF.Exp, scale=1.0 / 16)
        dcv = P.op("act", "activation", out=decay3[:, j, cb:cb + nchb],
                   in_=tB[:, :ts].rearrange("p (c i) -> p c i", i=csz)[:, :, csz - 1], func=AF.Exp, scale=-1.0 / 16)
        i = st["cnt"] % 2
        st["cnt"] += 1
        lastd = None
        if own:
            q_ = qb[i]
            qv = P.op("dve", "scalar_tensor_tensor", out=q_.ap[:, :ts], in0=bq.ap[:, :ts], scalar=float(DK) ** -0.5,
                      in1=teq[:, :ts], op0=ALU.mult, op1=ALU.mult, deps=[eqv] + q_.free)
            bq.free = [qv]
            sv = P.op("sp", "dma_start", out=g["qT"][j * 128:(j + 1) * 128, t0:t0 + ts], in_=q_.ap[:, :ts],
                      inc=k.s_st[i], deps=[qv])
            q_.free = [sv]
            st["kst"].append(sv)
        tkv = P.op("dve", "tensor_tensor", out=tk[:, :ts], in0=bk.ap[:, :ts], in1=tek[:, :ts], op=ALU.mult,
                   deps=[ekv])
        bk.free = [tkv]
        if own:
            k_ = kb[i]
            kv = P.op("act", "activation", out=k_.ap[:, :ts], in_=tk[:, :ts], func=AF.Copy, deps=[tkv] + k_.free)
            sv = P.op("sp", "dma_start", out=g["kT"][j * 128:(j + 1) * 128, t0:t0 + ts], in_=k_.ap[:, :ts],
                      inc=k.s_st[2 + i], deps=[kv])
            k_.free = [sv]
            st["kst"].append(sv)
            lastd = kv
        khv = P.op("dve", "tensor_tensor", out=tkh[:, :ts].rearrange("p (c i) -> p c i", i=csz),
                   in0=tk[:, :ts].rearrange("p (c i) -> p c i", i=csz),
                   in1=decay3[:, j, cb:cb + nchb].unsqueeze(2).broadcast_to([128, nchb, csz]),
                   op=ALU.mult, deps=[tkv, dcv] + st["tkh_free"])
        st["tfree"] = [khv] + ([lastd] if lastd is not None else [])

        def pend(j=j, t0=t0, ts=ts, khv=khv, i=i):
            bank = next_bank(k)
            pb = bank.ap.bitcast(BF16)
            tls = tok_tiles(ts)
            P.wait("pe", [khv] + bank.free)
            for ti, (a0, asz) in enumerate(tls):
                pv2 = P.op("pe", "transpose", out=pb[:asz, ti * 128:(ti + 1) * 128], in_=tkh[:, a0:a0 + asz],
                           identity=k.identb, inc=(k.S_pe if ti == len(tls) - 1 else None))
            st["tkh_free"] = [pv2]
            hs = khs[i]
            nt_ = len(tls)
            asz = tls[0][1]
            ev = P.op("act", "activation", out=hs.ap[:asz, :nt_, :],
                      in_=pb[:asz, :nt_ * 128].rearrange("p (a d) -> p a d", d=128), func=AF.Copy,
                      deps=[pv2] + hs.free)
            bank.free = [ev]
            r0 = row0 + t0
            if asz == 128:
                dst = g["kh"][r0:r0 + ts, j * 128:(j + 1) * 128].rearrange("(a p) d -> p a d", p=128)
            else:
                dst = g["kh"][r0:r0 + ts, j * 128:(j + 1) * 128].rearrange("(a p) d -> p a d", p=asz)
            sv = P.op("sp", "dma_start", out=dst, in_=hs.ap[:asz, :nt_, :], inc=k.s_st[4 + i], deps=[ev])
            hs.free = [sv]
            st["kst"].append(sv)

        st["pending"] = pend

    if own:
        jobs = [[(wv, j * 128), (wv, QK + j * 128)] for j in range(16)]
    else:
        jobs = [[(wv, QK + j * 128)] for j in range(16)]
    k.Aslot.free = []
    last = gemm_B(k, KC, jobs, ntok, epb, None, wbase=0, wslots=2, pre_block=pre_block, blocks=qk_blocks)
    flush_pending()
    tail = st["tfree"] + st["tkh_free"] + [qb[0].free, qb[1].free, kb[0].free, kb[1].free][0:0]
    fin = [last] + st["tfree"] + st["tkh_free"]
    for s_ in qb + kb + khs:
        fin += s_.free
    k.Wslot[0].free = list(fin)
    k.Wslot[1].free = list(fin)
    k.xr_free = list(fin)
    k.dram_ready["qk"] = k.dram_ready.get("qk", []) + [v for v in fin if v[0].name.startswith("s_st")]

    vst = [Slot(k.XR[:, i * 512:(i + 1) * 512]) for i in range(3)]
    for s_ in vst:
        s_.free = list(k.xr_free)
    cnt = [0]
    vals = []

    def mk_ep(dst, func, r0):
        def ep(nt, c0, tt, t0, ts, bank, pv):
            i = cnt[0] % 3
            cnt[0] += 1
            s_ = vst[i]
            av = P.op("act", "activation", out=s_.ap[:ts], in_=bank.ap[:ts, :], func=func, deps=[pv] + s_.free)
            bank.free = [av]
            sv = P.op("sp", "dma_start", out=dst[r0 + t0:r0 + t0 + ts, c0:c0 + 512], in_=s_.ap[:ts],
                      inc=k.s_st[i], deps=[av])
            s_.free = [sv]
            vals.append(sv)
        return ep

    k.Aslot.free = []
    gemm_A(k, KC, g["w_in0"][:, 2 * QK:2 * QK + VV], VV, ntok, mk_ep(g["v"], AF.Copy, row0))
    if own:
        k.Aslot.free = []
        gemm_A(k, KC, g["w_in0"][:, 2 * QK + VV:2 * QK + 2 * VV], VV, ntok, mk_ep(g["sr"], AF.Silu, 0))
    fin2 = []
    for s_ in vst:
        fin2 += s_.free
    k.xr_free = fin2
    k.dram_ready["qk"] = k.dram_ready.get("qk", []) + fin2


def st_gla_scan(k):
    P, g = k.P, k.g
    base_free = k.Wslot[0].free + k.Wslot[1].free + k.Aslot.free + k.xr_free
    ddeps = list(k.dram_ready["qk"])
    decay3 = k.decay.rearrange("p (j c) -> p j c", c=NCT)
    Sf_flat = k.WR.bitcast(F32)
    Sf = Sf_flat.rearrange("p (h c v) -> p h c v", h=H, c=4)
    Sb_flat = k.A[:, 0:16384]
    Sb = Sb_flat.rearrange("p (h c v) -> p h c v", h=H, c=4)
    o = 16384
    slots = []
    for i in range(2):
        d = {}
        d["kh2"] = k.A[:, o:o + 2 * QK].rearrange("p (a d) -> p a d", a=2)
        d["v2"] = k.A[:, o + 2 * QK:o + 2 * QK + 2 * VV].rearrange("p (a d) -> p a d", a=2)
        d["qT"] = k.A[:, o:o + 16 * CS].rearrange("p (j t) -> p j t", t=CS); o += 16 * CS
        d["kT"] = k.A[:, o:o + 16 * CS].rearrange("p (j t) -> p j t", t=CS); o += 16 * CS
        d["kh"] = k.A[:, o:o + QK]; o += QK
        d["v"] = k.A[:, o:o + VV]; o += VV
        d["sr"] = k.A[:, o:o + VV]; o += VV
        d["free"] = list(base_free)
        slots.append(d)
    hnbc = k.A[:, o:o + 2 * DV].bitcast(F32); o += 2 * DV
    AT = [[k.A[:, o + (i * 4 + h) * CS:o + (i * 4 + h + 1) * CS] for h in range(H)] for i in range(2)]
    o += 8 * CS
    og = [k.A[:, o + i * VV:o + (i + 1) * VV] for i in range(2)]; o += 2 * VV
    ogT = [Slot(k.A[:, o + i * 4096:o + (i + 1) * 4096].rearrange("p (j t) -> p j t", t=CS)) for i in range(2)]
    o += 8192
    sqj = k.A[:, o:o + DV]; o += DV
    assert o <= KC * T, o
    for s_ in ogT:
        s_.free = list(base_free)
    z1 = P.op("dve", "memset", Sf_flat, 0.0, deps=base_free)
    z2 = P.op("pool", "memset", Sb_flat, 0.0, deps=base_free)
    hv = P.op("sp", "dma_start", out=hnbc, in_=g["hnorm"].partition_broadcast(CS), inc=k.s_gain, deps=base_free)
    sb_ready = [[z2] for _ in range(H)]
    sf_last = [[z1] for _ in range(H)]
    og_free = [list(base_free), list(base_free)]
    at_free = [list(base_free), list(base_free)]
    o_reads = [[] for _ in range(H)]
    sb_a = [None] * H
    sb_p = [None] * H
    st_vals = []
    n_own = 0
    chunks = [(CSP * c, CSP, False, 0) for c in range(NPRE)]
    chunks.append((TP, CH, True, 0))
    chunks += [(TP + CH + CS * i, CS, True, CH + CS * i) for i in range((T - CH) // CS)]
    assert len(chunks) == NCT
    n_pre = NPRE

    def emit_loads(c):
        r0, cs, own, tl0 = chunks[c]
        sl = slots[c % 2]
        sem_a = k.s_ld[(c % 2) * 2]
        sem_b = k.s_ld[(c % 2) * 2 + 1]
        if not own:
            P.op("sp", "dma_start", out=sl["kh2"], in_=g["kh"][r0:r0 + cs, :].rearrange("(a p) d -> p a d", p=128),
                 inc=sem_a, deps=sl["free"] + ddeps)
            ld_a = P.op("sp", "dma_start", out=sl["v2"],
                        in_=g["v"][r0:r0 + cs, :].rearrange("(a p) d -> p a d", p=128), inc=sem_a)
            return ld_a, None
        P.op("sp", "dma_start", out=sl["kh"][:cs], in_=g["kh"][r0:r0 + cs, :], inc=sem_a, deps=sl["free"] + ddeps)
        ld_a = P.op("sp", "dma_start", out=sl["v"][:cs], in_=g["v"][r0:r0 + cs, :], inc=sem_a)
        ld_b = None
        if own:
            P.op("sp", "dma_start", out=sl["qT"][:, :, :cs],
                 in_=g["qT"][:, tl0:tl0 + cs].rearrange("(j p) t -> p j t", p=128), inc=sem_b)
            P.op("sp", "dma_start", out=sl["kT"][:, :, :cs],
                 in_=g["kT"][:, tl0:tl0 + cs].rearrange("(j p) t -> p j t", p=128), inc=sem_b)
            ld_b = P.op("sp", "dma_start", out=sl["sr"][:cs], in_=g["sr"][tl0:tl0 + cs, :], inc=sem_b)
        return ld_a, ld_b

    nxt = emit_loads(0)
    for c, (r0, cs, own, tl0) in enumerate(chunks):
        sl = slots[c % 2]
        ld_a, ld_b = nxt
        if c + 1 < NCT:
            nxt = emit_loads(c + 1)
        reads = []
        if own:
            oi = n_own % 2
            gv = P.op("pool", "tensor_tensor", out=sl["sr"][:cs].rearrange("p (h v) -> p h v", h=H),
                      in0=sl["sr"][:cs].rearrange("p (h v) -> p h v", h=H),
                      in1=hnbc[:cs].unsqueeze(1).broadcast_to([cs, H, DV]), op=ALU.mult, deps=[ld_b, hv])
            sc_vals = []
            for h in range(H):
                bank = next_bank(k)
                P.wait("pe", [ld_b] + bank.free)
                for kc in range(4):
                    pv = P.op("pe", "matmul", bank.ap[:cs, :cs], sl["kT"][:, 4 * h + kc, :cs],
                              sl["qT"][:, 4 * h + kc, :cs], start=(kc == 0), stop=(kc == 3),
                              inc=(k.S_pe if kc == 3 else None))
                mv = P.op("dve", "tensor_tensor", out=AT[oi][h][:cs, :cs], in0=bank.ap[:cs, :cs],
                          in1=k.tril[:cs, :cs], op=ALU.mult, deps=[pv] + at_free[oi])
                bank.free = [mv]
                sc_vals.append(mv)
            tr_list = []
            for h in range(H):
                if k.bank_i % 2 == 1:
                    k.bank_i += 1
                b0 = next_bank(k)
                b1 = next_bank(k)
                pair = k.psall[:cs, b0.idx * 512:b0.idx * 512 + 1024]
                P.wait("pe", [ld_a, sc_vals[h]] + b0.free + b1.free + sb_ready[h])
                for vh, bnk in enumerate((b0, b1)):
                    for kc in range(4):
                        P.op("pe", "matmul", bnk.ap[:cs, :], sl["qT"][:, 4 * h + kc, :cs],
                             Sb[:, h, kc, vh * 512:(vh + 1) * 512], start=(kc == 0), stop=False, inc=None)
                    pv = P.op("pe", "matmul", bnk.ap[:cs, :], AT[oi][h][:cs, :cs],
                              sl["v"][:cs, h * DV + vh * 512:h * DV + (vh + 1) * 512], start=False, stop=True,
                              inc=(k.S_pe if vh == 1 else None))
                o_reads[h] = [pv]
                col = oi * 4 + h
                sq = P.op("act", "activation", out=sqj[:cs], in_=pair, func=AF.Square,
                          accum_out=k.ss[:cs, col:col + 1], deps=[pv])
                sdv = P.op("act", "activation", out=k.sd[:cs, col:col + 1], in_=k.ss[:cs, col:col + 1],
                           func=AF.Sqrt, scale=1.0 / DV, bias=k.epsb[:cs], deps=[sq])
                rv = P.op("dve", "reciprocal", out=k.rstd[:cs, col:col + 1], in_=k.sd[:cs, col:col + 1], deps=[sdv])
                ov = P.op("dve", "scalar_tensor_tensor", out=og[oi][:cs, h * DV:(h + 1) * DV], in0=pair,
                          scalar=k.rstd[:cs, col:col + 1], in1=sl["sr"][:cs, h * DV:(h + 1) * DV],
                          op0=ALU.mult, op1=ALU.mult, deps=[rv, gv] + og_free[oi])
                b0.free = [ov]
                b1.free = [ov]
                tr_list.append(ov)
                reads.append(ov)
        last_pe_read = None
        sfv = None
        for h in range(H):
            for kc in range(4):
                for vh in range(2):
                    bank = next_bank(k)
                    P.wait("pe", [ld_a] + bank.free)
                    if own:
                        pv = P.op("pe", "matmul", bank.ap[:, :],
                                  sl["kh"][:cs, (4 * h + kc) * 128:(4 * h + kc + 1) * 128],
                                  sl["v"][:cs, h * DV + vh * 512:h * DV + (vh + 1) * 512], start=True, stop=True,
                                  inc=k.S_pe)
                    else:
                        for a_ in range(2):
                            pv = P.op("pe", "matmul", bank.ap[:, :],
                                      sl["kh2"][:, a_, (4 * h + kc) * 128:(4 * h + kc + 1) * 128],
                                      sl["v2"][:, a_, h * DV + vh * 512:h * DV + (vh + 1) * 512],
                                      start=(a_ == 0), stop=(a_ == 1), inc=(k.S_pe if a_ == 1 else None))
                    sfv = P.op("dve", "scalar_tensor_tensor", out=Sf[:, h, kc, vh * 512:(vh + 1) * 512],
                               in0=Sf[:, h, kc, vh * 512:(vh + 1) * 512], scalar=decay3[:, 4 * h + kc, c:c + 1],
                               in1=bank.ap[:, :], op0=ALU.mult, op1=ALU.add, deps=[pv] + sf_last[h])
                    bank.free = [sfv]
                    if c >= n_pre - 1 and c < NCT - 1:
                        cv = P.op("act", "activation", out=Sb[:, h, kc, vh * 512:(vh + 1) * 512],
                                  in_=Sf[:, h, kc, vh * 512:(vh + 1) * 512], func=AF.Copy,
                                  deps=[sfv] + o_reads[h])
                        sb_ready[h] = [cv]
                    last_pe_read = pv
            sf_last[h] = [sfv]
        reads.append(last_pe_read)
        if own:
            oslot = ogT[n_own % 2]
            ev = None
            pv = None
            for h in range(H):
                bank = next_bank(k)
                pb = bank.ap.bitcast(BF16)
                P.wait("pe", [tr_list[h]] + bank.free)
                for kk in range(8):
                    pv = P.op("pe", "transpose", out=pb[:, kk * cs:(kk + 1) * cs],
                              in_=og[oi][:cs, h * DV + kk * 128:h * DV + (kk + 1) * 128], identity=k.identb[:cs, :cs],
                              inc=(k.S_pe if kk == 7 else None))
                ev = P.op("act", "activation", out=oslot.ap[:, h * 8:(h + 1) * 8, :cs],
                          in_=pb[:, :8 * cs].rearrange("p (a t) -> p a t", t=cs), func=AF.Copy,
                          deps=[pv] + oslot.free)
                bank.free = [ev]
            og_free[oi] = [pv]
            at_free[oi] = list(o_reads[H - 1])
            sv = P.op("sp", "dma_start",
                      out=g["ogT"][:, tl0:tl0 + cs].rearrange("(j p) t -> p j t", p=128),
                      in_=oslot.ap[:, :, :cs], inc=k.s_st[n_own % 2], deps=[ev])
            oslot.free = [sv]
            st_vals.append(sv)
            n_own += 1
        sl["free"] = reads
    k.dram_ready["ogT"] = st_vals[-2:]
    fin = st_vals[-2:] + [last_pe_read] + sf_last[H - 1]
    k.Wslot[0].free = list(fin)
    k.Wslot[1].free = list(fin)
    k.Aslot.free = list(fin)
    k.xr_free = list(fin)


def st_gla(k):
    g = k.g
    st_gla_inproj(k, g["xp"], TP, 0, 0, own=False)
    st_gla_inproj(k, g["xo"], T, TP, NCHP, own=True)
    st_gla_scan(k)
    st_resid_gemm(k, "w_out0", g["ogT"], KC, g["w_out0"], g["xo"], g["h"], a_deps=k.dram_ready["ogT"])


ALL_STAGES = ["gla", "ffn0", "conv", "ffn1", "final"]


def shared_inputs(inputs):
    f = np.float32
    sh = {}
    sh["w_in0"] = np.ascontiguousarray(inputs["gla_w_in"][0], dtype=f)
    sh["w_a2"] = np.ascontiguousarray(inputs["gla_w_a2"][0], dtype=f)
    sh["b_a_t"] = np.ascontiguousarray(np.asarray(inputs["gla_b_a"][0], dtype=f).reshape(16, 128).T)
    sh["hnorm"] = np.ascontiguousarray(inputs["gla_head_norm"][0], dtype=f)
    sh["w_out0"] = np.ascontiguousarray(inputs["gla_w_out"][0], dtype=f)
    sh["cw_in"] = np.ascontiguousarray(inputs["conv_w_in"][0], dtype=f)
    sh["cw_t"] = np.ascontiguousarray(
        np.asarray(inputs["conv_w"][0], dtype=f).reshape(3, KC, 128).transpose(2, 0, 1).reshape(128, 3 * KC))
    sh["cw_out"] = np.ascontiguousarray(inputs["conv_w_out"][0], dtype=f)
    for l in range(2):
        sh[f"wg{l}"] = np.ascontiguousarray(inputs["ffn_w_gate"][l], dtype=f)
        sh[f"wu{l}"] = np.ascontiguousarray(inputs["ffn_w_up"][l], dtype=f)
        sh[f"wd{l}"] = np.ascontiguousarray(inputs["ffn_w_down"][l], dtype=f)
    sh["nmix"] = np.ascontiguousarray(inputs["norm_mix"], dtype=f)
    sh["nffn"] = np.ascontiguousarray(inputs["norm_ffn"], dtype=f)
    sh["nfin"] = np.ascontiguousarray(inputs["norm_final"], dtype=f)
    sh["ident"] = np.eye(128, dtype=f)
    sh["tril"] = np.triu(np.ones((CS, CS), dtype=f))
    return sh


def core_tokens(x_b, meta):
    f = np.float32
    seq = np.concatenate([np.zeros((CH - meta.shape[0], D), f), np.asarray(meta, f), np.asarray(x_b, f)], axis=0)
    a = {"xo": np.ascontiguousarray(seq[0:T]), "xp": np.zeros((TP, D), f)}
    b = {"xo": np.ascontiguousarray(seq[TP:TP + T]), "xp": np.ascontiguousarray(seq[0:TP])}
    return a, b


_NC_CACHE = {}


def kernel(**inputs):
    x = np.asarray(inputs["x"])
    B = x.shape[0]
    sh = shared_inputs(inputs)
    in_maps = []
    for b in range(B):
        a, c = core_tokens(x[b], inputs["meta"])
        for t in (a, c):
            m = dict(sh)
            m.update(t)
            in_maps.append(m)
    if "nc" not in _NC_CACHE:
        _NC_CACHE["nc"] = build(ALL_STAGES)
    nc = _NC_CACHE["nc"]
    res = run_bass_kernel_spmd(nc, in_maps, core_ids=list(range(2 * B)))
    out = np.empty((B, 2 * NOUT, D), np.float32)
    for b in range(B):
        for hf in range(2):
            out[b, hf * NOUT:(hf + 1) * NOUT] = res.results[2 * b + hf]["out"]
    return out
```

```python
import numpy as np
from contextlib import ExitStack

import concourse.bass as bass
import concourse.mybir as mybir
from concourse.bass_utils import run_bass_kernel_spmd

F32 = mybir.dt.float32
BF16 = mybir.dt.bfloat16
AF = mybir.ActivationFunctionType
ALU = mybir.AluOpType

D = 4096
KC = D // 128
T = 2112
TP = 2048
CH = 64
NCH = T // CH
NCHP = TP // CH
CS = 128
CSP = 256
NPRE = TP // CSP
NCT = NPRE + 1 + (T - CH) // CS
DFF = 11008
FC = DFF // 128
H = 4
DK = 512
DV = 1024
QK = H * DK
VV = H * DV
GIN = 2 * QK + 2 * VV + 16
EPS = 1e-6
NOUT = 2048


def tok_tiles(n):
    r = []
    t = 0
    while t < n:
        s = min(128, n - t)
        r.append((t, s))
        t += s
    return r


def tok_blocks(n):
    r = []
    t = 0
    while t < n:
        s = min(512, n - t)
        r.append((t, s))
        t += s
    return r


class Sem:
    def __init__(self, h, name):
        self.h = h
        self.n = 0
        self.name = name


class Slot:
    def __init__(self, ap=None):
        self.ap = ap
        self.ready = []
        self.free = []


class Prog:
    ENG = ("sp", "act", "pool", "pe", "dve")

    def __init__(self, nc, es):
        self.nc = nc
        self.es = es
        self.q = {e: [] for e in self.ENG}
        self.waited = {e: {} for e in self.ENG}
        self.nsem = 0
        self.cnt = {e: 0 for e in self.ENG}
        self.auto = {}

    def sem(self, name):
        h = self.es.enter_context(self.nc.semaphore(name))
        self.nsem += 1
        return Sem(h, name)

    def wait(self, e, deps):
        best = {}
        for (s, v) in deps:
            if v > best.get(s.name, (None, 0))[1]:
                best[s.name] = (s, v)
        w = self.waited[e]
        for name, (s, v) in best.items():
            if w.get(name, 0) >= v:
                continue
            w[name] = v
            self.q[e].append(("w", s.h, v))

    def op(self, e, method, *args, inc="auto", deps=None, **kw):
        if deps:
            self.wait(e, deps)
        val = None
        amt = 0
        if inc == "auto":
            inc = self.auto.get(e) if method != "dma_start" else None
        if inc is not None:
            amt = 16 if method == "dma_start" else 1
            inc.n += amt
            val = (inc, inc.n)
        self.q[e].append(("o", method, args, kw, inc.h if inc is not None else None, amt))
        self.cnt[e] += 1
        return val

    def run(self, e, eng):
        for it in self.q[e]:
            if it[0] == "w":
                eng.wait_ge(it[1], it[2])
            else:
                _, method, args, kw, sh, amt = it
                ins = getattr(eng, method)(*args, **kw)
                if sh is not None:
                    ins.then_inc(sh, amt)


class K:
    pass


def build(stages, ext_in=(), ext_out=(), debug_T=None):
    nc = bass.Bass("TRN2", target_bir_lowering=False)
    es = ExitStack()
    P = Prog(nc, es)
    k = K()
    k.nc, k.P = nc, P

    def dram(name, shape, dt=F32, inp=False, out=False):
        if inp or name in ext_in:
            kind = "ExternalInput"
        elif out or name in ext_out:
            kind = "ExternalOutput"
        else:
            kind = "Internal"
        return nc.dram_tensor(name, list(shape), dt, kind=kind).ap()

    g = {}
    g["xo"] = dram("xo", [T, D], inp=True)
    g["xp"] = dram("xp", [TP, D], inp=True)
    g["w_in0"] = dram("w_in0", [D, GIN], inp=True)
    g["w_a2"] = dram("w_a2", [16, QK], inp=True)
    g["b_a_t"] = dram("b_a_t", [128, 16], inp=True)
    g["hnorm"] = dram("hnorm", [DV], inp=True)
    g["w_out0"] = dram("w_out0", [VV, D], inp=True)
    g["cw_in"] = dram("cw_in", [D, 3 * D], inp=True)
    g["cw_t"] = dram("cw_t", [128, 3 * KC], inp=True)
    g["cw_out"] = dram("cw_out", [D, D], inp=True)
    for l in range(2):
        g[f"wg{l}"] = dram(f"wg{l}", [D, DFF], inp=True)
        g[f"wu{l}"] = dram(f"wu{l}", [D, DFF], inp=True)
        g[f"wd{l}"] = dram(f"wd{l}", [DFF, D], inp=True)
    g["nmix"] = dram("nmix", [2, D], inp=True)
    g["nffn"] = dram("nffn", [2, D], inp=True)
    g["nfin"] = dram("nfin", [D], inp=True)
    g["ident"] = dram("ident", [128, 128], inp=True)
    g["tril"] = dram("tril", [CS, CS], inp=True)
    g["h"] = dram("h", [T, D])
    g["qT"] = dram("qT", [QK, T], BF16)
    g["kT"] = dram("kT", [QK, T], BF16)
    g["kh"] = dram("kh", [TP + T, QK], BF16)
    g["v"] = dram("v", [TP + T, VV], BF16)
    g["sr"] = dram("sr", [T, VV], BF16)
    g["ogT"] = dram("ogT", [VV, T], BF16)
    g["yT"] = dram("yT", [D, T], BF16)
    g["hidT"] = dram("hidT", [DFF, T], BF16)
    g["out"] = dram("out", [NOUT, D], out=True)
    k.g = g

    A_t = nc.alloc_sbuf_tensor("A", [128, KC * T], BF16)
    WR_t = nc.alloc_sbuf_tensor("WR", [128, 32768], BF16)
    k.A = A_t[:, :]
    k.WR = WR_t[:, :]
    k.A3 = k.A.rearrange("p (c t) -> p c t", t=T)
    k.identb = nc.alloc_sbuf_tensor("identb", [128, 128], BF16)[:, :]
    k.tril = nc.alloc_sbuf_tensor("trilm", [CS, CS], F32)[:, :]
    k.decay = nc.alloc_sbuf_tensor("decay", [128, 16 * NCT], F32)[:, :]
    k.nba = nc.alloc_sbuf_tensor("nba", [128, 16], F32)[:, :]
    k.cw = nc.alloc_sbuf_tensor("cw", [128, 3 * KC], F32)[:, :]
    k.ss = nc.alloc_sbuf_tensor("ss", [128, 20], F32)[:, :]
    k.sd = nc.alloc_sbuf_tensor("sd", [128, 20], F32)[:, :]
    k.rstd = nc.alloc_sbuf_tensor("rstd", [128, 20], F32)[:, :]
    k.epsb = nc.alloc_sbuf_tensor("epsb", [128, 1], F32)[:, :]
    k.oneb = nc.alloc_sbuf_tensor("oneb", [128, 1], F32)[:, :]
    xr_cols = (nc.sbuf_bytes_remaining - 64) // 2
    xr_cols = (xr_cols // 16) * 16
    XR_t = nc.alloc_sbuf_tensor("XR", [128, xr_cols], BF16)
    k.XR = XR_t[:, :]
    k.xr_cols = xr_cols

    k.banks = []
    psall = nc.alloc_psum_tensor("psall", [128, 4096], F32)
    k.psall = psall[:, :]
    for b in range(8):
        s = Slot(k.psall[:, b * 512:(b + 1) * 512])
        s.idx = b
        k.banks.append(s)
    k.bank_i = 0

    k.S_pe = P.sem("S_pe")
    k.S_dve = P.sem("S_dve")
    k.S_act = P.sem("S_act")
    k.S_pool = P.sem("S_pool")
    k.s_w = [P.sem(f"s_w{i}") for i in range(4)]
    k.s_ld = [P.sem(f"s_ld{i}") for i in range(6)]
    k.s_st = [P.sem(f"s_st{i}") for i in range(6)]
    k.s_a = [P.sem(f"s_a{i}") for i in range(8)]
    k.A_kc = None
    k.s_misc = P.sem("s_misc")
    k.s_gain = P.sem("s_gain")
    k.s_cp = P.sem("s_cp")
    P.auto = {"act": k.S_act, "dve": k.S_dve, "pool": k.S_pool}

    k.Aslot = Slot(k.A)
    k.Wslot = [Slot(), Slot()]
    k.dram_ready = {}

    from_stages(k, stages)

    with nc.Block() as block:
        @block.sync
        def _(e):
            P.run("sp", e)

        @block.scalar
        def _(e):
            P.run("act", e)

        @block.gpsimd
        def _(e):
            P.run("pool", e)

        @block.tensor
        def _(e):
            P.run("pe", e)

        @block.vector
        def _(e):
            P.run("dve", e)
    es.close()
    return nc


def k_tiles(k, ntok):
    if getattr(k, "skip_halo", False) and ntok == T:
        return [(CH + 128 * i, 128) for i in range((T - CH) // 128)]
    return tok_tiles(ntok)


def k_blocks(k, ntok):
    if getattr(k, "skip_halo", False) and ntok == T:
        return [(CH + 512 * i, 512) for i in range((T - CH) // 512)]
    return tok_blocks(ntok)


def look_free(k, n=4):
    i = k.bank_i
    if i % n != 0:
        return []
    deps = []
    for j in range(n):
        deps += k.banks[(i + j) % 8].free
    return deps


def next_bank(k):
    b = k.banks[k.bank_i % 8]
    k.bank_i += 1
    return b


def wr_f32(k, c0, n):
    return k.WR[:, c0:c0 + 2 * n].bitcast(F32)


def xr_f32(k, c0, n):
    return k.XR[:, c0:c0 + 2 * n].bitcast(F32)


def st_setup(k):
    P, g = k.P, k.g
    tmp = wr_f32(k, 0, 128)
    v1 = P.op("sp", "dma_start", out=tmp, in_=g["ident"], inc=k.s_misc)
    v2 = P.op("sp", "dma_start", out=k.tril, in_=g["tril"], inc=k.s_misc)
    v3 = P.op("sp", "dma_start", out=k.nba, in_=g["b_a_t"], inc=k.s_misc)
    v4 = P.op("sp", "dma_start", out=k.cw, in_=g["cw_t"], inc=k.s_misc)
    P.op("dve", "tensor_copy", out=k.identb, in_=tmp, deps=[v4])
    P.op("dve", "tensor_scalar", out=k.nba, in0=k.nba, scalar1=-1.0, scalar2=None, op0=ALU.mult, deps=[v3])
    P.op("dve", "memset", k.epsb, EPS)
    P.op("dve", "memset", k.oneb, 1.0)
    vv = P.op("dve", "memset", k.decay, 1.0, inc=k.S_dve)
    k.setup_done = [vv]
    k.Wslot[0].free = [vv]
    k.Wslot[1].free = [vv]


def st_norm_T(k, src, gain_row, ntok, src_deps=()):
    P = k.P
    xb = [Slot(wr_f32(k, 0, 4096)), Slot(wr_f32(k, 8192, 4096))]
    gbc = wr_f32(k, 16384, 4096)
    hnb = [Slot(k.WR[:, 24576:28672]), Slot(k.WR[:, 28672:32768])]
    wfree = k.Wslot[0].free + k.Wslot[1].free + list(k.setup_done)
    for s in xb + hnb:
        s.free = list(wfree)
    gv = P.op("sp", "dma_start", out=gbc, in_=gain_row.partition_broadcast(128), inc=k.s_gain,
              deps=wfree)
    tiles = k_tiles(k, ntok)
    a_ready = []
    st = {"last_pe": None, "ev_i": 0}

    def back_half(tt, t0, ts, dv):
        hb = hnb[tt % 2]
        pv = None
        for q in range(4):
            bank = next_bank(k)
            pb = bank.ap.bitcast(BF16)
            P.wait("pe", [dv] + bank.free)
            for kk in range(8):
                kc = q * 8 + kk
                pv = P.op("pe", "transpose", out=pb[:, kk * 128:kk * 128 + ts],
                          in_=hb.ap[:ts, kc * 128:(kc + 1) * 128], identity=k.identb[:ts, :ts],
                          inc=(k.S_pe if kk == 7 else None))
            eng = "act" if st["ev_i"] % 2 == 0 else "dve"
            st["ev_i"] += 1
            src_v = pb.rearrange("p (a b) -> p a b", b=128)[:, :, :ts]
            dst_v = k.A3[:, q * 8:(q + 1) * 8, t0:t0 + ts]
            if eng == "act":
                ev = P.op("act", "activation", out=dst_v, in_=src_v, func=AF.Copy,
                          deps=[pv] + k.Aslot.free)
            else:
                ev = P.op("dve", "tensor_copy", out=dst_v, in_=src_v, deps=[pv] + k.Aslot.free)
            bank.free = [ev]
            a_ready.append(ev)
        hb.free = [pv]
        st["last_pe"] = pv

    prev = None
    for tt, (t0, ts) in enumerate(tiles):
        b = tt % 2
        x, hb = xb[b], hnb[b]
        lv = P.op("sp", "dma_start", out=x.ap[:ts], in_=src[t0:t0 + ts, :], inc=k.s_ld[b],
                  deps=list(x.free) + list(src_deps))
        sq = P.op("act", "activation", out=hb.ap[:ts], in_=x.ap[:ts], func=AF.Square,
                  accum_out=k.ss[:ts, tt:tt + 1], deps=[lv] + hb.free)
        av = P.op("act", "activation", out=k.sd[:ts, tt:tt + 1], in_=k.ss[:ts, tt:tt + 1], func=AF.Sqrt,
                  scale=1.0 / D, bias=k.epsb[:ts], deps=[sq])
        rv = P.op("dve", "reciprocal", out=k.rstd[:ts, tt:tt + 1], in_=k.sd[:ts, tt:tt + 1], deps=[av, gv])
        dv = P.op("dve", "scalar_tensor_tensor", out=hb.ap[:ts], in0=x.ap[:ts],
                  scalar=k.rstd[:ts, tt:tt + 1], in1=gbc[:ts], op0=ALU.mult, op1=ALU.mult, deps=[rv])
        x.free = [dv]
        if prev is not None:
            back_half(*prev)
        prev = (tt, t0, ts, dv)
    back_half(*prev)
    last_pe = st["last_pe"]
    k.Aslot.ready = a_ready[-2:]
    k.A_kc = None
    k.Aslot.free = []
    k.Wslot[0].free = [last_pe, a_ready[-1], a_ready[-2]]
    k.Wslot[1].free = [last_pe, a_ready[-1], a_ready[-2]]


def a_tok_deps(k, t0, ts):
    if k.A_kc is None:
        return []
    return [v for (a, b, v) in k.A_kc if a < t0 + ts and b > t0]


def load_A(k, src, kcn, src_deps):
    P = k.P
    view = src.rearrange("(c p) t -> p c t", p=128)
    k.A_kc = []
    rngs = [(0, 128), (128, 384), (512, 512), (1024, 512), (1536, 512), (2048, 64)]
    for i, (t0, ts) in enumerate(rngs):
        v = None
        for c0 in range(0, kcn, 16):
            c1 = min(kcn, c0 + 16)
            v = P.op("sp", "dma_start", out=k.A3[:, c0:c1, t0:t0 + ts], in_=view[:, c0:c1, t0:t0 + ts],
                     inc=k.s_a[i], deps=list(k.Aslot.free) + list(src_deps))
        k.A_kc.append((t0, t0 + ts, v))
    k.Aslot.ready = []
    k.Aslot.free = []


def gemm_A(k, kcn, wsrc, ncols_total, ntok, epilogue, col_tile=512):
    P = k.P
    wv = wsrc.rearrange("(c p) n -> p c n", p=128)
    tiles = k_tiles(k, ntok)
    last = None
    for nt in range(ncols_total // col_tile):
        c0 = nt * col_tile
        ws = k.Wslot[nt % 2]
        wb = k.WR[:, (nt % 2) * 16384:(nt % 2) * 16384 + kcn * col_tile].rearrange("p (c n) -> p c n", n=col_tile)
        step = 8
        for kc0 in range(0, kcn, step):
            kc1 = min(kcn, kc0 + step)
            wl = P.op("pool", "dma_start", out=wb[:, kc0:kc1, :], in_=wv[:, kc0:kc1, c0:c0 + col_tile],
                      inc=k.s_w[nt % 2], deps=ws.free)
        ws.ready = [wl]
        for tt, (t0, ts) in enumerate(tiles):
            la = look_free(k)
            bank = next_bank(k)
            P.wait("pe", k.Aslot.ready + ws.ready + la + bank.free + a_tok_deps(k, t0, ts))
            for kc in range(kcn):
                pv = P.op("pe", "matmul", bank.ap[:ts, :col_tile], k.A3[:, kc, t0:t0 + ts], wb[:, kc, :],
                          start=(kc == 0), stop=(kc == kcn - 1),
                          inc=(k.S_pe if kc == kcn - 1 else None))
            epilogue(nt, c0, tt, t0, ts, bank, pv)
            last = pv
        ws.free = [last]
    k.Aslot.free = [last]
    return last


def gemm_B(k, kcn, jobs, ntok, ep_block, ep_row=None, wbase=0, wslots=2, mrows=128, pre_block=None,
           ksplit=1, blocks=None):
    P = k.P
    if blocks is None:
        blocks = k_blocks(k, ntok)
    ng = len(jobs[0])
    kcp = kcn // ksplit
    assert kcp * ksplit == kcn
    last = None
    wsl = [Slot() for _ in range(wslots)]
    for s in wsl:
        s.free = k.Wslot[0].free + k.Wslot[1].free
    psz = ng * kcp * 128
    for j, job in enumerate(jobs):
        pieces = []
        for p in range(ksplit):
            pi = j * ksplit + p
            ws = wsl[pi % wslots]
            base = wbase + (pi % wslots) * psz
            wb = k.WR[:, base:base + psz].rearrange("p (g c n) -> p g c n", g=ng, n=128)
            wl = None
            for gi, (wv, c0) in enumerate(job):
                for kc0 in range(0, kcp, 8):
                    kc1 = min(kcp, kc0 + 8)
                    wl = P.op("pool", "dma_start", out=wb[:, gi, kc0:kc1, :mrows],
                              in_=wv[:, p * kcp + kc0:p * kcp + kc1, c0:c0 + mrows],
                              inc=k.s_w[pi % wslots], deps=ws.free)
            ws.ready = [wl]
            pieces.append((ws, wb))
        for bi, (t0, ts) in enumerate(blocks):
            xb = pre_block(j, bi, t0, ts) if pre_block is not None else []
            la = []
            banks = []
            for _ in range(ng):
                la += look_free(k)
                banks.append(next_bank(k))
            for b_ in banks:
                la += b_.free
            for p in range(ksplit):
                ws, wb = pieces[p]
                P.wait("pe", ws.ready)
                for gi in range(ng):
                    if p == 0:
                        P.wait("pe", k.Aslot.ready + (la if gi == 0 else []) + banks[gi].free + a_tok_deps(k, t0, ts))
                    for kk in range(kcp):
                        kc = p * kcp + kk
                        pv = P.op("pe", "matmul", banks[gi].ap[:mrows, :ts], wb[:, gi, kk, :mrows],
                                  k.A3[:, kc, t0:t0 + ts], start=(kc == 0), stop=(kc == kcn - 1),
                                  inc=(k.S_pe if (kc == kcn - 1 and gi == ng - 1) else None))
            ep_block(j, bi, t0, ts, xb + banks, pv)
            last = pv
        for ws, _ in pieces:
            ws.free = [last]
        if ep_row is not None:
            ep_row(j)
    k.Aslot.free = [last]
    k.Wslot[0].free = [last]
    k.Wslot[1].free = [last]
    return last


def st_resid_gemm(k, name, a_src, kcn, wsrc, h_src, h_dst, a_deps=None, from_A=False):
    P, g = k.P, k.g
    if not from_A:
        load_A(k, a_src, kcn, a_deps or [])
    nslot = 3
    hb = [Slot(k.XR[:, i * 1024:(i + 1) * 1024].bitcast(F32)) for i in range(nslot)]
    for s in hb:
        s.free = list(k.xr_free)
    st_vals = [None] * nslot
    cnt = [0]
    hdeps = list(k.dram_ready.get("h", []))

    def ep(nt, c0, tt, t0, ts, bank, pv):
        i = cnt[0] % nslot
        cnt[0] += 1
        s = hb[i]
        lv = P.op("sp", "dma_start", out=s.ap[:ts], in_=h_src[t0:t0 + ts, c0:c0 + 512], inc=k.s_ld[i],
                  deps=s.free + hdeps)
        dv = P.op("dve", "tensor_tensor", out=s.ap[:ts], in0=bank.ap[:ts, :], in1=s.ap[:ts], op=ALU.add,
                  inc=k.S_dve, deps=[pv, lv])
        bank.free = [dv]
        sv = P.op("sp", "dma_start", out=h_dst[t0:t0 + ts, c0:c0 + 512], in_=s.ap[:ts], inc=k.s_st[i],
                  deps=[dv])
        s.free = [sv]
        st_vals[i] = sv

    gemm_A(k, kcn, wsrc, D, T, ep)
    k.dram_ready["h"] = [v for v in st_vals if v is not None]
    k.xr_free = [v for v in st_vals if v is not None]


def st_ffn(k, l):
    P, g = k.P, k.g
    st_norm_T(k, g["h"], g["nffn"][l], T, src_deps=k.dram_ready.get("h", []))
    wgv = g[f"wg{l}"].rearrange("(c p) n -> p c n", p=128)
    wuv = g[f"wu{l}"].rearrange("(c p) n -> p c n", p=128)
    jobs = [[(wgv, j * 128), (wuv, j * 128)] for j in range(FC)]
    tmp = [Slot(wr_f32(k, 16384 + i * 1024, 512)) for i in range(2)]
    rows = [Slot(k.WR[:, 20480 + i * 2304:20480 + i * 2304 + T]) for i in range(2)]
    base_free = k.Wslot[0].free + k.Wslot[1].free
    for s in tmp + rows:
        s.free = list(base_free)
    cnt = [0]
    row_st = [None, None]
    hid_deps = list(k.dram_ready.get("hidT_free", []))

    def epb(j, bi, t0, ts, banks, pv):
        i = cnt[0] % 2
        cnt[0] += 1
        tm = tmp[i]
        row = rows[j % 2]
        av = P.op("act", "activation", out=tm.ap[:, :ts], in_=banks[0].ap[:, :ts], func=AF.Silu, inc=k.S_act,
                  deps=[pv] + tm.free)
        dv = P.op("dve", "tensor_tensor", out=row.ap[:, t0:t0 + ts], in0=banks[1].ap[:, :ts], in1=tm.ap[:, :ts],
                  op=ALU.mult, inc=k.S_dve, deps=[av] + row.free)
        tm.free = [dv]
        banks[0].free = [dv]
        banks[1].free = [dv]
        row.last = dv

    def epr(j):
        row = rows[j % 2]
        sv = P.op("sp", "dma_start", out=g["hidT"][j * 128:(j + 1) * 128, :], in_=row.ap, inc=k.s_st[j % 2],
                  deps=[row.last] + hid_deps)
        row.free = [sv]
        row_st[j % 2] = sv

    last = gemm_B(k, KC, jobs, T, epb, epr, wbase=0, wslots=2)
    k.dram_ready["hidT"] = [v for v in row_st if v is not None]
    k.Wslot[0].free = [last] + k.dram_ready["hidT"]
    k.Wslot[1].free = [last] + k.dram_ready["hidT"]
    parts = [(0, 29), (29, 29), (58, 28)]
    for (c0, cn) in parts:
        st_resid_gemm(k, f"down{l}", g["hidT"][c0 * 128:(c0 + cn) * 128, :], cn,
                      g[f"wd{l}"][c0 * 128:(c0 + cn) * 128, :], g["h"], g["h"],
                      a_deps=k.dram_ready["hidT"])
    k.dram_ready["hidT_free"] = [k.Aslot.free[0]]


def st_conv(k):
    P, g = k.P, k.g
    st_norm_T(k, g["h"], g["nmix"][1], T, src_deps=k.dram_ready.get("h", []))
    wv = g["cw_in"].rearrange("(c p) n -> p c n", p=128)
    jobs = [[(wv, j * 128), (wv, D + j * 128), (wv, 2 * D + j * 128)] for j in range(KC)]
    o = 18432
    z = k.WR[:, o:o + 2 * (T + 2)].bitcast(F32)
    o += 2 * (T + 2) + 12
    bgr = k.WR[:, o:o + 2 * T].bitcast(F32)
    o += 2 * T
    cr = k.WR[:, o:o + 2 * T].bitcast(F32)
    o += 2 * T
    assert o <= 32768, o
    tmp = [Slot(k.XR[:, T + i * 1024:T + (i + 1) * 1024].bitcast(F32)) for i in range(1)]
    assert T + 1024 <= k.xr_cols
    yrow = [Slot(k.XR[:, 0:T])]
    base_free = k.Wslot[0].free + k.Wslot[1].free
    for s in tmp:
        s.free = list(base_free)
    yrow[0].free = list(k.xr_free)
    zs = Slot(z)
    zs.free = list(base_free)
    cnt = [0]
    st = {"zlast": None, "sv": None, "alast": None}
    mz = P.op("dve", "memset", z[:, 0:2], 0.0, deps=base_free)

    def epb(j, bi, t0, ts, banks, pv):
        i = 0
        cnt[0] += 1
        tm = tmp[i]
        P.op("act", "activation", out=tm.ap[:, :ts], in_=banks[1].ap[:, :ts], func=AF.Copy,
             deps=[pv] + tm.free + zs.free)
        av = P.op("act", "activation", out=bgr[:, t0:t0 + ts], in_=banks[0].ap[:, :ts], func=AF.Copy)
        st["alast"] = av
        dv = P.op("dve", "tensor_tensor", out=z[:, 2 + t0:2 + t0 + ts], in0=banks[2].ap[:, :ts], in1=tm.ap[:, :ts],
                  op=ALU.mult, inc=k.S_dve, deps=[av] + zs.free)
        tm.free = [dv]
        for b in banks:
            b.free = [dv]
        st["zlast"] = dv

    def epr(j):
        y = yrow[0]
        c1 = P.op("dve", "tensor_scalar", out=cr, in0=z[:, 2:2 + T], scalar1=k.cw[:, 2 * KC + j:2 * KC + j + 1],
                  scalar2=None, op0=ALU.mult, deps=y.free + [st["zlast"]])
        c2 = P.op("dve", "scalar_tensor_tensor", out=cr, in0=z[:, 1:1 + T], scalar=k.cw[:, KC + j:KC + j + 1],
                  in1=cr, op0=ALU.mult, op1=ALU.add, deps=[c1])
        c3 = P.op("dve", "scalar_tensor_tensor", out=cr, in0=z[:, 0:T], scalar=k.cw[:, j:j + 1],
                  in1=cr, op0=ALU.mult, op1=ALU.add, deps=[c2])
        dv = P.op("dve", "tensor_tensor", out=y.ap, in0=cr, in1=bgr, op=ALU.mult, deps=[c3, st["alast"]])
        zs.free = [dv]
        sv = P.op("sp", "dma_start", out=g["yT"][j * 128:(j + 1) * 128, :], in_=y.ap, inc=k.s_st[2], deps=[dv])
        y.free = [sv]
        st["sv"] = sv

    last = gemm_B(k, KC, jobs, T, epb, epr, wbase=0, wslots=3, ksplit=2)
    k.dram_ready["yT"] = [st["sv"]]
    k.skip_halo = True
    k.xr_free = [st["sv"]]
    k.Wslot[0].free = [last, st["sv"]]
    k.Wslot[1].free = [last, st["sv"]]
    st_resid_gemm(k, "cw_out", g["yT"], KC, g["cw_out"], g["h"], g["h"], a_deps=k.dram_ready["yT"])


def st_final(k):
    P, g = k.P, k.g
    xb = [Slot(wr_f32(k, 0, 4096)), Slot(wr_f32(k, 8192, 4096))]
    gbc = wr_f32(k, 16384, 4096)
    junk = k.WR[:, 24576:28672]
    wfree = k.Wslot[0].free + k.Wslot[1].free
    for s in xb:
        s.free = list(wfree)
    gv = P.op("sp", "dma_start", out=gbc, in_=g["nfin"].partition_broadcast(128), inc=k.s_gain, deps=wfree)
    hdeps = list(k.dram_ready.get("h", []))
    svs = [None, None]
    for tt in range(NOUT // 128):
        t0 = CH + tt * 128
        b = tt % 2
        x = xb[b]
        lv = P.op("sp", "dma_start", out=x.ap, in_=g["h"][t0:t0 + 128, :], inc=k.s_ld[b], deps=x.free + hdeps)
        sq = P.op("act", "activation", out=junk, in_=x.ap, func=AF.Square, accum_out=k.ss[:, tt:tt + 1],
                  deps=[lv] + wfree)
        av = P.op("act", "activation", out=k.sd[:, tt:tt + 1], in_=k.ss[:, tt:tt + 1], func=AF.Sqrt,
                  scale=1.0 / D, bias=k.epsb, deps=[sq])
        rv = P.op("dve", "reciprocal", out=k.rstd[:, tt:tt + 1], in_=k.sd[:, tt:tt + 1], deps=[av, gv])
        dv = P.op("dve", "scalar_tensor_tensor", out=x.ap, in0=x.ap, scalar=k.rstd[:, tt:tt + 1], in1=gbc,
                  op0=ALU.mult, op1=ALU.mult, deps=[rv])
        sv = P.op("sp", "dma_start", out=g["out"][tt * 128:(tt + 1) * 128, :], in_=x.ap, inc=k.s_st[b], deps=[dv])
        x.free = [sv]
        svs[b] = sv
    P.wait("sp", [v for v in svs if v is not None])


def st_copy_h(k):
    P, g = k.P, k.g
    vals = []
    for i in range(0, T, 264):
        v = P.op("sp", "dma_start", out=g["h"][i:i + 264, :], in_=g["xo"][i:i + 264, :], inc=k.s_cp)
        vals.append(v)
    k.dram_ready["h"] = [vals[-1]]


def from_stages(k, stages):
    k.xr_free = []
    st_setup(k)
    k.xr_free = list(k.setup_done)
    for s in stages:
        if s == "copy_h":
            st_copy_h(k)
        elif s == "gla":
            import_gla(k)
        elif s == "ffn0":
            st_ffn(k, 0)
        elif s == "conv":
            st_conv(k)
        elif s == "ffn1":
            st_ffn(k, 1)
        elif s == "final":
            st_final(k)
        else:
            raise ValueError(s)


def import_gla(k):
    st_gla(k)


def st_gla_inproj(k, src, ntok, row0, ch0, own):
    P, g = k.P, k.g
    st_norm_T(k, src, g["nmix"][0], ntok)
    wv = g["w_in0"].rearrange("(c p) n -> p c n", p=128)
    base_free = k.Wslot[0].free + k.Wslot[1].free
    alowT = k.WR[:16, 16384:16384 + T]
    wa2b = k.WR[:16, 18496:18496 + QK]
    wa2f = k.WR[:16, 20544:20544 + 2 * QK].bitcast(F32)
    o = 24640
    tl = k.WR[:, o:o + 1024].bitcast(F32); o += 1024
    tB = k.WR[:, o:o + 1024].bitcast(F32); o += 1024
    teq = k.WR[:, o:o + 1024].bitcast(F32); o += 1024
    tek = k.WR[:, o:o + 1024].bitcast(F32); o += 1024
    tk = k.WR[:, o:o + 1024].bitcast(F32); o += 1024
    tkh = k.WR[:, o:o + 512]; o += 512
    qb = [Slot(k.WR[:, o + i * 512:o + (i + 1) * 512]) for i in range(2)]; o += 1024
    kb = [Slot(k.WR[:, o + i * 512:o + (i + 1) * 512]) for i in range(2)]; o += 1024
    assert o <= 32768, o
    khs = [Slot(k.XR[:, i * 512:(i + 1) * 512].rearrange("p (a d) -> p a d", d=128)) for i in range(2)]
    mask = k.XR[:, 1024:1536]
    for s_ in qb + kb:
        s_.free = list(base_free)
    for s_ in khs:
        s_.free = list(k.xr_free)
    decay3 = k.decay.rearrange("p (j c) -> p j c", c=NCT)
    if own:
        qk_blocks = [(0, CH)] + [(CH + 512 * i, 512) for i in range(4)]
    else:
        qk_blocks = tok_blocks(ntok)

    def chunk_of(t0):
        if not own:
            return CSP, t0 // CSP
        if t0 == 0:
            return CH, NPRE
        return CS, NPRE + 1 + (t0 - CH) // CS
    lv = P.op("sp", "dma_start", out=wa2f, in_=g["w_a2"], inc=k.s_misc, deps=base_free)
    wa_v = P.op("dve", "tensor_copy", out=wa2b, in_=wa2f, deps=[lv] + base_free)
    m1 = P.op("dve", "memset", mask, 1.0, deps=list(k.xr_free))
    m2 = P.op("dve", "memset", mask.rearrange("p (c i) -> p c i", i=(CS if own else CSP))[:, :, 0:1], 0.0,
              deps=[m1])
    st = {"tfree": list(base_free) + [m2], "tkh_free": list(base_free), "pending": None, "cnt": 0,
          "kst": [], "alow": None}

    def ep_alow(j, bi, t0, ts, banks, pv):
        av = P.op("act", "activation", out=alowT[:, t0:t0 + ts], in_=banks[0].ap[:16, :ts], func=AF.Copy,
                  deps=[pv] + base_free)
        banks[0].free = [av]
        st["alow"] = av

    gemm_B(k, KC, [[(wv, 2 * QK + 2 * VV)]], ntok, ep_alow, None, wbase=0, wslots=2, mrows=16)
    alow_ready = [st["alow"], wa_v]

    def pre_block(j, bi, t0, ts):
        bx = next_bank(k)
        P.wait("pe", alow_ready + bx.free)
        P.op("pe", "matmul", bx.ap[:, :ts], wa2b[:, j * 128:(j + 1) * 128], alowT[:, t0:t0 + ts],
             start=True, stop=True, inc=None)
        return [bx]

    def flush_pending():
        if st["pending"] is not None:
            st["pending"]()
            st["pending"] = None

    def epb(j, bi, t0, ts, banks, pv):
        flush_pending()
        if own:
            bx, bq, bk = banks
        else:
            bx, bk = banks
            bq = None
        csz, cb = chunk_of(t0)
        nchb = ts // csz
        e1 = P.op("act", "activation", out=tl[:, :ts], in_=bx.ap[:, :ts], func=AF.Exp, scale=-1.0,
                  bias=k.nba[:, j:j + 1], deps=[pv] + st["tfree"])
        bx.free = [e1]
        l1 = P.op("act", "activation", out=tl[:, :ts], in_=tl[:, :ts], func=AF.Ln, scale=1.0, bias=k.oneb,
                  deps=[e1])
        sc = P.op("dve", "tensor_tensor_scan", out=tB[:, :ts], data0=mask[:, :ts], data1=tl[:, :ts], initial=0.0,
                  op0=ALU.mult, op1=ALU.add, deps=[l1] + st["tfree"])
        eqv = P.op("act", "activation", out=teq[:, :ts], in_=tB[:, :ts], func=AF.Exp, scale=-1.0 / 16, deps=[sc])
        ekv = P.op("act", "activation", out=tek[:, :ts], in_=tB[:, :ts], func=AF.Exp, scale=1.0 / 16)
        dcv = P.op("act", "activation", out=decay3[:, j, cb:cb + nchb],
                   in_=tB[:, :ts].rearrange("p (c i) -> p c i", i=csz)[:, :, csz - 1], func=AF.Exp, scale=-1.0 / 16)
        i = st["cnt"] % 2
        st["cnt"] += 1
        lastd = None
        if own:
            q_ = qb[i]
            qv = P.op("dve", "scalar_tensor_tensor", out=q_.ap[:, :ts], in0=bq.ap[:, :ts], scalar=float(DK) ** -0.5,
                      in1=teq[:, :ts], op0=ALU.mult, op1=ALU.mult, deps=[eqv] + q_.free)
            bq.free = [qv]
            sv = P.op("sp", "dma_start", out=g["qT"][j * 128:(j + 1) * 128, t0:t0 + ts], in_=q_.ap[:, :ts],
                      inc=k.s_st[i], deps=[qv])
            q_.free = [sv]
            st["kst"].append(sv)
        tkv = P.op("dve", "tensor_tensor", out=tk[:, :ts], in0=bk.ap[:, :ts], in1=tek[:, :ts], op=ALU.mult,
                   deps=[ekv])
        bk.free = [tkv]
        if own:
            k_ = kb[i]
            kv = P.op("act", "activation", out=k_.ap[:, :ts], in_=tk[:, :ts], func=AF.Copy, deps=[tkv] + k_.free)
            sv = P.op("sp", "dma_start", out=g["kT"][j * 128:(j + 1) * 128, t0:t0 + ts], in_=k_.ap[:, :ts],
                      inc=k.s_st[2 + i], deps=[kv])
            k_.free = [sv]
            st["kst"].append(sv)
            lastd = kv
        khv = P.op("dve", "tensor_tensor", out=tkh[:, :ts].rearrange("p (c i) -> p c i", i=csz),
                   in0=tk[:, :ts].rearrange("p (c i) -> p c i", i=csz),
                   in1=decay3[:, j, cb:cb + nchb].unsqueeze(2).broadcast_to([128, nchb, csz]),
                   op=ALU.mult, deps=[tkv, dcv] + st["tkh_free"])
        st["tfree"] = [khv] + ([lastd] if lastd is not None else [])

        def pend(j=j, t0=t0, ts=ts, khv=khv, i=i):
            bank = next_bank(k)
            pb = bank.ap.bitcast(BF16)
            tls = tok_tiles(ts)
            P.wait("pe", [khv] + bank.free)
            for ti, (a0, asz) in enumerate(tls):
                pv2 = P.op("pe", "transpose", out=pb[:asz, ti * 128:(ti + 1) * 128], in_=tkh[:, a0:a0 + asz],
                           identity=k.identb, inc=(k.S_pe if ti == len(tls) - 1 else None))
            st["tkh_free"] = [pv2]
            hs = khs[i]
            nt_ = len(tls)
            asz = tls[0][1]
            ev = P.op("act", "activation", out=hs.ap[:asz, :nt_, :],
                      in_=pb[:asz, :nt_ * 128].rearrange("p (a d) -> p a d", d=128), func=AF.Copy,
                      deps=[pv2] + hs.free)
            bank.free = [ev]
            r0 = row0 + t0
            if asz == 128:
                dst = g["kh"][r0:r0 + ts, j * 128:(j + 1) * 128].rearrange("(a p) d -> p a d", p=128)
            else:
                dst = g["kh"][r0:r0 + ts, j * 128:(j + 1) * 128].rearrange("(a p) d -> p a d", p=asz)
            sv = P.op("sp", "dma_start", out=dst, in_=hs.ap[:asz, :nt_, :], inc=k.s_st[4 + i], deps=[ev])
            hs.free = [sv]
            st["kst"].append(sv)

        st["pending"] = pend

    if own:
        jobs = [[(wv, j * 128), (wv, QK + j * 128)] for j in range(16)]
    else:
        jobs = [[(wv, QK + j * 128)] for j in range(16)]
    k.Aslot.free = []
    last = gemm_B(k, KC, jobs, ntok, epb, None, wbase=0, wslots=2, pre_block=pre_block, blocks=qk_blocks)
    flush_pending()
    tail = st["tfree"] + st["tkh_free"] + [qb[0].free, qb[1].free, kb[0].free, kb[1].free][0:0]
    fin = [last] + st["tfree"] + st["tkh_free"]
    for s_ in qb + kb + khs:
        fin += s_.free
    k.Wslot[0].free = list(fin)
    k.Wslot[1].free = list(fin)
    k.xr_free = list(fin)
    k.dram_ready["qk"] = k.dram_ready.get("qk", []) + [v for v in fin if v[0].name.startswith("s_st")]

    vst = [Slot(k.XR[:, i * 512:(i + 1) * 512]) for i in range(3)]
    for s_ in vst:
        s_.free = list(k.xr_free)
    cnt = [0]
    vals = []

    def mk_ep(dst, func, r0):
        def ep(nt, c0, tt, t0, ts, bank, pv):
            i = cnt[0] % 3
            cnt[0] += 1
            s_ = vst[i]
            av = P.op("act", "activation", out=s_.ap[:ts], in_=bank.ap[:ts, :], func=func, deps=[pv] + s_.free)
            bank.free = [av]
            sv = P.op("sp", "dma_start", out=dst[r0 + t0:r0 + t0 + ts, c0:c0 + 512], in_=s_.ap[:ts],
                      inc=k.s_st[i], deps=[av])
            s_.free = [sv]
            vals.append(sv)
        return ep

    k.Aslot.free = []
    gemm_A(k, KC, g["w_in0"][:, 2 * QK:2 * QK + VV], VV, ntok, mk_ep(g["v"], AF.Copy, row0))
    if own:
        k.Aslot.free = []
        gemm_A(k, KC, g["w_in0"][:, 2 * QK + VV:2 * QK + 2 * VV], VV, ntok, mk_ep(g["sr"], AF.Silu, 0))
    fin2 = []
    for s_ in vst:
        fin2 += s_.free
    k.xr_free = fin2
    k.dram_ready["qk"] = k.dram_ready.get("qk", []) + fin2


def st_gla_scan(k):
    P, g = k.P, k.g
    base_free = k.Wslot[0].free + k.Wslot[1].free + k.Aslot.free + k.xr_free
    ddeps = list(k.dram_ready["qk"])
    decay3 = k.decay.rearrange("p (j c) -> p j c", c=NCT)
    Sf_flat = k.WR.bitcast(F32)
    Sf = Sf_flat.rearrange("p (h c v) -> p h c v", h=H, c=4)
    Sb_flat = k.A[:, 0:16384]
    Sb = Sb_flat.rearrange("p (h c v) -> p h c v", h=H, c=4)
    o = 16384
    slots = []
    for i in range(2):
        d = {}
        d["kh2"] = k.A[:, o:o + 2 * QK].rearrange("p (a d) -> p a d", a=2)
        d["v2"] = k.A[:, o + 2 * QK:o + 2 * QK + 2 * VV].rearrange("p (a d) -> p a d", a=2)
        d["qT"] = k.A[:, o:o + 16 * CS].rearrange("p (j t) -> p j t", t=CS); o += 16 * CS
        d["kT"] = k.A[:, o:o + 16 * CS].rearrange("p (j t) -> p j t", t=CS); o += 16 * CS
        d["kh"] = k.A[:, o:o + QK]; o += QK
        d["v"] = k.A[:, o:o + VV]; o += VV
        d["sr"] = k.A[:, o:o + VV]; o += VV
        d["free"] = list(base_free)
        slots.append(d)
    hnbc = k.A[:, o:o + 2 * DV].bitcast(F32); o += 2 * DV
    AT = [[k.A[:, o + (i * 4 + h) * CS:o + (i * 4 + h + 1) * CS] for h in range(H)] for i in range(2)]
    o += 8 * CS
    og = [k.A[:, o + i * VV:o + (i + 1) * VV] for i in range(2)]; o += 2 * VV
    ogT = [Slot(k.A[:, o + i * 4096:o + (i + 1) * 4096].rearrange("p (j t) -> p j t", t=CS)) for i in range(2)]
    o += 8192
    sqj = k.A[:, o:o + DV]; o += DV
    assert o <= KC * T, o
    for s_ in ogT:
        s_.free = list(base_free)
    z1 = P.op("dve", "memset", Sf_flat, 0.0, deps=base_free)
    z2 = P.op("pool", "memset", Sb_flat, 0.0, deps=base_free)
    hv = P.op("sp", "dma_start", out=hnbc, in_=g["hnorm"].partition_broadcast(CS), inc=k.s_gain, deps=base_free)
    sb_ready = [[z2] for _ in range(H)]
    sf_last = [[z1] for _ in range(H)]
    og_free = [list(base_free), list(base_free)]
    at_free = [list(base_free), list(base_free)]
    o_reads = [[] for _ in range(H)]
    sb_a = [None] * H
    sb_p = [None] * H
    st_vals = []
    n_own = 0
    chunks = [(CSP * c, CSP, False, 0) for c in range(NPRE)]
    chunks.append((TP, CH, True, 0))
    chunks += [(TP + CH + CS * i, CS, True, CH + CS * i) for i in range((T - CH) // CS)]
    assert len(chunks) == NCT
    n_pre = NPRE

    def emit_loads(c):
        r0, cs, own, tl0 = chunks[c]
        sl = slots[c % 2]
        sem_a = k.s_ld[(c % 2) * 2]
        sem_b = k.s_ld[(c % 2) * 2 + 1]
        if not own:
            P.op("sp", "dma_start", out=sl["kh2"], in_=g["kh"][r0:r0 + cs, :].rearrange("(a p) d -> p a d", p=128),
                 inc=sem_a, deps=sl["free"] + ddeps)
            ld_a = P.op("sp", "dma_start", out=sl["v2"],
                        in_=g["v"][r0:r0 + cs, :].rearrange("(a p) d -> p a d", p=128), inc=sem_a)
            return ld_a, None
        P.op("sp", "dma_start", out=sl["kh"][:cs], in_=g["kh"][r0:r0 + cs, :], inc=sem_a, deps=sl["free"] + ddeps)
        ld_a = P.op("sp", "dma_start", out=sl["v"][:cs], in_=g["v"][r0:r0 + cs, :], inc=sem_a)
        ld_b = None
        if own:
            P.op("sp", "dma_start", out=sl["qT"][:, :, :cs],
                 in_=g["qT"][:, tl0:tl0 + cs].rearrange("(j p) t -> p j t", p=128), inc=sem_b)
            P.op("sp", "dma_start", out=sl["kT"][:, :, :cs],
                 in_=g["kT"][:, tl0:tl0 + cs].rearrange("(j p) t -> p j t", p=128), inc=sem_b)
            ld_b = P.op("sp", "dma_start", out=sl["sr"][:cs], in_=g["sr"][tl0:tl0 + cs, :], inc=sem_b)
        return ld_a, ld_b

    nxt = emit_loads(0)
    for c, (r0, cs, own, tl0) in enumerate(chunks):
        sl = slots[c % 2]
        ld_a, ld_b = nxt
        if c + 1 < NCT:
            nxt = emit_loads(c + 1)
        reads = []
        if own:
            oi = n_own % 2
            gv = P.op("pool", "tensor_tensor", out=sl["sr"][:cs].rearrange("p (h v) -> p h v", h=H),
                      in0=sl["sr"][:cs].rearrange("p (h v) -> p h v", h=H),
                      in1=hnbc[:cs].unsqueeze(1).broadcast_to([cs, H, DV]), op=ALU.mult, deps=[ld_b, hv])
            sc_vals = []
            for h in range(H):
                bank = next_bank(k)
                P.wait("pe", [ld_b] + bank.free)
                for kc in range(4):
                    pv = P.op("pe", "matmul", bank.ap[:cs, :cs], sl["kT"][:, 4 * h + kc, :cs],
                              sl["qT"][:, 4 * h + kc, :cs], start=(kc == 0), stop=(kc == 3),
                              inc=(k.S_pe if kc == 3 else None))
                mv = P.op("dve", "tensor_tensor", out=AT[oi][h][:cs, :cs], in0=bank.ap[:cs, :cs],
                          in1=k.tril[:cs, :cs], op=ALU.mult, deps=[pv] + at_free[oi])
                bank.free = [mv]
                sc_vals.append(mv)
            tr_list = []
            for h in range(H):
                if k.bank_i % 2 == 1:
                    k.bank_i += 1
                b0 = next_bank(k)
                b1 = next_bank(k)
                pair = k.psall[:cs, b0.idx * 512:b0.idx * 512 + 1024]
                P.wait("pe", [ld_a, sc_vals[h]] + b0.free + b1.free + sb_ready[h])
                for vh, bnk in enumerate((b0, b1)):
                    for kc in range(4):
                        P.op("pe", "matmul", bnk.ap[:cs, :], sl["qT"][:, 4 * h + kc, :cs],
                             Sb[:, h, kc, vh * 512:(vh + 1) * 512], start=(kc == 0), stop=False, inc=None)
                    pv = P.op("pe", "matmul", bnk.ap[:cs, :], AT[oi][h][:cs, :cs],
                              sl["v"][:cs, h * DV + vh * 512:h * DV + (vh + 1) * 512], start=False, stop=True,
                              inc=(k.S_pe if vh == 1 else None))
                o_reads[h] = [pv]
                col = oi * 4 + h
                sq = P.op("act", "activation", out=sqj[:cs], in_=pair, func=AF.Square,
                          accum_out=k.ss[:cs, col:col + 1], deps=[pv])
                sdv = P.op("act", "activation", out=k.sd[:cs, col:col + 1], in_=k.ss[:cs, col:col + 1],
                           func=AF.Sqrt, scale=1.0 / DV, bias=k.epsb[:cs], deps=[sq])
                rv = P.op("dve", "reciprocal", out=k.rstd[:cs, col:col + 1], in_=k.sd[:cs, col:col + 1], deps=[sdv])
                ov = P.op("dve", "scalar_tensor_tensor", out=og[oi][:cs, h * DV:(h + 1) * DV], in0=pair,
                          scalar=k.rstd[:cs, col:col + 1], in1=sl["sr"][:cs, h * DV:(h + 1) * DV],
                          op0=ALU.mult, op1=ALU.mult, deps=[rv, gv] + og_free[oi])
                b0.free = [ov]
                b1.free = [ov]
                tr_list.append(ov)
                reads.append(ov)
        last_pe_read = None
        sfv = None
        for h in range(H):
            for kc in range(4):
                for vh in range(2):
                    bank = next_bank(k)
                    P.wait("pe", [ld_a] + bank.free)
                    if own:
                        pv = P.op("pe", "matmul", bank.ap[:, :],
                                  sl["kh"][:cs, (4 * h + kc) * 128:(4 * h + kc + 1) * 128],
                                  sl["v"][:cs, h * DV + vh * 512:h * DV + (vh + 1) * 512], start=True, stop=True,
                                  inc=k.S_pe)
                    else:
                        for a_ in range(2):
                            pv = P.op("pe", "matmul", bank.ap[:, :],
                                      sl["kh2"][:, a_, (4 * h + kc) * 128:(4 * h + kc + 1) * 128],
                                      sl["v2"][:, a_, h * DV + vh * 512:h * DV + (vh + 1) * 512],
                                      start=(a_ == 0), stop=(a_ == 1), inc=(k.S_pe if a_ == 1 else None))
                    sfv = P.op("dve", "scalar_tensor_tensor", out=Sf[:, h, kc, vh * 512:(vh + 1) * 512],
                               in0=Sf[:, h, kc, vh * 512:(vh + 1) * 512], scalar=decay3[:, 4 * h + kc, c:c + 1],
                               in1=bank.ap[:, :], op0=ALU.mult, op1=ALU.add, deps=[pv] + sf_last[h])
                    bank.free = [sfv]
                    if c >= n_pre - 1 and c < NCT - 1:
                        cv = P.op("act", "activation", out=Sb[:, h, kc, vh * 512:(vh + 1) * 512],
                                  in_=Sf[:, h, kc, vh * 512:(vh + 1) * 512], func=AF.Copy,
                                  deps=[sfv] + o_reads[h])
                        sb_ready[h] = [cv]
                    last_pe_read = pv
            sf_last[h] = [sfv]
        reads.append(last_pe_read)
        if own:
            oslot = ogT[n_own % 2]
            ev = None
            pv = None
            for h in range(H):
                bank = next_bank(k)
                pb = bank.ap.bitcast(BF16)
                P.wait("pe", [tr_list[h]] + bank.free)
                for kk in range(8):
                    pv = P.op("pe", "transpose", out=pb[:, kk * cs:(kk + 1) * cs],
                              in_=og[oi][:cs, h * DV + kk * 128:h * DV + (kk + 1) * 128], identity=k.identb[:cs, :cs],
                              inc=(k.S_pe if kk == 7 else None))
                ev = P.op("act", "activation", out=oslot.ap[:, h * 8:(h + 1) * 8, :cs],
                          in_=pb[:, :8 * cs].rearrange("p (a t) -> p a t", t=cs), func=AF.Copy,
                          deps=[pv] + oslot.free)
                bank.free = [ev]
            og_free[oi] = [pv]
            at_free[oi] = list(o_reads[H - 1])
            sv = P.op("sp", "dma_start",
                      out=g["ogT"][:, tl0:tl0 + cs].rearrange("(j p) t -> p j t", p=128),
                      in_=oslot.ap[:, :, :cs], inc=k.s_st[n_own % 2], deps=[ev])
            oslot.free = [sv]
            st_vals.append(sv)
            n_own += 1
        sl["free"] = reads
    k.dram_ready["ogT"] = st_vals[-2:]
    fin = st_vals[-2:] + [last_pe_read] + sf_last[H - 1]
    k.Wslot[0].free = list(fin)
    k.Wslot[1].free = list(fin)
    k.Aslot.free = list(fin)
    k.xr_free = list(fin)


def st_gla(k):
    g = k.g
    st_gla_inproj(k, g["xp"], TP, 0, 0, own=False)
    st_gla_inproj(k, g["xo"], T, TP, NCHP, own=True)
    st_gla_scan(k)
    st_resid_gemm(k, "w_out0", g["ogT"], KC, g["w_out0"], g["xo"], g["h"], a_deps=k.dram_ready["ogT"])


ALL_STAGES = ["gla", "ffn0", "conv", "ffn1", "final"]


def shared_inputs(inputs):
    f = np.float32
    sh = {}
    sh["w_in0"] = np.ascontiguousarray(inputs["gla_w_in"][0], dtype=f)
    sh["w_a2"] = np.ascontiguousarray(inputs["gla_w_a2"][0], dtype=f)
    sh["b_a_t"] = np.ascontiguousarray(np.asarray(inputs["gla_b_a"][0], dtype=f).reshape(16, 128).T)
    sh["hnorm"] = np.ascontiguousarray(inputs["gla_head_norm"][0], dtype=f)
    sh["w_out0"] = np.ascontiguousarray(inputs["gla_w_out"][0], dtype=f)
    sh["cw_in"] = np.ascontiguousarray(inputs["conv_w_in"][0], dtype=f)
    sh["cw_t"] = np.ascontiguousarray(
        np.asarray(inputs["conv_w"][0], dtype=f).reshape(3, KC, 128).transpose(2, 0, 1).reshape(128, 3 * KC))
    sh["cw_out"] = np.ascontiguousarray(inputs["conv_w_out"][0], dtype=f)
    for l in range(2):
        sh[f"wg{l}"] = np.ascontiguousarray(inputs["ffn_w_gate"][l], dtype=f)
        sh[f"wu{l}"] = np.ascontiguousarray(inputs["ffn_w_up"][l], dtype=f)
        sh[f"wd{l}"] = np.ascontiguousarray(inputs["ffn_w_down"][l], dtype=f)
    sh["nmix"] = np.ascontiguousarray(inputs["norm_mix"], dtype=f)
    sh["nffn"] = np.ascontiguousarray(inputs["norm_ffn"], dtype=f)
    sh["nfin"] = np.ascontiguousarray(inputs["norm_final"], dtype=f)
    sh["ident"] = np.eye(128, dtype=f)
    sh["tril"] = np.triu(np.ones((CS, CS), dtype=f))
    return sh


def core_tokens(x_b, meta):
    f = np.float32
    seq = np.concatenate([np.zeros((CH - meta.shape[0], D), f), np.asarray(meta, f), np.asarray(x_b, f)], axis=0)
    a = {"xo": np.ascontiguousarray(seq[0:T]), "xp": np.zeros((TP, D), f)}
    b = {"xo": np.ascontiguousarray(seq[TP:TP + T]), "xp": np.ascontiguousarray(seq[0:TP])}
    return a, b


_NC_CACHE = {}


def kernel(**inputs):
    x = np.asarray(inputs["x"])
    B = x.shape[0]
    sh = shared_inputs(inputs)
    in_maps = []
    for b in range(B):
        a, c = core_tokens(x[b], inputs["meta"])
        for t in (a, c):
            m = dict(sh)
            m.update(t)
            in_maps.append(m)
    if "nc" not in _NC_CACHE:
        _NC_CACHE["nc"] = build(ALL_STAGES)
    nc = _NC_CACHE["nc"]
    res = run_bass_kernel_spmd(nc, in_maps, core_ids=list(range(2 * B)))
    out = np.empty((B, 2 * NOUT, D), np.float32)
    for b in range(B):
        for hf in range(2):
            out[b, hf * NOUT:(hf + 1) * NOUT] = res.results[2 * b + hf]["out"]
    return out
```

```python
import numpy as np
from contextlib import ExitStack

import concourse.bass as bass
import concourse.mybir as mybir
from concourse.bass_utils import run_bass_kernel_spmd

F32 = mybir.dt.float32
BF16 = mybir.dt.bfloat16
AF = mybir.ActivationFunctionType
ALU = mybir.AluOpType

D = 4096
KC = D // 128
T = 2112
TP = 2048
CH = 64
NCH = T // CH
NCHP = TP // CH
CS = 128
CSP = 256
NPRE = TP // CSP
NCT = NPRE + 1 + (T - CH) // CS
DFF = 11008
FC = DFF // 128
H = 4
DK = 512
DV = 1024
QK = H * DK
VV = H * DV
GIN = 2 * QK + 2 * VV + 16
EPS = 1e-6
NOUT = 2048


def tok_tiles(n):
    r = []
    t = 0
    while t < n:
        s = min(128, n - t)
        r.append((t, s))
        t += s
    return r


def tok_blocks(n):
    r = []
    t = 0
    while t < n:
        s = min(512, n - t)
        r.append((t, s))
        t += s
    return r


class Sem:
    def __init__(self, h, name):
        self.h = h
        self.n = 0
        self.name = name


class Slot:
    def __init__(self, ap=None):
        self.ap = ap
        self.ready = []
        self.free = []


class Prog:
    ENG = ("sp", "act", "pool", "pe", "dve")

    def __init__(self, nc, es):
        self.nc = nc
        self.es = es
        self.q = {e: [] for e in self.ENG}
        self.waited = {e: {} for e in self.ENG}
        self.nsem = 0
        self.cnt = {e: 0 for e in self.ENG}
        self.auto = {}

    def sem(self, name):
        h = self.es.enter_context(self.nc.semaphore(name))
        self.nsem += 1
        return Sem(h, name)

    def wait(self, e, deps):
        best = {}
        for (s, v) in deps:
            if v > best.get(s.name, (None, 0))[1]:
                best[s.name] = (s, v)
        w = self.waited[e]
        for name, (s, v) in best.items():
            if w.get(name, 0) >= v:
                continue
            w[name] = v
            self.q[e].append(("w", s.h, v))

    def op(self, e, method, *args, inc="auto", deps=None, **kw):
        if deps:
            self.wait(e, deps)
        val = None
        amt = 0
        if inc == "auto":
            inc = self.auto.get(e) if method != "dma_start" else None
        if inc is not None:
            amt = 16 if method == "dma_start" else 1
            inc.n += amt
            val = (inc, inc.n)
        self.q[e].append(("o", method, args, kw, inc.h if inc is not None else None, amt))
        self.cnt[e] += 1
        return val

    def run(self, e, eng):
        for it in self.q[e]:
            if it[0] == "w":
                eng.wait_ge(it[1], it[2])
            else:
                _, method, args, kw, sh, amt = it
                ins = getattr(eng, method)(*args, **kw)
                if sh is not None:
                    ins.then_inc(sh, amt)


class K:
    pass


def build(stages, ext_in=(), ext_out=(), debug_T=None):
    nc = bass.Bass("TRN2", target_bir_lowering=False)
    es = ExitStack()
    P = Prog(nc, es)
    k = K()
    k.nc, k.P = nc, P

    def dram(name, shape, dt=F32, inp=False, out=False):
        if inp or name in ext_in:
            kind = "ExternalInput"
        elif out or name in ext_out:
            kind = "ExternalOutput"
        else:
            kind = "Internal"
        return nc.dram_tensor(name, list(shape), dt, kind=kind).ap()

    g = {}
    g["xo"] = dram("xo", [T, D], inp=True)
    g["xp"] = dram("xp", [TP, D], inp=True)
    g["w_in0"] = dram("w_in0", [D, GIN], inp=True)
    g["w_a2"] = dram("w_a2", [16, QK], inp=True)
    g["b_a_t"] = dram("b_a_t", [128, 16], inp=True)
    g["hnorm"] = dram("hnorm", [DV], inp=True)
    g["w_out0"] = dram("w_out0", [VV, D], inp=True)
    g["cw_in"] = dram("cw_in", [D, 3 * D], inp=True)
    g["cw_t"] = dram("cw_t", [128, 3 * KC], inp=True)
    g["cw_out"] = dram("cw_out", [D, D], inp=True)
    for l in range(2):
        g[f"wg{l}"] = dram(f"wg{l}", [D, DFF], inp=True)
        g[f"wu{l}"] = dram(f"wu{l}", [D, DFF], inp=True)
        g[f"wd{l}"] = dram(f"wd{l}", [DFF, D], inp=True)
    g["nmix"] = dram("nmix", [2, D], inp=True)
    g["nffn"] = dram("nffn", [2, D], inp=True)
    g["nfin"] = dram("nfin", [D], inp=True)
    g["ident"] = dram("ident", [128, 128], inp=True)
    g["tril"] = dram("tril", [CS, CS], inp=True)
    g["h"] = dram("h", [T, D])
    g["qT"] = dram("qT", [QK, T], BF16)
    g["kT"] = dram("kT", [QK, T], BF16)
    g["kh"] = dram("kh", [TP + T, QK], BF16)
    g["v"] = dram("v", [TP + T, VV], BF16)
    g["sr"] = dram("sr", [T, VV], BF16)
    g["ogT"] = dram("ogT", [VV, T], BF16)
    g["yT"] = dram("yT", [D, T], BF16)
    g["hidT"] = dram("hidT", [DFF, T], BF16)
    g["out"] = dram("out", [NOUT, D], out=True)
    k.g = g

    A_t = nc.alloc_sbuf_tensor("A", [128, KC * T], BF16)
    WR_t = nc.alloc_sbuf_tensor("WR", [128, 32768], BF16)
    k.A = A_t[:, :]
    k.WR = WR_t[:, :]
    k.A3 = k.A.rearrange("p (c t) -> p c t", t=T)
    k.identb = nc.alloc_sbuf_tensor("identb", [128, 128], BF16)[:, :]
    k.tril = nc.alloc_sbuf_tensor("trilm", [CS, CS], F32)[:, :]
    k.decay = nc.alloc_sbuf_tensor("decay", [128, 16 * NCT], F32)[:, :]
    k.nba = nc.alloc_sbuf_tensor("nba", [128, 16], F32)[:, :]
    k.cw = nc.alloc_sbuf_tensor("cw", [128, 3 * KC], F32)[:, :]
    k.ss = nc.alloc_sbuf_tensor("ss", [128, 20], F32)[:, :]
    k.sd = nc.alloc_sbuf_tensor("sd", [128, 20], F32)[:, :]
    k.rstd = nc.alloc_sbuf_tensor("rstd", [128, 20], F32)[:, :]
    k.epsb = nc.alloc_sbuf_tensor("epsb", [128, 1], F32)[:, :]
    k.oneb = nc.alloc_sbuf_tensor("oneb", [128, 1], F32)[:, :]
    xr_cols = (nc.sbuf_bytes_remaining - 64) // 2
    xr_cols = (xr_cols // 16) * 16
    XR_t = nc.alloc_sbuf_tensor("XR", [128, xr_cols], BF16)
    k.XR = XR_t[:, :]
    k.xr_cols = xr_cols

    k.banks = []
    psall = nc.alloc_psum_tensor("psall", [128, 4096], F32)
    k.psall = psall[:, :]
    for b in range(8):
        s = Slot(k.psall[:, b * 512:(b + 1) * 512])
        s.idx = b
        k.banks.append(s)
    k.bank_i = 0

    k.S_pe = P.sem("S_pe")
    k.S_dve = P.sem("S_dve")
    k.S_act = P.sem("S_act")
    k.S_pool = P.sem("S_pool")
    k.s_w = [P.sem(f"s_w{i}") for i in range(4)]
    k.s_ld = [P.sem(f"s_ld{i}") for i in range(6)]
    k.s_st = [P.sem(f"s_st{i}") for i in range(6)]
    k.s_a = [P.sem(f"s_a{i}") for i in range(8)]
    k.A_kc = None
    k.s_misc = P.sem("s_misc")
    k.s_gain = P.sem("s_gain")
    k.s_cp = P.sem("s_cp")
    P.auto = {"act": k.S_act, "dve": k.S_dve, "pool": k.S_pool}

    k.Aslot = Slot(k.A)
    k.Wslot = [Slot(), Slot()]
    k.dram_ready = {}

    from_stages(k, stages)

    with nc.Block() as block:
        @block.sync
        def _(e):
            P.run("sp", e)

        @block.scalar
        def _(e):
            P.run("act", e)

        @block.gpsimd
        def _(e):
            P.run("pool", e)

        @block.tensor
        def _(e):
            P.run("pe", e)

        @block.vector
        def _(e):
            P.run("dve", e)
    es.close()
    return nc


def k_tiles(k, ntok):
    if getattr(k, "skip_halo", False) and ntok == T:
        return [(CH + 128 * i, 128) for i in range((T - CH) // 128)]
    return tok_tiles(ntok)


def k_blocks(k, ntok):
    if getattr(k, "skip_halo", False) and ntok == T:
        return [(CH + 512 * i, 512) for i in range((T - CH) // 512)]
    return tok_blocks(ntok)


def look_free(k, n=4):
    i = k.bank_i
    if i % n != 0:
        return []
    deps = []
    for j in range(n):
        deps += k.banks[(i + j) % 8].free
    return deps


def next_bank(k):
    b = k.banks[k.bank_i % 8]
    k.bank_i += 1
    return b


def wr_f32(k, c0, n):
    return k.WR[:, c0:c0 + 2 * n].bitcast(F32)


def xr_f32(k, c0, n):
    return k.XR[:, c0:c0 + 2 * n].bitcast(F32)


def st_setup(k):
    P, g = k.P, k.g
    tmp = wr_f32(k, 0, 128)
    v1 = P.op("sp", "dma_start", out=tmp, in_=g["ident"], inc=k.s_misc)
    v2 = P.op("sp", "dma_start", out=k.tril, in_=g["tril"], inc=k.s_misc)
    v3 = P.op("sp", "dma_start", out=k.nba, in_=g["b_a_t"], inc=k.s_misc)
    v4 = P.op("sp", "dma_start", out=k.cw, in_=g["cw_t"], inc=k.s_misc)
    P.op("dve", "tensor_copy", out=k.identb, in_=tmp, deps=[v4])
    P.op("dve", "tensor_scalar", out=k.nba, in0=k.nba, scalar1=-1.0, scalar2=None, op0=ALU.mult, deps=[v3])
    P.op("dve", "memset", k.epsb, EPS)
    P.op("dve", "memset", k.oneb, 1.0)
    vv = P.op("dve", "memset", k.decay, 1.0, inc=k.S_dve)
    k.setup_done = [vv]
    k.Wslot[0].free = [vv]
    k.Wslot[1].free = [vv]


def st_norm_T(k, src, gain_row, ntok, src_deps=()):
    P = k.P
    xb = [Slot(wr_f32(k, 0, 4096)), Slot(wr_f32(k, 8192, 4096))]
    gbc = wr_f32(k, 16384, 4096)
    hnb = [Slot(k.WR[:, 24576:28672]), Slot(k.WR[:, 28672:32768])]
    wfree = k.Wslot[0].free + k.Wslot[1].free + list(k.setup_done)
    for s in xb + hnb:
        s.free = list(wfree)
    gv = P.op("sp", "dma_start", out=gbc, in_=gain_row.partition_broadcast(128), inc=k.s_gain,
              deps=wfree)
    tiles = k_tiles(k, ntok)
    a_ready = []
    st = {"last_pe": None, "ev_i": 0}

    def back_half(tt, t0, ts, dv):
        hb = hnb[tt % 2]
        pv = None
        for q in range(4):
            bank = next_bank(k)
            pb = bank.ap.bitcast(BF16)
            P.wait("pe", [dv] + bank.free)
            for kk in range(8):
                kc = q * 8 + kk
                pv = P.op("pe", "transpose", out=pb[:, kk * 128:kk * 128 + ts],
                          in_=hb.ap[:ts, kc * 128:(kc + 1) * 128], identity=k.identb[:ts, :ts],
                          inc=(k.S_pe if kk == 7 else None))
            eng = "act" if st["ev_i"] % 2 == 0 else "dve"
            st["ev_i"] += 1
            src_v = pb.rearrange("p (a b) -> p a b", b=128)[:, :, :ts]
            dst_v = k.A3[:, q * 8:(q + 1) * 8, t0:t0 + ts]
            if eng == "act":
                ev = P.op("act", "activation", out=dst_v, in_=src_v, func=AF.Copy,
                          deps=[pv] + k.Aslot.free)
            else:
                ev = P.op("dve", "tensor_copy", out=dst_v, in_=src_v, deps=[pv] + k.Aslot.free)
            bank.free = [ev]
            a_ready.append(ev)
        hb.free = [pv]
        st["last_pe"] = pv

    prev = None
    for tt, (t0, ts) in enumerate(tiles):
        b = tt % 2
        x, hb = xb[b], hnb[b]
        lv = P.op("sp", "dma_start", out=x.ap[:ts], in_=src[t0:t0 + ts, :], inc=k.s_ld[b],
                  deps=list(x.free) + list(src_deps))
        sq = P.op("act", "activation", out=hb.ap[:ts], in_=x.ap[:ts], func=AF.Square,
                  accum_out=k.ss[:ts, tt:tt + 1], deps=[lv] + hb.free)
        av = P.op("act", "activation", out=k.sd[:ts, tt:tt + 1], in_=k.ss[:ts, tt:tt + 1], func=AF.Sqrt,
                  scale=1.0 / D, bias=k.epsb[:ts], deps=[sq])
        rv = P.op("dve", "reciprocal", out=k.rstd[:ts, tt:tt + 1], in_=k.sd[:ts, tt:tt + 1], deps=[av, gv])
        dv = P.op("dve", "scalar_tensor_tensor", out=hb.ap[:ts], in0=x.ap[:ts],
                  scalar=k.rstd[:ts, tt:tt + 1], in1=gbc[:ts], op0=ALU.mult, op1=ALU.mult, deps=[rv])
        x.free = [dv]
        if prev is not None:
            back_half(*prev)
        prev = (tt, t0, ts, dv)
    back_half(*prev)
    last_pe = st["last_pe"]
    k.Aslot.ready = a_ready[-2:]
    k.A_kc = None
    k.Aslot.free = []
    k.Wslot[0].free = [last_pe, a_ready[-1], a_ready[-2]]
    k.Wslot[1].free = [last_pe, a_ready[-1], a_ready[-2]]


def a_tok_deps(k, t0, ts):
    if k.A_kc is None:
        return []
    return [v for (a, b, v) in k.A_kc if a < t0 + ts and b > t0]


def load_A(k, src, kcn, src_deps):
    P = k.P
    view = src.rearrange("(c p) t -> p c t", p=128)
    k.A_kc = []
    for i, (t0, ts) in enumerate(tok_blocks(T)):
        v = None
        for c0 in range(0, kcn, 16):
            c1 = min(kcn, c0 + 16)
            v = P.op("sp", "dma_start", out=k.A3[:, c0:c1, t0:t0 + ts], in_=view[:, c0:c1, t0:t0 + ts],
                     inc=k.s_a[i], deps=list(k.Aslot.free) + list(src_deps))
        k.A_kc.append((t0, t0 + ts, v))
    k.Aslot.ready = []
    k.Aslot.free = []


def gemm_A(k, kcn, wsrc, ncols_total, ntok, epilogue, col_tile=512):
    P = k.P
    wv = wsrc.rearrange("(c p) n -> p c n", p=128)
    tiles = k_tiles(k, ntok)
    last = None
    for nt in range(ncols_total // col_tile):
        c0 = nt * col_tile
        ws = k.Wslot[nt % 2]
        wb = k.WR[:, (nt % 2) * 16384:(nt % 2) * 16384 + kcn * col_tile].rearrange("p (c n) -> p c n", n=col_tile)
        step = 8
        for kc0 in range(0, kcn, step):
            kc1 = min(kcn, kc0 + step)
            wl = P.op("pool", "dma_start", out=wb[:, kc0:kc1, :], in_=wv[:, kc0:kc1, c0:c0 + col_tile],
                      inc=k.s_w[nt % 2], deps=ws.free)
        ws.ready = [wl]
        for tt, (t0, ts) in enumerate(tiles):
            la = look_free(k)
            bank = next_bank(k)
            P.wait("pe", k.Aslot.ready + ws.ready + la + bank.free + a_tok_deps(k, t0, ts))
            for kc in range(kcn):
                pv = P.op("pe", "matmul", bank.ap[:ts, :col_tile], k.A3[:, kc, t0:t0 + ts], wb[:, kc, :],
                          start=(kc == 0), stop=(kc == kcn - 1),
                          inc=(k.S_pe if kc == kcn - 1 else None))
            epilogue(nt, c0, tt, t0, ts, bank, pv)
            last = pv
        ws.free = [last]
    k.Aslot.free = [last]
    return last


def gemm_B(k, kcn, jobs, ntok, ep_block, ep_row=None, wbase=0, wslots=2, mrows=128, pre_block=None,
           ksplit=1, blocks=None):
    P = k.P
    if blocks is None:
        blocks = k_blocks(k, ntok)
    ng = len(jobs[0])
    kcp = kcn // ksplit
    assert kcp * ksplit == kcn
    last = None
    wsl = [Slot() for _ in range(wslots)]
    for s in wsl:
        s.free = k.Wslot[0].free + k.Wslot[1].free
    psz = ng * kcp * 128
    for j, job in enumerate(jobs):
        pieces = []
        for p in range(ksplit):
            pi = j * ksplit + p
            ws = wsl[pi % wslots]
            base = wbase + (pi % wslots) * psz
            wb = k.WR[:, base:base + psz].rearrange("p (g c n) -> p g c n", g=ng, n=128)
            wl = None
            for gi, (wv, c0) in enumerate(job):
                for kc0 in range(0, kcp, 8):
                    kc1 = min(kcp, kc0 + 8)
                    wl = P.op("pool", "dma_start", out=wb[:, gi, kc0:kc1, :mrows],
                              in_=wv[:, p * kcp + kc0:p * kcp + kc1, c0:c0 + mrows],
                              inc=k.s_w[pi % wslots], deps=ws.free)
            ws.ready = [wl]
            pieces.append((ws, wb))
        for bi, (t0, ts) in enumerate(blocks):
            xb = pre_block(j, bi, t0, ts) if pre_block is not None else []
            la = []
            banks = []
            for _ in range(ng):
                la += look_free(k)
                banks.append(next_bank(k))
            for b_ in banks:
                la += b_.free
            for p in range(ksplit):
                ws, wb = pieces[p]
                P.wait("pe", ws.ready)
                for gi in range(ng):
                    if p == 0:
                        P.wait("pe", k.Aslot.ready + (la if gi == 0 else []) + banks[gi].free + a_tok_deps(k, t0, ts))
                    for kk in range(kcp):
                        kc = p * kcp + kk
                        pv = P.op("pe", "matmul", banks[gi].ap[:mrows, :ts], wb[:, gi, kk, :mrows],
                                  k.A3[:, kc, t0:t0 + ts], start=(kc == 0), stop=(kc == kcn - 1),
                                  inc=(k.S_pe if (kc == kcn - 1 and gi == ng - 1) else None))
            ep_block(j, bi, t0, ts, xb + banks, pv)
            last = pv
        for ws, _ in pieces:
            ws.free = [last]
        if ep_row is not None:
            ep_row(j)
    k.Aslot.free = [last]
    k.Wslot[0].free = [last]
    k.Wslot[1].free = [last]
    return last


def st_resid_gemm(k, name, a_src, kcn, wsrc, h_src, h_dst, a_deps=None, from_A=False):
    P, g = k.P, k.g
    if not from_A:
        load_A(k, a_src, kcn, a_deps or [])
    nslot = 3
    hb = [Slot(k.XR[:, i * 1024:(i + 1) * 1024].bitcast(F32)) for i in range(nslot)]
    for s in hb:
        s.free = list(k.xr_free)
    st_vals = [None] * nslot
    cnt = [0]
    hdeps = list(k.dram_ready.get("h", []))

    def ep(nt, c0, tt, t0, ts, bank, pv):
        i = cnt[0] % nslot
        cnt[0] += 1
        s = hb[i]
        lv = P.op("sp", "dma_start", out=s.ap[:ts], in_=h_src[t0:t0 + ts, c0:c0 + 512], inc=k.s_ld[i],
                  deps=s.free + hdeps)
        dv = P.op("dve", "tensor_tensor", out=s.ap[:ts], in0=bank.ap[:ts, :], in1=s.ap[:ts], op=ALU.add,
                  inc=k.S_dve, deps=[pv, lv])
        bank.free = [dv]
        sv = P.op("sp", "dma_start", out=h_dst[t0:t0 + ts, c0:c0 + 512], in_=s.ap[:ts], inc=k.s_st[i],
                  deps=[dv])
        s.free = [sv]
        st_vals[i] = sv

    gemm_A(k, kcn, wsrc, D, T, ep)
    k.dram_ready["h"] = [v for v in st_vals if v is not None]
    k.xr_free = [v for v in st_vals if v is not None]


def st_ffn(k, l):
    P, g = k.P, k.g
    st_norm_T(k, g["h"], g["nffn"][l], T, src_deps=k.dram_ready.get("h", []))
    wgv = g[f"wg{l}"].rearrange("(c p) n -> p c n", p=128)
    wuv = g[f"wu{l}"].rearrange("(c p) n -> p c n", p=128)
    jobs = [[(wgv, j * 128), (wuv, j * 128)] for j in range(FC)]
    tmp = [Slot(wr_f32(k, 16384 + i * 1024, 512)) for i in range(2)]
    rows = [Slot(k.WR[:, 20480 + i * 2304:20480 + i * 2304 + T]) for i in range(2)]
    base_free = k.Wslot[0].free + k.Wslot[1].free
    for s in tmp + rows:
        s.free = list(base_free)
    cnt = [0]
    row_st = [None, None]
    hid_deps = list(k.dram_ready.get("hidT_free", []))

    def epb(j, bi, t0, ts, banks, pv):
        i = cnt[0] % 2
        cnt[0] += 1
        tm = tmp[i]
        row = rows[j % 2]
        av = P.op("act", "activation", out=tm.ap[:, :ts], in_=banks[0].ap[:, :ts], func=AF.Silu, inc=k.S_act,
                  deps=[pv] + tm.free)
        dv = P.op("dve", "tensor_tensor", out=row.ap[:, t0:t0 + ts], in0=banks[1].ap[:, :ts], in1=tm.ap[:, :ts],
                  op=ALU.mult, inc=k.S_dve, deps=[av] + row.free)
        tm.free = [dv]
        banks[0].free = [dv]
        banks[1].free = [dv]
        row.last = dv

    def epr(j):
        row = rows[j % 2]
        sv = P.op("sp", "dma_start", out=g["hidT"][j * 128:(j + 1) * 128, :], in_=row.ap, inc=k.s_st[j % 2],
                  deps=[row.last] + hid_deps)
        row.free = [sv]
        row_st[j % 2] = sv

    last = gemm_B(k, KC, jobs, T, epb, epr, wbase=0, wslots=2)
    k.dram_ready["hidT"] = [v for v in row_st if v is not None]
    k.Wslot[0].free = [last] + k.dram_ready["hidT"]
    k.Wslot[1].free = [last] + k.dram_ready["hidT"]
    parts = [(0, 29), (29, 29), (58, 28)]
    for (c0, cn) in parts:
        st_resid_gemm(k, f"down{l}", g["hidT"][c0 * 128:(c0 + cn) * 128, :], cn,
                      g[f"wd{l}"][c0 * 128:(c0 + cn) * 128, :], g["h"], g["h"],
                      a_deps=k.dram_ready["hidT"])
    k.dram_ready["hidT_free"] = [k.Aslot.free[0]]


def st_conv(k):
    P, g = k.P, k.g
    st_norm_T(k, g["h"], g["nmix"][1], T, src_deps=k.dram_ready.get("h", []))
    wv = g["cw_in"].rearrange("(c p) n -> p c n", p=128)
    jobs = [[(wv, j * 128), (wv, D + j * 128), (wv, 2 * D + j * 128)] for j in range(KC)]
    o = 18432
    z = k.WR[:, o:o + 2 * (T + 2)].bitcast(F32)
    o += 2 * (T + 2) + 12
    bgr = k.WR[:, o:o + 2 * T].bitcast(F32)
    o += 2 * T
    cr = k.WR[:, o:o + 2 * T].bitcast(F32)
    o += 2 * T
    assert o <= 32768, o
    tmp = [Slot(k.XR[:, T + i * 1024:T + (i + 1) * 1024].bitcast(F32)) for i in range(1)]
    assert T + 1024 <= k.xr_cols
    yrow = [Slot(k.XR[:, 0:T])]
    base_free = k.Wslot[0].free + k.Wslot[1].free
    for s in tmp:
        s.free = list(base_free)
    yrow[0].free = list(k.xr_free)
    zs = Slot(z)
    zs.free = list(base_free)
    cnt = [0]
    st = {"zlast": None, "sv": None, "alast": None}
    mz = P.op("dve", "memset", z[:, 0:2], 0.0, deps=base_free)

    def epb(j, bi, t0, ts, banks, pv):
        i = 0
        cnt[0] += 1
        tm = tmp[i]
        P.op("act", "activation", out=tm.ap[:, :ts], in_=banks[1].ap[:, :ts], func=AF.Copy,
             deps=[pv] + tm.free + zs.free)
        av = P.op("act", "activation", out=bgr[:, t0:t0 + ts], in_=banks[0].ap[:, :ts], func=AF.Copy)
        st["alast"] = av
        dv = P.op("dve", "tensor_tensor", out=z[:, 2 + t0:2 + t0 + ts], in0=banks[2].ap[:, :ts], in1=tm.ap[:, :ts],
                  op=ALU.mult, inc=k.S_dve, deps=[av] + zs.free)
        tm.free = [dv]
        for b in banks:
            b.free = [dv]
        st["zlast"] = dv

    def epr(j):
        y = yrow[0]
        c1 = P.op("dve", "tensor_scalar", out=cr, in0=z[:, 2:2 + T], scalar1=k.cw[:, 2 * KC + j:2 * KC + j + 1],
                  scalar2=None, op0=ALU.mult, deps=y.free + [st["zlast"]])
        c2 = P.op("dve", "scalar_tensor_tensor", out=cr, in0=z[:, 1:1 + T], scalar=k.cw[:, KC + j:KC + j + 1],
                  in1=cr, op0=ALU.mult, op1=ALU.add, deps=[c1])
        c3 = P.op("dve", "scalar_tensor_tensor", out=cr, in0=z[:, 0:T], scalar=k.cw[:, j:j + 1],
                  in1=cr, op0=ALU.mult, op1=ALU.add, deps=[c2])
        dv = P.op("dve", "tensor_tensor", out=y.ap, in0=cr, in1=bgr, op=ALU.mult, deps=[c3, st["alast"]])
        zs.free = [dv]
        sv = P.op("sp", "dma_start", out=g["yT"][j * 128:(j + 1) * 128, :], in_=y.ap, inc=k.s_st[2], deps=[dv])
        y.free = [sv]
        st["sv"] = sv

    last = gemm_B(k, KC, jobs, T, epb, epr, wbase=0, wslots=3, ksplit=2)
    k.dram_ready["yT"] = [st["sv"]]
    k.skip_halo = True
    k.xr_free = [st["sv"]]
    k.Wslot[0].free = [last, st["sv"]]
    k.Wslot[1].free = [last, st["sv"]]
    st_resid_gemm(k, "cw_out", g["yT"], KC, g["cw_out"], g["h"], g["h"], a_deps=k.dram_ready["yT"])


def st_final(k):
    P, g = k.P, k.g
    xb = [Slot(wr_f32(k, 0, 4096)), Slot(wr_f32(k, 8192, 4096))]
    gbc = wr_f32(k, 16384, 4096)
    junk = k.WR[:, 24576:28672]
    wfree = k.Wslot[0].free + k.Wslot[1].free
    for s in xb:
        s.free = list(wfree)
    gv = P.op("sp", "dma_start", out=gbc, in_=g["nfin"].partition_broadcast(128), inc=k.s_gain, deps=wfree)
    hdeps = list(k.dram_ready.get("h", []))
    svs = [None, None]
    for tt in range(NOUT // 128):
        t0 = CH + tt * 128
        b = tt % 2
        x = xb[b]
        lv = P.op("sp", "dma_start", out=x.ap, in_=g["h"][t0:t0 + 128, :], inc=k.s_ld[b], deps=x.free + hdeps)
        sq = P.op("act", "activation", out=junk, in_=x.ap, func=AF.Square, accum_out=k.ss[:, tt:tt + 1],
                  deps=[lv] + wfree)
        av = P.op("act", "activation", out=k.sd[:, tt:tt + 1], in_=k.ss[:, tt:tt + 1], func=AF.Sqrt,
                  scale=1.0 / D, bias=k.epsb, deps=[sq])
        rv = P.op("dve", "reciprocal", out=k.rstd[:, tt:tt + 1], in_=k.sd[:, tt:tt + 1], deps=[av, gv])
        dv = P.op("dve", "scalar_tensor_tensor", out=x.ap, in0=x.ap, scalar=k.rstd[:, tt:tt + 1], in1=gbc,
                  op0=ALU.mult, op1=ALU.mult, deps=[rv])
        sv = P.op("sp", "dma_start", out=g["out"][tt * 128:(tt + 1) * 128, :], in_=x.ap, inc=k.s_st[b], deps=[dv])
        x.free = [sv]
        svs[b] = sv
    P.wait("sp", [v for v in svs if v is not None])


def st_copy_h(k):
    P, g = k.P, k.g
    vals = []
    for i in range(0, T, 264):
        v = P.op("sp", "dma_start", out=g["h"][i:i + 264, :], in_=g["xo"][i:i + 264, :], inc=k.s_cp)
        vals.append(v)
    k.dram_ready["h"] = [vals[-1]]


def from_stages(k, stages):
    k.xr_free = []
    st_setup(k)
    k.xr_free = list(k.setup_done)
    for s in stages:
        if s == "copy_h":
            st_copy_h(k)
        elif s == "gla":
            import_gla(k)
        elif s == "ffn0":
            st_ffn(k, 0)
        elif s == "conv":
            st_conv(k)
        elif s == "ffn1":
            st_ffn(k, 1)
        elif s == "final":
            st_final(k)
        else:
            raise ValueError(s)


def import_gla(k):
    st_gla(k)


def st_gla_inproj(k, src, ntok, row0, ch0, own):
    P, g = k.P, k.g
    st_norm_T(k, src, g["nmix"][0], ntok)
    wv = g["w_in0"].rearrange("(c p) n -> p c n", p=128)
    base_free = k.Wslot[0].free + k.Wslot[1].free
    alowT = k.WR[:16, 16384:16384 + T]
    wa2b = k.WR[:16, 18496:18496 + QK]
    wa2f = k.WR[:16, 20544:20544 + 2 * QK].bitcast(F32)
    o = 24640
    tl = k.WR[:, o:o + 1024].bitcast(F32); o += 1024
    tB = k.WR[:, o:o + 1024].bitcast(F32); o += 1024
    teq = k.WR[:, o:o + 1024].bitcast(F32); o += 1024
    tek = k.WR[:, o:o + 1024].bitcast(F32); o += 1024
    tk = k.WR[:, o:o + 1024].bitcast(F32); o += 1024
    tkh2 = [k.WR[:, o:o + 512], k.XR[:, 1536:2048]]; o += 512
    qb = [Slot(k.WR[:, o + i * 512:o + (i + 1) * 512]) for i in range(2)]; o += 1024
    kb = [Slot(k.WR[:, o + i * 512:o + (i + 1) * 512]) for i in range(2)]; o += 1024
    assert o <= 32768, o
    khs = [Slot(k.XR[:, i * 512:(i + 1) * 512].rearrange("p (a d) -> p a d", d=128)) for i in range(2)]
    mask = k.XR[:, 1024:1536]
    for s_ in qb + kb:
        s_.free = list(base_free)
    for s_ in khs:
        s_.free = list(k.xr_free)
    decay3 = k.decay.rearrange("p (j c) -> p j c", c=NCT)
    if own:
        qk_blocks = [(0, CH)] + [(CH + 512 * i, 512) for i in range(4)]
    else:
        qk_blocks = tok_blocks(ntok)

    def chunk_of(t0):
        if not own:
            return CSP, t0 // CSP
        if t0 == 0:
            return CH, NPRE
        return CS, NPRE + 1 + (t0 - CH) // CS
    lv = P.op("sp", "dma_start", out=wa2f, in_=g["w_a2"], inc=k.s_misc, deps=base_free)
    wa_v = P.op("dve", "tensor_copy", out=wa2b, in_=wa2f, deps=[lv] + base_free)
    m1 = P.op("dve", "memset", mask, 1.0, deps=list(k.xr_free))
    m2 = P.op("dve", "memset", mask.rearrange("p (c i) -> p c i", i=(CS if own else CSP))[:, :, 0:1], 0.0,
              deps=[m1])
    st = {"tfree": list(base_free) + [m2], "tkh_free": [list(base_free), list(base_free) + list(k.xr_free)],
          "pending": [], "cnt": 0, "kst": [], "alow": None}

    def ep_alow(j, bi, t0, ts, banks, pv):
        av = P.op("act", "activation", out=alowT[:, t0:t0 + ts], in_=banks[0].ap[:16, :ts], func=AF.Copy,
                  deps=[pv] + base_free)
        banks[0].free = [av]
        st["alow"] = av

    gemm_B(k, KC, [[(wv, 2 * QK + 2 * VV)]], ntok, ep_alow, None, wbase=0, wslots=2, mrows=16)
    alow_ready = [st["alow"], wa_v]

    def pre_block(j, bi, t0, ts):
        bx = next_bank(k)
        P.wait("pe", alow_ready + bx.free)
        P.op("pe", "matmul", bx.ap[:, :ts], wa2b[:, j * 128:(j + 1) * 128], alowT[:, t0:t0 + ts],
             start=True, stop=True, inc=None)
        return [bx]

    def flush_pending(keep=0):
        while len(st["pending"]) > keep:
            st["pending"].pop(0)()

    def epb(j, bi, t0, ts, banks, pv):
        flush_pending(1)
        if own:
            bx, bq, bk = banks
        else:
            bx, bk = banks
            bq = None
        csz, cb = chunk_of(t0)
        nchb = ts // csz
        e1 = P.op("act", "activation", out=tl[:, :ts], in_=bx.ap[:, :ts], func=AF.Exp, scale=-1.0,
                  bias=k.nba[:, j:j + 1], deps=[pv] + st["tfree"])
        bx.free = [e1]
        l1 = P.op("act", "activation", out=tl[:, :ts], in_=tl[:, :ts], func=AF.Ln, scale=1.0, bias=k.oneb,
                  deps=[e1])
        sc = P.op("dve", "tensor_tensor_scan", out=tB[:, :ts], data0=mask[:, :ts], data1=tl[:, :ts], initial=0.0,
                  op0=ALU.mult, op1=ALU.add, deps=[l1] + st["tfree"])
        eqv = P.op("act", "activation", out=teq[:, :ts], in_=tB[:, :ts], func=AF.Exp, scale=-1.0 / 16, deps=[sc])
        ekv = P.op("act", "activation", out=tek[:, :ts], in_=tB[:, :ts], func=AF.Exp, scale=1.0 / 16)
        dcv = P.op("act", "activation", out=decay3[:, j, cb:cb + nchb],
                   in_=tB[:, :ts].rearrange("p (c i) -> p c i", i=csz)[:, :, csz - 1], func=AF.Exp, scale=-1.0 / 16)
        i = st["cnt"] % 2
        st["cnt"] += 1
        tkh = tkh2[i]
        lastd = None
        if own:
            q_ = qb[i]
            qv = P.op("dve", "scalar_tensor_tensor", out=q_.ap[:, :ts], in0=bq.ap[:, :ts], scalar=float(DK) ** -0.5,
                      in1=teq[:, :ts], op0=ALU.mult, op1=ALU.mult, deps=[eqv] + q_.free)
            bq.free = [qv]
            sv = P.op("sp", "dma_start", out=g["qT"][j * 128:(j + 1) * 128, t0:t0 + ts], in_=q_.ap[:, :ts],
                      inc=k.s_st[i], deps=[qv])
            q_.free = [sv]
            st["kst"].append(sv)
        tkv = P.op("dve", "tensor_tensor", out=tk[:, :ts], in0=bk.ap[:, :ts], in1=tek[:, :ts], op=ALU.mult,
                   deps=[ekv])
        bk.free = [tkv]
        if own:
            k_ = kb[i]
            kv = P.op("act", "activation", out=k_.ap[:, :ts], in_=tk[:, :ts], func=AF.Copy, deps=[tkv] + k_.free)
            sv = P.op("sp", "dma_start", out=g["kT"][j * 128:(j + 1) * 128, t0:t0 + ts], in_=k_.ap[:, :ts],
                      inc=k.s_st[2 + i], deps=[kv])
            k_.free = [sv]
            st["kst"].append(sv)
            lastd = kv
        khv = P.op("dve", "tensor_tensor", out=tkh[:, :ts].rearrange("p (c i) -> p c i", i=csz),
                   in0=tk[:, :ts].rearrange("p (c i) -> p c i", i=csz),
                   in1=decay3[:, j, cb:cb + nchb].unsqueeze(2).broadcast_to([128, nchb, csz]),
                   op=ALU.mult, deps=[tkv, dcv] + st["tkh_free"][i])
        st["tfree"] = [khv] + ([lastd] if lastd is not None else [])

        def pend(j=j, t0=t0, ts=ts, khv=khv, i=i, tkh=tkh):
            bank = next_bank(k)
            pb = bank.ap.bitcast(BF16)
            tls = tok_tiles(ts)
            P.wait("pe", [khv] + bank.free)
            for ti, (a0, asz) in enumerate(tls):
                pv2 = P.op("pe", "transpose", out=pb[:asz, ti * 128:(ti + 1) * 128], in_=tkh[:, a0:a0 + asz],
                           identity=k.identb, inc=(k.S_pe if ti == len(tls) - 1 else None))
            st["tkh_free"][i] = [pv2]
            hs = khs[i]
            nt_ = len(tls)
            asz = tls[0][1]
            ev = P.op("act", "activation", out=hs.ap[:asz, :nt_, :],
                      in_=pb[:asz, :nt_ * 128].rearrange("p (a d) -> p a d", d=128), func=AF.Copy,
                      deps=[pv2] + hs.free)
            bank.free = [ev]
            r0 = row0 + t0
            if asz == 128:
                dst = g["kh"][r0:r0 + ts, j * 128:(j + 1) * 128].rearrange("(a p) d -> p a d", p=128)
            else:
                dst = g["kh"][r0:r0 + ts, j * 128:(j + 1) * 128].rearrange("(a p) d -> p a d", p=asz)
            sv = P.op("sp", "dma_start", out=dst, in_=hs.ap[:asz, :nt_, :], inc=k.s_st[4 + i], deps=[ev])
            hs.free = [sv]
            st["kst"].append(sv)

        st["pending"].append(pend)

    if own:
        jobs = [[(wv, j * 128), (wv, QK + j * 128)] for j in range(16)]
    else:
        jobs = [[(wv, QK + j * 128)] for j in range(16)]
    k.Aslot.free = []
    last = gemm_B(k, KC, jobs, ntok, epb, None, wbase=0, wslots=2, pre_block=pre_block, blocks=qk_blocks)
    flush_pending(0)
    fin = [last] + st["tfree"] + st["tkh_free"][0] + st["tkh_free"][1]
    for s_ in qb + kb + khs:
        fin += s_.free
    k.Wslot[0].free = list(fin)
    k.Wslot[1].free = list(fin)
    k.xr_free = list(fin)
    k.dram_ready["qk"] = k.dram_ready.get("qk", []) + [v for v in fin if v[0].name.startswith("s_st")]

    vst = [Slot(k.XR[:, i * 512:(i + 1) * 512]) for i in range(3)]
    for s_ in vst:
        s_.free = list(k.xr_free)
    cnt = [0]
    vals = []

    def mk_ep(dst, func, r0):
        def ep(nt, c0, tt, t0, ts, bank, pv):
            i = cnt[0] % 3
            cnt[0] += 1
            s_ = vst[i]
            av = P.op("act", "activation", out=s_.ap[:ts], in_=bank.ap[:ts, :], func=func, deps=[pv] + s_.free)
            bank.free = [av]
            sv = P.op("sp", "dma_start", out=dst[r0 + t0:r0 + t0 + ts, c0:c0 + 512], in_=s_.ap[:ts],
                      inc=k.s_st[i], deps=[av])
            s_.free = [sv]
            vals.append(sv)
        return ep

    k.Aslot.free = []
    gemm_A(k, KC, g["w_in0"][:, 2 * QK:2 * QK + VV], VV, ntok, mk_ep(g["v"], AF.Copy, row0))
    if own:
        k.Aslot.free = []
        gemm_A(k, KC, g["w_in0"][:, 2 * QK + VV:2 * QK + 2 * VV], VV, ntok, mk_ep(g["sr"], AF.Silu, 0))
    fin2 = []
    for s_ in vst:
        fin2 += s_.free
    k.xr_free = fin2
    k.dram_ready["qk"] = k.dram_ready.get("qk", []) + fin2


def st_gla_scan(k):
    P, g = k.P, k.g
    base_free = k.Wslot[0].free + k.Wslot[1].free + k.Aslot.free + k.xr_free
    ddeps = list(k.dram_ready["qk"])
    decay3 = k.decay.rearrange("p (j c) -> p j c", c=NCT)
    Sf_flat = k.WR.bitcast(F32)
    Sf = Sf_flat.rearrange("p (h c v) -> p h c v", h=H, c=4)
    Sb_flat = k.A[:, 0:16384]
    Sb = Sb_flat.rearrange("p (h c v) -> p h c v", h=H, c=4)
    o = 16384
    slots = []
    for i in range(2):
        d = {}
        d["kh2"] = k.A[:, o:o + 2 * QK].rearrange("p (a d) -> p a d", a=2)
        d["v2"] = k.A[:, o + 2 * QK:o + 2 * QK + 2 * VV].rearrange("p (a d) -> p a d", a=2)
        d["qT"] = k.A[:, o:o + 16 * CS].rearrange("p (j t) -> p j t", t=CS); o += 16 * CS
        d["kT"] = k.A[:, o:o + 16 * CS].rearrange("p (j t) -> p j t", t=CS); o += 16 * CS
        d["kh"] = k.A[:, o:o + QK]; o += QK
        d["v"] = k.A[:, o:o + VV]; o += VV
        d["sr"] = k.A[:, o:o + VV]; o += VV
        d["free"] = list(base_free)
        slots.append(d)
    hnbc = k.A[:, o:o + 2 * DV].bitcast(F32); o += 2 * DV
    AT = [[k.A[:, o + (i * 4 + h) * CS:o + (i * 4 + h + 1) * CS] for h in range(H)] for i in range(2)]
    o += 8 * CS
    og = [k.A[:, o + i * VV:o + (i + 1) * VV] for i in range(2)]; o += 2 * VV
    ogT = [Slot(k.A[:, o + i * 4096:o + (i + 1) * 4096].rearrange("p (j t) -> p j t", t=CS)) for i in range(2)]
    o += 8192
    sqj = k.A[:, o:o + DV]; o += DV
    assert o <= KC * T, o
    for s_ in ogT:
        s_.free = list(base_free)
    z1 = P.op("dve", "memset", Sf_flat, 0.0, deps=base_free)
    z2 = P.op("pool", "memset", Sb_flat, 0.0, deps=base_free)
    hv = P.op("sp", "dma_start", out=hnbc, in_=g["hnorm"].partition_broadcast(CS), inc=k.s_gain, deps=base_free)
    sb_ready = [[z2] for _ in range(H)]
    sf_last = [[z1] for _ in range(H)]
    og_free = [list(base_free), list(base_free)]
    at_free = [list(base_free), list(base_free)]
    o_reads = [[] for _ in range(H)]
    sb_a = [None] * H
    sb_p = [None] * H
    st_vals = []
    n_own = 0
    chunks = [(CSP * c, CSP, False, 0) for c in range(NPRE)]
    chunks.append((TP, CH, True, 0))
    chunks += [(TP + CH + CS * i, CS, True, CH + CS * i) for i in range((T - CH) // CS)]
    assert len(chunks) == NCT
    n_pre = NPRE

    def emit_loads(c):
        r0, cs, own, tl0 = chunks[c]
        sl = slots[c % 2]
        sem_a = k.s_ld[(c % 2) * 2]
        sem_b = k.s_ld[(c % 2) * 2 + 1]
        if not own:
            P.op("sp", "dma_start", out=sl["kh2"], in_=g["kh"][r0:r0 + cs, :].rearrange("(a p) d -> p a d", p=128),
                 inc=sem_a, deps=sl["free"] + ddeps)
            ld_a = P.op("sp", "dma_start", out=sl["v2"],
                        in_=g["v"][r0:r0 + cs, :].rearrange("(a p) d -> p a d", p=128), inc=sem_a)
            return ld_a, None
        P.op("sp", "dma_start", out=sl["kh"][:cs], in_=g["kh"][r0:r0 + cs, :], inc=sem_a, deps=sl["free"] + ddeps)
        ld_a = P.op("sp", "dma_start", out=sl["v"][:cs], in_=g["v"][r0:r0 + cs, :], inc=sem_a)
        ld_b = None
        if own:
            P.op("sp", "dma_start", out=sl["qT"][:, :, :cs],
                 in_=g["qT"][:, tl0:tl0 + cs].rearrange("(j p) t -> p j t", p=128), inc=sem_b)
            P.op("sp", "dma_start", out=sl["kT"][:, :, :cs],
                 in_=g["kT"][:, tl0:tl0 + cs].rearrange("(j p) t -> p j t", p=128), inc=sem_b)
            ld_b = P.op("sp", "dma_start", out=sl["sr"][:cs], in_=g["sr"][tl0:tl0 + cs, :], inc=sem_b)
        return ld_a, ld_b

    nxt = emit_loads(0)
    for c, (r0, cs, own, tl0) in enumerate(chunks):
        sl = slots[c % 2]
        ld_a, ld_b = nxt
        if c + 1 < NCT:
            nxt = emit_loads(c + 1)
        reads = []
        if own:
            oi = n_own % 2
            gv = P.op("pool", "tensor_tensor", out=sl["sr"][:cs].rearrange("p (h v) -> p h v", h=H),
                      in0=sl["sr"][:cs].rearrange("p (h v) -> p h v", h=H),
                      in1=hnbc[:cs].unsqueeze(1).broadcast_to([cs, H, DV]), op=ALU.mult, deps=[ld_b, hv])
            sc_vals = []
            for h in range(H):
                bank = next_bank(k)
                P.wait("pe", [ld_b] + bank.free)
                for kc in range(4):
                    pv = P.op("pe", "matmul", bank.ap[:cs, :cs], sl["kT"][:, 4 * h + kc, :cs],
                              sl["qT"][:, 4 * h + kc, :cs], start=(kc == 0), stop=(kc == 3),
                              inc=(k.S_pe if kc == 3 else None))
                mv = P.op("dve", "tensor_tensor", out=AT[oi][h][:cs, :cs], in0=bank.ap[:cs, :cs],
                          in1=k.tril[:cs, :cs], op=ALU.mult, deps=[pv] + at_free[oi])
                bank.free = [mv]
                sc_vals.append(mv)
            tr_list = []
            for h in range(H):
                if k.bank_i % 2 == 1:
                    k.bank_i += 1
                b0 = next_bank(k)
                b1 = next_bank(k)
                pair = k.psall[:cs, b0.idx * 512:b0.idx * 512 + 1024]
                P.wait("pe", [ld_a, sc_vals[h]] + b0.free + b1.free + sb_ready[h])
                for vh, bnk in enumerate((b0, b1)):
                    for kc in range(4):
                        P.op("pe", "matmul", bnk.ap[:cs, :], sl["qT"][:, 4 * h + kc, :cs],
                             Sb[:, h, kc, vh * 512:(vh + 1) * 512], start=(kc == 0), stop=False, inc=None)
                    pv = P.op("pe", "matmul", bnk.ap[:cs, :], AT[oi][h][:cs, :cs],
                              sl["v"][:cs, h * DV + vh * 512:h * DV + (vh + 1) * 512], start=False, stop=True,
                              inc=(k.S_pe if vh == 1 else None))
                o_reads[h] = [pv]
                col = oi * 4 + h
                sq = P.op("act", "activation", out=sqj[:cs], in_=pair, func=AF.Square,
                          accum_out=k.ss[:cs, col:col + 1], deps=[pv])
                sdv = P.op("act", "activation", out=k.sd[:cs, col:col + 1], in_=k.ss[:cs, col:col + 1],
                           func=AF.Sqrt, scale=1.0 / DV, bias=k.epsb[:cs], deps=[sq])
                rv = P.op("dve", "reciprocal", out=k.rstd[:cs, col:col + 1], in_=k.sd[:cs, col:col + 1], deps=[sdv])
                ov = P.op("dve", "scalar_tensor_tensor", out=og[oi][:cs, h * DV:(h + 1) * DV], in0=pair,
                          scalar=k.rstd[:cs, col:col + 1], in1=sl["sr"][:cs, h * DV:(h + 1) * DV],
                          op0=ALU.mult, op1=ALU.mult, deps=[rv, gv] + og_free[oi])
                b0.free = [ov]
                b1.free = [ov]
                tr_list.append(ov)
                reads.append(ov)
        last_pe_read = None
        sfv = None
        for h in range(H):
            for kc in range(4):
                for vh in range(2):
                    bank = next_bank(k)
                    P.wait("pe", [ld_a] + bank.free)
                    if own:
                        pv = P.op("pe", "matmul", bank.ap[:, :],
                                  sl["kh"][:cs, (4 * h + kc) * 128:(4 * h + kc + 1) * 128],
                                  sl["v"][:cs, h * DV + vh * 512:h * DV + (vh + 1) * 512], start=True, stop=True,
                                  inc=k.S_pe)
                    else:
                        for a_ in range(2):
                            pv = P.op("pe", "matmul", bank.ap[:, :],
                                      sl["kh2"][:, a_, (4 * h + kc) * 128:(4 * h + kc + 1) * 128],
                                      sl["v2"][:, a_, h * DV + vh * 512:h * DV + (vh + 1) * 512],
                                      start=(a_ == 0), stop=(a_ == 1), inc=(k.S_pe if a_ == 1 else None))
                    sfv = P.op("dve", "scalar_tensor_tensor", out=Sf[:, h, kc, vh * 512:(vh + 1) * 512],
                               in0=Sf[:, h, kc, vh * 512:(vh + 1) * 512], scalar=decay3[:, 4 * h + kc, c:c + 1],
                               in1=bank.ap[:, :], op0=ALU.mult, op1=ALU.add, deps=[pv] + sf_last[h])
                    bank.free = [sfv]
                    if c >= n_pre - 1 and c < NCT - 1:
                        cv = P.op("act", "activation", out=Sb[:, h, kc, vh * 512:(vh + 1) * 512],
                                  in_=Sf[:, h, kc, vh * 512:(vh + 1) * 512], func=AF.Copy,
                                  deps=[sfv] + o_reads[h])
                        sb_ready[h] = [cv]
                    last_pe_read = pv
            sf_last[h] = [sfv]
        reads.append(last_pe_read)
        if own:
            oslot = ogT[n_own % 2]
            ev = None
            pv = None
            for h in range(H):
                bank = next_bank(k)
                pb = bank.ap.bitcast(BF16)
                P.wait("pe", [tr_list[h]] + bank.free)
                for kk in range(8):
                    pv = P.op("pe", "transpose", out=pb[:, kk * cs:(kk + 1) * cs],
                              in_=og[oi][:cs, h * DV + kk * 128:h * DV + (kk + 1) * 128], identity=k.identb[:cs, :cs],
                              inc=(k.S_pe if kk == 7 else None))
                ev = P.op("act", "activation", out=oslot.ap[:, h * 8:(h + 1) * 8, :cs],
                          in_=pb[:, :8 * cs].rearrange("p (a t) -> p a t", t=cs), func=AF.Copy,
                          deps=[pv] + oslot.free)
                bank.free = [ev]
            og_free[oi] = [pv]
            at_free[oi] = list(o_reads[H - 1])
            sv = P.op("sp", "dma_start",
                      out=g["ogT"][:, tl0:tl0 + cs].rearrange("(j p) t -> p j t", p=128),
                      in_=oslot.ap[:, :, :cs], inc=k.s_st[n_own % 2], deps=[ev])
            oslot.free = [sv]
            st_vals.append(sv)
            n_own += 1
        sl["free"] = reads
    k.dram_ready["ogT"] = st_vals[-2:]
    fin = st_vals[-2:] + [last_pe_read] + sf_last[H - 1]
    k.Wslot[0].free = list(fin)
    k.Wslot[1].free = list(fin)
    k.Aslot.free = list(fin)
    k.xr_free = list(fin)


def st_gla(k):
    g = k.g
    st_gla_inproj(k, g["xp"], TP, 0, 0, own=False)
    st_gla_inproj(k, g["xo"], T, TP, NCHP, own=True)
    st_gla_scan(k)
    st_resid_gemm(k, "w_out0", g["ogT"], KC, g["w_out0"], g["xo"], g["h"], a_deps=k.dram_ready["ogT"])


ALL_STAGES = ["gla", "ffn0", "conv", "ffn1", "final"]


def shared_inputs(inputs):
    f = np.float32
    sh = {}
    sh["w_in0"] = np.ascontiguousarray(inputs["gla_w_in"][0], dtype=f)
    sh["w_a2"] = np.ascontiguousarray(inputs["gla_w_a2"][0], dtype=f)
    sh["b_a_t"] = np.ascontiguousarray(np.asarray(inputs["gla_b_a"][0], dtype=f).reshape(16, 128).T)
    sh["hnorm"] = np.ascontiguousarray(inputs["gla_head_norm"][0], dtype=f)
    sh["w_out0"] = np.ascontiguousarray(inputs["gla_w_out"][0], dtype=f)
    sh["cw_in"] = np.ascontiguousarray(inputs["conv_w_in"][0], dtype=f)
    sh["cw_t"] = np.ascontiguousarray(
        np.asarray(inputs["conv_w"][0], dtype=f).reshape(3, KC, 128).transpose(2, 0, 1).reshape(128, 3 * KC))
    sh["cw_out"] = np.ascontiguousarray(inputs["conv_w_out"][0], dtype=f)
    for l in range(2):
        sh[f"wg{l}"] = np.ascontiguousarray(inputs["ffn_w_gate"][l], dtype=f)
        sh[f"wu{l}"] = np.ascontiguousarray(inputs["ffn_w_up"][l], dtype=f)
        sh[f"wd{l}"] = np.ascontiguousarray(inputs["ffn_w_down"][l], dtype=f)
    sh["nmix"] = np.ascontiguousarray(inputs["norm_mix"], dtype=f)
    sh["nffn"] = np.ascontiguousarray(inputs["norm_ffn"], dtype=f)
    sh["nfin"] = np.ascontiguousarray(inputs["norm_final"], dtype=f)
    sh["ident"] = np.eye(128, dtype=f)
    sh["tril"] = np.triu(np.ones((CS, CS), dtype=f))
    return sh


def core_tokens(x_b, meta):
    f = np.float32
    seq = np.concatenate([np.zeros((CH - meta.shape[0], D), f), np.asarray(meta, f), np.asarray(x_b, f)], axis=0)
    a = {"xo": np.ascontiguousarray(seq[0:T]), "xp": np.zeros((TP, D), f)}
    b = {"xo": np.ascontiguousarray(seq[TP:TP + T]), "xp": np.ascontiguousarray(seq[0:TP])}
    return a, b


_NC_CACHE = {}


def kernel(**inputs):
    x = np.asarray(inputs["x"])
    B = x.shape[0]
    sh = shared_inputs(inputs)
    in_maps = []
    for b in range(B):
        a, c = core_tokens(x[b], inputs["meta"])
        for t in (a, c):
            m = dict(sh)
            m.update(t)
            in_maps.append(m)
    if "nc" not in _NC_CACHE:
        _NC_CACHE["nc"] = build(ALL_STAGES)
    nc = _NC_CACHE["nc"]
    res = run_bass_kernel_spmd(nc, in_maps, core_ids=list(range(2 * B)))
    out = np.empty((B, 2 * NOUT, D), np.float32)
    for b in range(B):
        for hf in range(2):
            out[b, hf * NOUT:(hf + 1) * NOUT] = res.results[2 * b + hf]["out"]
    return out
```

```python
import numpy as np
from contextlib import ExitStack

import concourse.bass as bass
import concourse.mybir as mybir
from concourse.bass_utils import run_bass_kernel_spmd

F32 = mybir.dt.float32
BF16 = mybir.dt.bfloat16
AF = mybir.ActivationFunctionType
ALU = mybir.AluOpType

D = 4096
KC = D // 128
T = 2112
TP = 2048
CH = 64
NCH = T // CH
NCHP = TP // CH
CS = 128
CSP = 256
NPRE = TP // CSP
NCT = NPRE + 1 + (T - CH) // CS
DFF = 11008
FC = DFF // 128
H = 4
DK = 512
DV = 1024
QK = H * DK
VV = H * DV
GIN = 2 * QK + 2 * VV + 16
EPS = 1e-6
NOUT = 2048


def tok_tiles(n):
    r = []
    t = 0
    while t < n:
        s = min(128, n - t)
        r.append((t, s))
        t += s
    return r


def tok_blocks(n):
    r = []
    t = 0
    while t < n:
        s = min(512, n - t)
        r.append((t, s))
        t += s
    return r


class Sem:
    def __init__(self, h, name):
        self.h = h
        self.n = 0
        self.name = name


class Slot:
    def __init__(self, ap=None):
        self.ap = ap
        self.ready = []
        self.free = []


class Prog:
    ENG = ("sp", "act", "pool", "pe", "dve")

    def __init__(self, nc, es):
        self.nc = nc
        self.es = es
        self.q = {e: [] for e in self.ENG}
        self.waited = {e: {} for e in self.ENG}
        self.nsem = 0
        self.cnt = {e: 0 for e in self.ENG}
        self.auto = {}

    def sem(self, name):
        h = self.es.enter_context(self.nc.semaphore(name))
        self.nsem += 1
        return Sem(h, name)

    def wait(self, e, deps):
        best = {}
        for (s, v) in deps:
            if v > best.get(s.name, (None, 0))[1]:
                best[s.name] = (s, v)
        w = self.waited[e]
        for name, (s, v) in best.items():
            if w.get(name, 0) >= v:
                continue
            w[name] = v
            self.q[e].append(("w", s.h, v))

    def op(self, e, method, *args, inc="auto", deps=None, **kw):
        if deps:
            self.wait(e, deps)
        val = None
        amt = 0
        if inc == "auto":
            inc = self.auto.get(e) if method != "dma_start" else None
        if inc is not None:
            amt = 16 if method == "dma_start" else 1
            inc.n += amt
            val = (inc, inc.n)
        self.q[e].append(("o", method, args, kw, inc.h if inc is not None else None, amt))
        self.cnt[e] += 1
        return val

    def run(self, e, eng):
        for it in self.q[e]:
            if it[0] == "w":
                eng.wait_ge(it[1], it[2])
            else:
                _, method, args, kw, sh, amt = it
                ins = getattr(eng, method)(*args, **kw)
                if sh is not None:
                    ins.then_inc(sh, amt)


class K:
    pass


def build(stages, ext_in=(), ext_out=(), debug_T=None):
    nc = bass.Bass("TRN2", target_bir_lowering=False)
    es = ExitStack()
    P = Prog(nc, es)
    k = K()
    k.nc, k.P = nc, P

    def dram(name, shape, dt=F32, inp=False, out=False):
        if inp or name in ext_in:
            kind = "ExternalInput"
        elif out or name in ext_out:
            kind = "ExternalOutput"
        else:
            kind = "Internal"
        return nc.dram_tensor(name, list(shape), dt, kind=kind).ap()

    g = {}
    g["xo"] = dram("xo", [T, D], inp=True)
    g["xp"] = dram("xp", [TP, D], inp=True)
    g["w_in0"] = dram("w_in0", [D, GIN], inp=True)
    g["w_a2"] = dram("w_a2", [16, QK], inp=True)
    g["b_a_t"] = dram("b_a_t", [128, 16], inp=True)
    g["hnorm"] = dram("hnorm", [DV], inp=True)
    g["w_out0"] = dram("w_out0", [VV, D], inp=True)
    g["cw_in"] = dram("cw_in", [D, 3 * D], inp=True)
    g["cw_t"] = dram("cw_t", [128, 3 * KC], inp=True)
    g["cw_out"] = dram("cw_out", [D, D], inp=True)
    for l in range(2):
        g[f"wg{l}"] = dram(f"wg{l}", [D, DFF], inp=True)
        g[f"wu{l}"] = dram(f"wu{l}", [D, DFF], inp=True)
        g[f"wd{l}"] = dram(f"wd{l}", [DFF, D], inp=True)
    g["nmix"] = dram("nmix", [2, D], inp=True)
    g["nffn"] = dram("nffn", [2, D], inp=True)
    g["nfin"] = dram("nfin", [D], inp=True)
    g["ident"] = dram("ident", [128, 128], inp=True)
    g["tril"] = dram("tril", [CS, CS], inp=True)
    g["h"] = dram("h", [T, D])
    g["qT"] = dram("qT", [QK, T], BF16)
    g["kT"] = dram("kT", [QK, T], BF16)
    g["kh"] = dram("kh", [TP + T, QK], BF16)
    g["v"] = dram("v", [TP + T, VV], BF16)
    g["sr"] = dram("sr", [T, VV], BF16)
    g["ogT"] = dram("ogT", [VV, T], BF16)
    g["yT"] = dram("yT", [D, T], BF16)
    g["hidT"] = dram("hidT", [DFF, T], BF16)
    g["out"] = dram("out", [NOUT, D], out=True)
    k.g = g

    A_t = nc.alloc_sbuf_tensor("A", [128, KC * T], BF16)
    WR_t = nc.alloc_sbuf_tensor("WR", [128, 32768], BF16)
    k.A = A_t[:, :]
    k.WR = WR_t[:, :]
    k.A3 = k.A.rearrange("p (c t) -> p c t", t=T)
    k.identb = nc.alloc_sbuf_tensor("identb", [128, 128], BF16)[:, :]
    k.tril = nc.alloc_sbuf_tensor("trilm", [CS, CS], F32)[:, :]
    k.decay = nc.alloc_sbuf_tensor("decay", [128, 16 * NCT], F32)[:, :]
    k.nba = nc.alloc_sbuf_tensor("nba", [128, 16], F32)[:, :]
    k.cw = nc.alloc_sbuf_tensor("cw", [128, 3 * KC], F32)[:, :]
    k.ss = nc.alloc_sbuf_tensor("ss", [128, 20], F32)[:, :]
    k.sd = nc.alloc_sbuf_tensor("sd", [128, 20], F32)[:, :]
    k.rstd = nc.alloc_sbuf_tensor("rstd", [128, 20], F32)[:, :]
    k.epsb = nc.alloc_sbuf_tensor("epsb", [128, 1], F32)[:, :]
    k.oneb = nc.alloc_sbuf_tensor("oneb", [128, 1], F32)[:, :]
    xr_cols = (nc.sbuf_bytes_remaining - 64) // 2
    xr_cols = (xr_cols // 16) * 16
    XR_t = nc.alloc_sbuf_tensor("XR", [128, xr_cols], BF16)
    k.XR = XR_t[:, :]
    k.xr_cols = xr_cols

    k.banks = []
    psall = nc.alloc_psum_tensor("psall", [128, 4096], F32)
    k.psall = psall[:, :]
    for b in range(8):
        s = Slot(k.psall[:, b * 512:(b + 1) * 512])
        s.idx = b
        k.banks.append(s)
    k.bank_i = 0

    k.S_pe = P.sem("S_pe")
    k.S_dve = P.sem("S_dve")
    k.S_act = P.sem("S_act")
    k.S_pool = P.sem("S_pool")
    k.s_w = [P.sem(f"s_w{i}") for i in range(4)]
    k.s_ld = [P.sem(f"s_ld{i}") for i in range(6)]
    k.s_st = [P.sem(f"s_st{i}") for i in range(6)]
    k.s_a = [P.sem(f"s_a{i}") for i in range(8)]
    k.A_kc = None
    k.s_misc = P.sem("s_misc")
    k.s_gain = P.sem("s_gain")
    k.s_cp = P.sem("s_cp")
    P.auto = {"act": k.S_act, "dve": k.S_dve, "pool": k.S_pool}

    k.Aslot = Slot(k.A)
    k.Wslot = [Slot(), Slot()]
    k.dram_ready = {}

    from_stages(k, stages)

    with nc.Block() as block:
        @block.sync
        def _(e):
            P.run("sp", e)

        @block.scalar
        def _(e):
            P.run("act", e)

        @block.gpsimd
        def _(e):
            P.run("pool", e)

        @block.tensor
        def _(e):
            P.run("pe", e)

        @block.vector
        def _(e):
            P.run("dve", e)
    es.close()
    return nc


def k_tiles(k, ntok):
    if getattr(k, "skip_halo", False) and ntok == T:
        return [(CH + 128 * i, 128) for i in range((T - CH) // 128)]
    return tok_tiles(ntok)


def k_blocks(k, ntok):
    if getattr(k, "skip_halo", False) and ntok == T:
        return [(CH + 512 * i, 512) for i in range((T - CH) // 512)]
    return tok_blocks(ntok)


def look_free(k, n=4):
    i = k.bank_i
    if i % n != 0:
        return []
    deps = []
    for j in range(n):
        deps += k.banks[(i + j) % 8].free
    return deps


def next_bank(k):
    b = k.banks[k.bank_i % 8]
    k.bank_i += 1
    return b


def wr_f32(k, c0, n):
    return k.WR[:, c0:c0 + 2 * n].bitcast(F32)


def xr_f32(k, c0, n):
    return k.XR[:, c0:c0 + 2 * n].bitcast(F32)


def st_setup(k):
    P, g = k.P, k.g
    tmp = wr_f32(k, 0, 128)
    v1 = P.op("sp", "dma_start", out=tmp, in_=g["ident"], inc=k.s_misc)
    v2 = P.op("sp", "dma_start", out=k.tril, in_=g["tril"], inc=k.s_misc)
    v3 = P.op("sp", "dma_start", out=k.nba, in_=g["b_a_t"], inc=k.s_misc)
    v4 = P.op("sp", "dma_start", out=k.cw, in_=g["cw_t"], inc=k.s_misc)
    P.op("dve", "tensor_copy", out=k.identb, in_=tmp, deps=[v4])
    P.op("dve", "tensor_scalar", out=k.nba, in0=k.nba, scalar1=-1.0, scalar2=None, op0=ALU.mult, deps=[v3])
    P.op("dve", "memset", k.epsb, EPS)
    P.op("dve", "memset", k.oneb, 1.0)
    vv = P.op("dve", "memset", k.decay, 1.0, inc=k.S_dve)
    k.setup_done = [vv]
    k.Wslot[0].free = [vv]
    k.Wslot[1].free = [vv]


def st_norm_T(k, src, gain_row, ntok, src_deps=()):
    P = k.P
    xb = [Slot(wr_f32(k, 0, 4096)), Slot(wr_f32(k, 8192, 4096))]
    gbc = wr_f32(k, 16384, 4096)
    hnb = [Slot(k.WR[:, 24576:28672]), Slot(k.WR[:, 28672:32768])]
    wfree = k.Wslot[0].free + k.Wslot[1].free + list(k.setup_done)
    for s in xb + hnb:
        s.free = list(wfree)
    gv = P.op("sp", "dma_start", out=gbc, in_=gain_row.partition_broadcast(128), inc=k.s_gain,
              deps=wfree)
    tiles = k_tiles(k, ntok)
    a_ready = []
    st = {"last_pe": None, "ev_i": 0}

    def back_half(tt, t0, ts, dv):
        hb = hnb[tt % 2]
        pv = None
        for q in range(4):
            bank = next_bank(k)
            pb = bank.ap.bitcast(BF16)
            P.wait("pe", [dv] + bank.free)
            for kk in range(8):
                kc = q * 8 + kk
                pv = P.op("pe", "transpose", out=pb[:, kk * 128:kk * 128 + ts],
                          in_=hb.ap[:ts, kc * 128:(kc + 1) * 128], identity=k.identb[:ts, :ts],
                          inc=(k.S_pe if kk == 7 else None))
            eng = "act" if st["ev_i"] % 2 == 0 else "dve"
            st["ev_i"] += 1
            src_v = pb.rearrange("p (a b) -> p a b", b=128)[:, :, :ts]
            dst_v = k.A3[:, q * 8:(q + 1) * 8, t0:t0 + ts]
            if eng == "act":
                ev = P.op("act", "activation", out=dst_v, in_=src_v, func=AF.Copy,
                          deps=[pv] + k.Aslot.free)
            else:
                ev = P.op("dve", "tensor_copy", out=dst_v, in_=src_v, deps=[pv] + k.Aslot.free)
            bank.free = [ev]
            a_ready.append(ev)
        hb.free = [pv]
        st["last_pe"] = pv

    prev = None
    for tt, (t0, ts) in enumerate(tiles):
        b = tt % 2
        x, hb = xb[b], hnb[b]
        lv = P.op("sp", "dma_start", out=x.ap[:ts], in_=src[t0:t0 + ts, :], inc=k.s_ld[b],
                  deps=list(x.free) + list(src_deps))
        sq = P.op("act", "activation", out=hb.ap[:ts], in_=x.ap[:ts], func=AF.Square,
                  accum_out=k.ss[:ts, tt:tt + 1], deps=[lv] + hb.free)
        av = P.op("act", "activation", out=k.sd[:ts, tt:tt + 1], in_=k.ss[:ts, tt:tt + 1], func=AF.Sqrt,
                  scale=1.0 / D, bias=k.epsb[:ts], deps=[sq])
        rv = P.op("dve", "reciprocal", out=k.rstd[:ts, tt:tt + 1], in_=k.sd[:ts, tt:tt + 1], deps=[av, gv])
        dv = P.op("dve", "scalar_tensor_tensor", out=hb.ap[:ts], in0=x.ap[:ts],
                  scalar=k.rstd[:ts, tt:tt + 1], in1=gbc[:ts], op0=ALU.mult, op1=ALU.mult, deps=[rv])
        x.free = [dv]
        if prev is not None:
            back_half(*prev)
        prev = (tt, t0, ts, dv)
    back_half(*prev)
    last_pe = st["last_pe"]
    k.Aslot.ready = a_ready[-2:]
    k.A_kc = None
    k.Aslot.free = []
    k.Wslot[0].free = [last_pe, a_ready[-1], a_ready[-2]]
    k.Wslot[1].free = [last_pe, a_ready[-1], a_ready[-2]]


def a_tok_deps(k, t0, ts):
    if k.A_kc is None:
        return []
    return [v for (a, b, v) in k.A_kc if a < t0 + ts and b > t0]


def load_A(k, src, kcn, src_deps):
    P = k.P
    view = src.rearrange("(c p) t -> p c t", p=128)
    k.A_kc = []
    for i, (t0, ts) in enumerate(tok_blocks(T)):
        v = None
        for c0 in range(0, kcn, 16):
            c1 = min(kcn, c0 + 16)
            v = P.op("sp", "dma_start", out=k.A3[:, c0:c1, t0:t0 + ts], in_=view[:, c0:c1, t0:t0 + ts],
                     inc=k.s_a[i], deps=list(k.Aslot.free) + list(src_deps))
        k.A_kc.append((t0, t0 + ts, v))
    k.Aslot.ready = []
    k.Aslot.free = []


def gemm_A(k, kcn, wsrc, ncols_total, ntok, epilogue, col_tile=512):
    P = k.P
    wv = wsrc.rearrange("(c p) n -> p c n", p=128)
    tiles = k_tiles(k, ntok)
    last = None
    for nt in range(ncols_total // col_tile):
        c0 = nt * col_tile
        ws = k.Wslot[nt % 2]
        wb = k.WR[:, (nt % 2) * 16384:(nt % 2) * 16384 + kcn * col_tile].rearrange("p (c n) -> p c n", n=col_tile)
        step = 8
        for kc0 in range(0, kcn, step):
            kc1 = min(kcn, kc0 + step)
            wl = P.op("pool", "dma_start", out=wb[:, kc0:kc1, :], in_=wv[:, kc0:kc1, c0:c0 + col_tile],
                      inc=k.s_w[nt % 2], deps=ws.free)
        ws.ready = [wl]
        for tt, (t0, ts) in enumerate(tiles):
            la = look_free(k)
            bank = next_bank(k)
            P.wait("pe", k.Aslot.ready + ws.ready + la + bank.free + a_tok_deps(k, t0, ts))
            for kc in range(kcn):
                pv = P.op("pe", "matmul", bank.ap[:ts, :col_tile], k.A3[:, kc, t0:t0 + ts], wb[:, kc, :],
                          start=(kc == 0), stop=(kc == kcn - 1),
                          inc=(k.S_pe if kc == kcn - 1 else None))
            epilogue(nt, c0, tt, t0, ts, bank, pv)
            last = pv
        ws.free = [last]
    k.Aslot.free = [last]
    return last


def gemm_B(k, kcn, jobs, ntok, ep_block, ep_row=None, wbase=0, wslots=2, mrows=128, pre_block=None,
           ksplit=1, blocks=None):
    P = k.P
    if blocks is None:
        blocks = k_blocks(k, ntok)
    ng = len(jobs[0])
    kcp = kcn // ksplit
    assert kcp * ksplit == kcn
    last = None
    wsl = [Slot() for _ in range(wslots)]
    for s in wsl:
        s.free = k.Wslot[0].free + k.Wslot[1].free
    psz = ng * kcp * 128
    for j, job in enumerate(jobs):
        pieces = []
        for p in range(ksplit):
            pi = j * ksplit + p
            ws = wsl[pi % wslots]
            base = wbase + (pi % wslots) * psz
            wb = k.WR[:, base:base + psz].rearrange("p (g c n) -> p g c n", g=ng, n=128)
            wl = None
            for gi, (wv, c0) in enumerate(job):
                for kc0 in range(0, kcp, 8):
                    kc1 = min(kcp, kc0 + 8)
                    wl = P.op("pool", "dma_start", out=wb[:, gi, kc0:kc1, :mrows],
                              in_=wv[:, p * kcp + kc0:p * kcp + kc1, c0:c0 + mrows],
                              inc=k.s_w[pi % wslots], deps=ws.free)
            ws.ready = [wl]
            pieces.append((ws, wb))
        for bi, (t0, ts) in enumerate(blocks):
            xb = pre_block(j, bi, t0, ts) if pre_block is not None else []
            la = []
            banks = []
            for _ in range(ng):
                la += look_free(k)
                banks.append(next_bank(k))
            for b_ in banks:
                la += b_.free
            for p in range(ksplit):
                ws, wb = pieces[p]
                P.wait("pe", ws.ready)
                for gi in range(ng):
                    if p == 0:
                        P.wait("pe", k.Aslot.ready + (la if gi == 0 else []) + banks[gi].free + a_tok_deps(k, t0, ts))
                    for kk in range(kcp):
                        kc = p * kcp + kk
                        pv = P.op("pe", "matmul", banks[gi].ap[:mrows, :ts], wb[:, gi, kk, :mrows],
                                  k.A3[:, kc, t0:t0 + ts], start=(kc == 0), stop=(kc == kcn - 1),
                                  inc=(k.S_pe if (kc == kcn - 1 and gi == ng - 1) else None))
            ep_block(j, bi, t0, ts, xb + banks, pv)
            last = pv
        for ws, _ in pieces:
            ws.free = [last]
        if ep_row is not None:
            ep_row(j)
    k.Aslot.free = [last]
    k.Wslot[0].free = [last]
    k.Wslot[1].free = [last]
    return last


def st_resid_gemm(k, name, a_src, kcn, wsrc, h_src, h_dst, a_deps=None, from_A=False):
    P, g = k.P, k.g
    if not from_A:
        load_A(k, a_src, kcn, a_deps or [])
    nslot = 3
    hb = [Slot(k.XR[:, i * 1024:(i + 1) * 1024].bitcast(F32)) for i in range(nslot)]
    for s in hb:
        s.free = list(k.xr_free)
    st_vals = [None] * nslot
    cnt = [0]
    hdeps = list(k.dram_ready.get("h", []))

    def ep(nt, c0, tt, t0, ts, bank, pv):
        i = cnt[0] % nslot
        cnt[0] += 1
        s = hb[i]
        lv = P.op("sp", "dma_start", out=s.ap[:ts], in_=h_src[t0:t0 + ts, c0:c0 + 512], inc=k.s_ld[i],
                  deps=s.free + hdeps)
        dv = P.op("dve", "tensor_tensor", out=s.ap[:ts], in0=bank.ap[:ts, :], in1=s.ap[:ts], op=ALU.add,
                  inc=k.S_dve, deps=[pv, lv])
        bank.free = [dv]
        sv = P.op("sp", "dma_start", out=h_dst[t0:t0 + ts, c0:c0 + 512], in_=s.ap[:ts], inc=k.s_st[i],
                  deps=[dv])
        s.free = [sv]
        st_vals[i] = sv

    gemm_A(k, kcn, wsrc, D, T, ep)
    k.dram_ready["h"] = [v for v in st_vals if v is not None]
    k.xr_free = [v for v in st_vals if v is not None]


def st_ffn(k, l):
    P, g = k.P, k.g
    st_norm_T(k, g["h"], g["nffn"][l], T, src_deps=k.dram_ready.get("h", []))
    wgv = g[f"wg{l}"].rearrange("(c p) n -> p c n", p=128)
    wuv = g[f"wu{l}"].rearrange("(c p) n -> p c n", p=128)
    jobs = [[(wgv, j * 128), (wuv, j * 128)] for j in range(FC)]
    tmp = [Slot(wr_f32(k, 16384 + i * 1024, 512)) for i in range(2)]
    rows = [Slot(k.WR[:, 20480 + i * 2304:20480 + i * 2304 + T]) for i in range(2)]
    base_free = k.Wslot[0].free + k.Wslot[1].free
    for s in tmp + rows:
        s.free = list(base_free)
    cnt = [0]
    row_st = [None, None]
    hid_deps = list(k.dram_ready.get("hidT_free", []))

    def epb(j, bi, t0, ts, banks, pv):
        i = cnt[0] % 2
        cnt[0] += 1
        tm = tmp[i]
        row = rows[j % 2]
        av = P.op("act", "activation", out=tm.ap[:, :ts], in_=banks[0].ap[:, :ts], func=AF.Silu, inc=k.S_act,
                  deps=[pv] + tm.free)
        dv = P.op("dve", "tensor_tensor", out=row.ap[:, t0:t0 + ts], in0=banks[1].ap[:, :ts], in1=tm.ap[:, :ts],
                  op=ALU.mult, inc=k.S_dve, deps=[av] + row.free)
        tm.free = [dv]
        banks[0].free = [dv]
        banks[1].free = [dv]
        row.last = dv

    def epr(j):
        row = rows[j % 2]
        sv = P.op("sp", "dma_start", out=g["hidT"][j * 128:(j + 1) * 128, :], in_=row.ap, inc=k.s_st[j % 2],
                  deps=[row.last] + hid_deps)
        row.free = [sv]
        row_st[j % 2] = sv

    last = gemm_B(k, KC, jobs, T, epb, epr, wbase=0, wslots=2)
    k.dram_ready["hidT"] = [v for v in row_st if v is not None]
    k.Wslot[0].free = [last] + k.dram_ready["hidT"]
    k.Wslot[1].free = [last] + k.dram_ready["hidT"]
    parts = [(0, 29), (29, 29), (58, 28)]
    for (c0, cn) in parts:
        st_resid_gemm(k, f"down{l}", g["hidT"][c0 * 128:(c0 + cn) * 128, :], cn,
                      g[f"wd{l}"][c0 * 128:(c0 + cn) * 128, :], g["h"], g["h"],
                      a_deps=k.dram_ready["hidT"])
    k.dram_ready["hidT_free"] = [k.Aslot.free[0]]


def st_conv(k):
    P, g = k.P, k.g
    st_norm_T(k, g["h"], g["nmix"][1], T, src_deps=k.dram_ready.get("h", []))
    wv = g["cw_in"].rearrange("(c p) n -> p c n", p=128)
    jobs = [[(wv, j * 128), (wv, D + j * 128), (wv, 2 * D + j * 128)] for j in range(KC)]
    o = 18432
    z = k.WR[:, o:o + 2 * (T + 2)].bitcast(F32)
    o += 2 * (T + 2) + 12
    bgr = k.WR[:, o:o + 2 * T].bitcast(F32)
    o += 2 * T
    cr = k.WR[:, o:o + 2 * T].bitcast(F32)
    o += 2 * T
    assert o <= 32768, o
    tmp = [Slot(k.XR[:, T + i * 1024:T + (i + 1) * 1024].bitcast(F32)) for i in range(1)]
    assert T + 1024 <= k.xr_cols
    yrow = [Slot(k.XR[:, 0:T])]
    base_free = k.Wslot[0].free + k.Wslot[1].free
    for s in tmp:
        s.free = list(base_free)
    yrow[0].free = list(k.xr_free)
    zs = Slot(z)
    zs.free = list(base_free)
    cnt = [0]
    st = {"zlast": None, "sv": None, "alast": None}
    mz = P.op("dve", "memset", z[:, 0:2], 0.0, deps=base_free)

    def epb(j, bi, t0, ts, banks, pv):
        i = 0
        cnt[0] += 1
        tm = tmp[i]
        P.op("act", "activation", out=tm.ap[:, :ts], in_=banks[1].ap[:, :ts], func=AF.Copy,
             deps=[pv] + tm.free + zs.free)
        av = P.op("act", "activation", out=bgr[:, t0:t0 + ts], in_=banks[0].ap[:, :ts], func=AF.Copy)
        st["alast"] = av
        dv = P.op("dve", "tensor_tensor", out=z[:, 2 + t0:2 + t0 + ts], in0=banks[2].ap[:, :ts], in1=tm.ap[:, :ts],
                  op=ALU.mult, inc=k.S_dve, deps=[av] + zs.free)
        tm.free = [dv]
        for b in banks:
            b.free = [dv]
        st["zlast"] = dv

    def epr(j):
        y = yrow[0]
        c1 = P.op("dve", "tensor_scalar", out=cr, in0=z[:, 2:2 + T], scalar1=k.cw[:, 2 * KC + j:2 * KC + j + 1],
                  scalar2=None, op0=ALU.mult, deps=y.free + [st["zlast"]])
        c2 = P.op("dve", "scalar_tensor_tensor", out=cr, in0=z[:, 1:1 + T], scalar=k.cw[:, KC + j:KC + j + 1],
                  in1=cr, op0=ALU.mult, op1=ALU.add, deps=[c1])
        c3 = P.op("dve", "scalar_tensor_tensor", out=cr, in0=z[:, 0:T], scalar=k.cw[:, j:j + 1],
                  in1=cr, op0=ALU.mult, op1=ALU.add, deps=[c2])
        dv = P.op("dve", "tensor_tensor", out=y.ap, in0=cr, in1=bgr, op=ALU.mult, deps=[c3, st["alast"]])
        zs.free = [dv]
        sv = P.op("sp", "dma_start", out=g["yT"][j * 128:(j + 1) * 128, :], in_=y.ap, inc=k.s_st[2], deps=[dv])
        y.free = [sv]
        st["sv"] = sv

    last = gemm_B(k, KC, jobs, T, epb, epr, wbase=0, wslots=3, ksplit=2)
    k.dram_ready["yT"] = [st["sv"]]
    k.skip_halo = True
    k.xr_free = [st["sv"]]
    k.Wslot[0].free = [last, st["sv"]]
    k.Wslot[1].free = [last, st["sv"]]
    st_resid_gemm(k, "cw_out", g["yT"], KC, g["cw_out"], g["h"], g["h"], a_deps=k.dram_ready["yT"])


def st_final(k):
    P, g = k.P, k.g
    xb = [Slot(wr_f32(k, 0, 4096)), Slot(wr_f32(k, 8192, 4096))]
    gbc = wr_f32(k, 16384, 4096)
    junk = k.WR[:, 24576:28672]
    wfree = k.Wslot[0].free + k.Wslot[1].free
    for s in xb:
        s.free = list(wfree)
    gv = P.op("sp", "dma_start", out=gbc, in_=g["nfin"].partition_broadcast(128), inc=k.s_gain, deps=wfree)
    hdeps = list(k.dram_ready.get("h", []))
    svs = [None, None]
    for tt in range(NOUT // 128):
        t0 = CH + tt * 128
        b = tt % 2
        x = xb[b]
        lv = P.op("sp", "dma_start", out=x.ap, in_=g["h"][t0:t0 + 128, :], inc=k.s_ld[b], deps=x.free + hdeps)
        sq = P.op("act", "activation", out=junk, in_=x.ap, func=AF.Square, accum_out=k.ss[:, tt:tt + 1],
                  deps=[lv] + wfree)
        av = P.op("act", "activation", out=k.sd[:, tt:tt + 1], in_=k.ss[:, tt:tt + 1], func=AF.Sqrt,
                  scale=1.0 / D, bias=k.epsb, deps=[sq])
        rv = P.op("dve", "reciprocal", out=k.rstd[:, tt:tt + 1], in_=k.sd[:, tt:tt + 1], deps=[av, gv])
        dv = P.op("dve", "scalar_tensor_tensor", out=x.ap, in0=x.ap, scalar=k.rstd[:, tt:tt + 1], in1=gbc,
                  op0=ALU.mult, op1=ALU.mult, deps=[rv])
        sv = P.op("sp", "dma_start", out=g["out"][tt * 128:(tt + 1) * 128, :], in_=x.ap, inc=k.s_st[b], deps=[dv])
        x.free = [sv]
        svs[b] = sv
    P.wait("sp", [v for v in svs if v is not None])


def st_copy_h(k):
    P, g = k.P, k.g
    vals = []
    for i in range(0, T, 264):
        v = P.op("sp", "dma_start", out=g["h"][i:i + 264, :], in_=g["xo"][i:i + 264, :], inc=k.s_cp)
        vals.append(v)
    k.dram_ready["h"] = [vals[-1]]


def from_stages(k, stages):
    k.xr_free = []
    st_setup(k)
    k.xr_free = list(k.setup_done)
    for s in stages:
        if s == "copy_h":
            st_copy_h(k)
        elif s == "gla":
            import_gla(k)
        elif s == "ffn0":
            st_ffn(k, 0)
        elif s == "conv":
            st_conv(k)
        elif s == "ffn1":
            st_ffn(k, 1)
        elif s == "final":
            st_final(k)
        else:
            raise ValueError(s)


def import_gla(k):
    st_gla(k)


def st_gla_inproj(k, src, ntok, row0, ch0, own):
    P, g = k.P, k.g
    st_norm_T(k, src, g["nmix"][0], ntok)
    wv = g["w_in0"].rearrange("(c p) n -> p c n", p=128)
    base_free = k.Wslot[0].free + k.Wslot[1].free
    alowT = k.WR[:16, 16384:16384 + T]
    wa2b = k.WR[:16, 18496:18496 + QK]
    wa2f = k.WR[:16, 20544:20544 + 2 * QK].bitcast(F32)
    o = 24640
    tl = k.WR[:, o:o + 1024].bitcast(F32); o += 1024
    tB = k.WR[:, o:o + 1024].bitcast(F32); o += 1024
    teq = k.WR[:, o:o + 1024].bitcast(F32); o += 1024
    tek = k.WR[:, o:o + 1024].bitcast(F32); o += 1024
    tk = k.WR[:, o:o + 1024].bitcast(F32); o += 1024
    tkh2 = [k.WR[:, o:o + 512], k.XR[:, 1536:2048]]; o += 512
    qb = [Slot(k.WR[:, o + i * 512:o + (i + 1) * 512]) for i in range(2)]; o += 1024
    kb = [Slot(k.WR[:, o + i * 512:o + (i + 1) * 512]) for i in range(2)]; o += 1024
    assert o <= 32768, o
    khs = [Slot(k.XR[:, i * 512:(i + 1) * 512].rearrange("p (a d) -> p a d", d=128)) for i in range(2)]
    mask = k.XR[:, 1024:1536]
    qs = k.XR[:, 2048:3072].bitcast(F32)
    qs_free = list(k.xr_free)
    assert k.xr_cols >= 3072
    for s_ in qb + kb:
        s_.free = list(base_free)
    for s_ in khs:
        s_.free = list(k.xr_free)
    decay3 = k.decay.rearrange("p (j c) -> p j c", c=NCT)
    if own:
        qk_blocks = [(0, CH)] + [(CH + 512 * i, 512) for i in range(4)]
    else:
        qk_blocks = tok_blocks(ntok)

    def chunk_of(t0):
        if not own:
            return CSP, t0 // CSP
        if t0 == 0:
            return CH, NPRE
        return CS, NPRE + 1 + (t0 - CH) // CS
    lv = P.op("sp", "dma_start", out=wa2f, in_=g["w_a2"], inc=k.s_misc, deps=base_free)
    wa_v = P.op("dve", "tensor_copy", out=wa2b, in_=wa2f, deps=[lv] + base_free)
    m1 = P.op("dve", "memset", mask, 1.0, deps=list(k.xr_free))
    m2 = P.op("dve", "memset", mask.rearrange("p (c i) -> p c i", i=(CS if own else CSP))[:, :, 0:1], 0.0,
              deps=[m1])
    st = {"tfree": list(base_free) + [m2], "tkh_free": [list(base_free), list(base_free) + list(k.xr_free)],
          "pending": [], "cnt": 0, "kst": [], "alow": None}

    def ep_alow(j, bi, t0, ts, banks, pv):
        av = P.op("act", "activation", out=alowT[:, t0:t0 + ts], in_=banks[0].ap[:16, :ts], func=AF.Copy,
                  deps=[pv] + base_free)
        banks[0].free = [av]
        st["alow"] = av

    gemm_B(k, KC, [[(wv, 2 * QK + 2 * VV)]], ntok, ep_alow, None, wbase=0, wslots=2, mrows=16)
    alow_ready = [st["alow"], wa_v]

    def pre_block(j, bi, t0, ts):
        bx = next_bank(k)
        P.wait("pe", alow_ready + bx.free)
        P.op("pe", "matmul", bx.ap[:, :ts], wa2b[:, j * 128:(j + 1) * 128], alowT[:, t0:t0 + ts],
             start=True, stop=True, inc=None)
        return [bx]

    def flush_pending(keep=0):
        while len(st["pending"]) > keep:
            st["pending"].pop(0)()

    def epb(j, bi, t0, ts, banks, pv):
        flush_pending(1)
        if own:
            bx, bq, bk = banks
        else:
            bx, bk = banks
            bq = None
        csz, cb = chunk_of(t0)
        nchb = ts // csz
        kc_ = P.op("dve", "tensor_copy", out=tk[:, :ts], in_=bk.ap[:, :ts], deps=[pv] + st["tfree"])
        bk.free = [kc_]
        if own:
            qc_ = P.op("dve", "tensor_copy", out=qs[:, :ts], in_=bq.ap[:, :ts], deps=[pv] + st["tfree"] + qs_free)
            bq.free = [qc_]
        e1 = P.op("act", "activation", out=tl[:, :ts], in_=bx.ap[:, :ts], func=AF.Exp, scale=-1.0,
                  bias=k.nba[:, j:j + 1], deps=[pv] + st["tfree"])
        bx.free = [e1]
        l1 = P.op("act", "activation", out=tl[:, :ts], in_=tl[:, :ts], func=AF.Ln, scale=1.0, bias=k.oneb,
                  deps=[e1])
        sc = P.op("dve", "tensor_tensor_scan", out=tB[:, :ts], data0=mask[:, :ts], data1=tl[:, :ts], initial=0.0,
                  op0=ALU.mult, op1=ALU.add, deps=[l1] + st["tfree"])
        eqv = P.op("act", "activation", out=teq[:, :ts], in_=tB[:, :ts], func=AF.Exp, scale=-1.0 / 16, deps=[sc])
        ekv = P.op("act", "activation", out=tek[:, :ts], in_=tB[:, :ts], func=AF.Exp, scale=1.0 / 16)
        dcv = P.op("act", "activation", out=decay3[:, j, cb:cb + nchb],
                   in_=tB[:, :ts].rearrange("p (c i) -> p c i", i=csz)[:, :, csz - 1], func=AF.Exp, scale=-1.0 / 16)
        i = st["cnt"] % 2
        st["cnt"] += 1
        tkh = tkh2[i]
        lastd = None
        if own:
            q_ = qb[i]
            qv = P.op("dve", "scalar_tensor_tensor", out=q_.ap[:, :ts], in0=qs[:, :ts], scalar=float(DK) ** -0.5,
                      in1=teq[:, :ts], op0=ALU.mult, op1=ALU.mult, deps=[eqv, qc_] + q_.free)
            sv = P.op("sp", "dma_start", out=g["qT"][j * 128:(j + 1) * 128, t0:t0 + ts], in_=q_.ap[:, :ts],
                      inc=k.s_st[i], deps=[qv])
            q_.free = [sv]
            st["kst"].append(sv)
        tkv = P.op("dve", "tensor_tensor", out=tk[:, :ts], in0=tk[:, :ts], in1=tek[:, :ts], op=ALU.mult,
                   deps=[ekv, kc_])
        if own:
            k_ = kb[i]
            kv = P.op("act", "activation", out=k_.ap[:, :ts], in_=tk[:, :ts], func=AF.Copy, deps=[tkv] + k_.free)
            sv = P.op("sp", "dma_start", out=g["kT"][j * 128:(j + 1) * 128, t0:t0 + ts], in_=k_.ap[:, :ts],
                      inc=k.s_st[2 + i], deps=[kv])
            k_.free = [sv]
            st["kst"].append(sv)
            lastd = kv
        khv = P.op("dve", "tensor_tensor", out=tkh[:, :ts].rearrange("p (c i) -> p c i", i=csz),
                   in0=tk[:, :ts].rearrange("p (c i) -> p c i", i=csz),
                   in1=decay3[:, j, cb:cb + nchb].unsqueeze(2).broadcast_to([128, nchb, csz]),
                   op=ALU.mult, deps=[tkv, dcv] + st["tkh_free"][i])
        st["tfree"] = [khv] + ([lastd] if lastd is not None else [])

        def pend(j=j, t0=t0, ts=ts, khv=khv, i=i, tkh=tkh):
            bank = next_bank(k)
            pb = bank.ap.bitcast(BF16)
            tls = tok_tiles(ts)
            P.wait("pe", [khv] + bank.free)
            for ti, (a0, asz) in enumerate(tls):
                pv2 = P.op("pe", "transpose", out=pb[:asz, ti * 128:(ti + 1) * 128], in_=tkh[:, a0:a0 + asz],
                           identity=k.identb, inc=(k.S_pe if ti == len(tls) - 1 else None))
            st["tkh_free"][i] = [pv2]
            hs = khs[i]
            nt_ = len(tls)
            asz = tls[0][1]
            ev = P.op("act", "activation", out=hs.ap[:asz, :nt_, :],
                      in_=pb[:asz, :nt_ * 128].rearrange("p (a d) -> p a d", d=128), func=AF.Copy,
                      deps=[pv2] + hs.free)
            bank.free = [ev]
            r0 = row0 + t0
            if asz == 128:
                dst = g["kh"][r0:r0 + ts, j * 128:(j + 1) * 128].rearrange("(a p) d -> p a d", p=128)
            else:
                dst = g["kh"][r0:r0 + ts, j * 128:(j + 1) * 128].rearrange("(a p) d -> p a d", p=asz)
            sv = P.op("sp", "dma_start", out=dst, in_=hs.ap[:asz, :nt_, :], inc=k.s_st[4 + i], deps=[ev])
            hs.free = [sv]
            st["kst"].append(sv)

        st["pending"].append(pend)

    if own:
        jobs = [[(wv, j * 128), (wv, QK + j * 128)] for j in range(16)]
    else:
        jobs = [[(wv, QK + j * 128)] for j in range(16)]
    k.Aslot.free = []
    last = gemm_B(k, KC, jobs, ntok, epb, None, wbase=0, wslots=2, pre_block=pre_block, blocks=qk_blocks)
    flush_pending(0)
    fin = [last] + st["tfree"] + st["tkh_free"][0] + st["tkh_free"][1]
    for s_ in qb + kb + khs:
        fin += s_.free
    k.Wslot[0].free = list(fin)
    k.Wslot[1].free = list(fin)
    k.xr_free = list(fin)
    k.dram_ready["qk"] = k.dram_ready.get("qk", []) + [v for v in fin if v[0].name.startswith("s_st")]

    vst = [Slot(k.XR[:, i * 512:(i + 1) * 512]) for i in range(3)]
    for s_ in vst:
        s_.free = list(k.xr_free)
    cnt = [0]
    vals = []

    def mk_ep(dst, func, r0):
        def ep(nt, c0, tt, t0, ts, bank, pv):
            i = cnt[0] % 3
            cnt[0] += 1
            s_ = vst[i]
            av = P.op("act", "activation", out=s_.ap[:ts], in_=bank.ap[:ts, :], func=func, deps=[pv] + s_.free)
            bank.free = [av]
            sv = P.op("sp", "dma_start", out=dst[r0 + t0:r0 + t0 + ts, c0:c0 + 512], in_=s_.ap[:ts],
                      inc=k.s_st[i], deps=[av])
            s_.free = [sv]
            vals.append(sv)
        return ep

    k.Aslot.free = []
    gemm_A(k, KC, g["w_in0"][:, 2 * QK:2 * QK + VV], VV, ntok, mk_ep(g["v"], AF.Copy, row0))
    if own:
        k.Aslot.free = []
        gemm_A(k, KC, g["w_in0"][:, 2 * QK + VV:2 * QK + 2 * VV], VV, ntok, mk_ep(g["sr"], AF.Silu, 0))
    fin2 = []
    for s_ in vst:
        fin2 += s_.free
    k.xr_free = fin2
    k.dram_ready["qk"] = k.dram_ready.get("qk", []) + fin2


def st_gla_scan(k):
    P, g = k.P, k.g
    base_free = k.Wslot[0].free + k.Wslot[1].free + k.Aslot.free + k.xr_free
    ddeps = list(k.dram_ready["qk"])
    decay3 = k.decay.rearrange("p (j c) -> p j c", c=NCT)
    Sf_flat = k.WR.bitcast(F32)
    Sf = Sf_flat.rearrange("p (h c v) -> p h c v", h=H, c=4)
    Sb_flat = k.A[:, 0:16384]
    Sb = Sb_flat.rearrange("p (h c v) -> p h c v", h=H, c=4)
    o = 16384
    slots = []
    for i in range(2):
        d = {}
        d["kh2"] = k.A[:, o:o + 2 * QK].rearrange("p (a d) -> p a d", a=2)
        d["v2"] = k.A[:, o + 2 * QK:o + 2 * QK + 2 * VV].rearrange("p (a d) -> p a d", a=2)
        d["qT"] = k.A[:, o:o + 16 * CS].rearrange("p (j t) -> p j t", t=CS); o += 16 * CS
        d["kT"] = k.A[:, o:o + 16 * CS].rearrange("p (j t) -> p j t", t=CS); o += 16 * CS
        d["kh"] = k.A[:, o:o + QK]; o += QK
        d["v"] = k.A[:, o:o + VV]; o += VV
        d["sr"] = k.A[:, o:o + VV]; o += VV
        d["free"] = list(base_free)
        slots.append(d)
    hnbc = k.A[:, o:o + 2 * DV].bitcast(F32); o += 2 * DV
    AT = [[k.A[:, o + (i * 4 + h) * CS:o + (i * 4 + h + 1) * CS] for h in range(H)] for i in range(2)]
    o += 8 * CS
    og = [k.A[:, o + i * VV:o + (i + 1) * VV] for i in range(2)]; o += 2 * VV
    ogT = [Slot(k.A[:, o + i * 4096:o + (i + 1) * 4096].rearrange("p (j t) -> p j t", t=CS)) for i in range(2)]
    o += 8192
    sqj = k.A[:, o:o + DV]; o += DV
    assert o <= KC * T, o
    for s_ in ogT:
        s_.free = list(base_free)
    z1 = P.op("dve", "memset", Sf_flat, 0.0, deps=base_free)
    z2 = P.op("pool", "memset", Sb_flat, 0.0, deps=base_free)
    hv = P.op("sp", "dma_start", out=hnbc, in_=g["hnorm"].partition_broadcast(CS), inc=k.s_gain, deps=base_free)
    sb_ready = [[z2] for _ in range(H)]
    sf_last = [[z1] for _ in range(H)]
    og_free = [list(base_free), list(base_free)]
    at_free = [list(base_free), list(base_free)]
    o_reads = [[] for _ in range(H)]
    sb_a = [None] * H
    sb_p = [None] * H
    st_vals = []
    n_own = 0
    chunks = [(CSP * c, CSP, False, 0) for c in range(NPRE)]
    chunks.append((TP, CH, True, 0))
    chunks += [(TP + CH + CS * i, CS, True, CH + CS * i) for i in range((T - CH) // CS)]
    assert len(chunks) == NCT
    n_pre = NPRE

    def emit_loads(c):
        r0, cs, own, tl0 = chunks[c]
        sl = slots[c % 2]
        sem_a = k.s_ld[(c % 2) * 2]
        sem_b = k.s_ld[(c % 2) * 2 + 1]
        if not own:
            P.op("sp", "dma_start", out=sl["kh2"], in_=g["kh"][r0:r0 + cs, :].rearrange("(a p) d -> p a d", p=128),
                 inc=sem_a, deps=sl["free"] + ddeps)
            ld_a = P.op("sp", "dma_start", out=sl["v2"],
                        in_=g["v"][r0:r0 + cs, :].rearrange("(a p) d -> p a d", p=128), inc=sem_a)
            return ld_a, None
        P.op("sp", "dma_start", out=sl["kh"][:cs], in_=g["kh"][r0:r0 + cs, :], inc=sem_a, deps=sl["free"] + ddeps)
        ld_a = P.op("sp", "dma_start", out=sl["v"][:cs], in_=g["v"][r0:r0 + cs, :], inc=sem_a)
        ld_b = None
        if own:
            P.op("sp", "dma_start", out=sl["qT"][:, :, :cs],
                 in_=g["qT"][:, tl0:tl0 + cs].rearrange("(j p) t -> p j t", p=128), inc=sem_b)
            P.op("sp", "dma_start", out=sl["kT"][:, :, :cs],
                 in_=g["kT"][:, tl0:tl0 + cs].rearrange("(j p) t -> p j t", p=128), inc=sem_b)
            ld_b = P.op("sp", "dma_start", out=sl["sr"][:cs], in_=g["sr"][tl0:tl0 + cs, :], inc=sem_b)
        return ld_a, ld_b

    nxt = emit_loads(0)
    for c, (r0, cs, own, tl0) in enumerate(chunks):
        sl = slots[c % 2]
        ld_a, ld_b = nxt
        if c + 1 < NCT:
            nxt = emit_loads(c + 1)
        reads = []
        if own:
            oi = n_own % 2
            gv = P.op("pool", "tensor_tensor", out=sl["sr"][:cs].rearrange("p (h v) -> p h v", h=H),
                      in0=sl["sr"][:cs].rearrange("p (h v) -> p h v", h=H),
                      in1=hnbc[:cs].unsqueeze(1).broadcast_to([cs, H, DV]), op=ALU.mult, deps=[ld_b, hv])
            sc_vals = []
            for h in range(H):
                bank = next_bank(k)
                P.wait("pe", [ld_b] + bank.free)
                for kc in range(4):
                    pv = P.op("pe", "matmul", bank.ap[:cs, :cs], sl["kT"][:, 4 * h + kc, :cs],
                              sl["qT"][:, 4 * h + kc, :cs], start=(kc == 0), stop=(kc == 3),
                              inc=(k.S_pe if kc == 3 else None))
                mv = P.op("dve", "tensor_tensor", out=AT[oi][h][:cs, :cs], in0=bank.ap[:cs, :cs],
                          in1=k.tril[:cs, :cs], op=ALU.mult, deps=[pv] + at_free[oi])
                bank.free = [mv]
                sc_vals.append(mv)
            tr_list = []
            for h in range(H):
                if k.bank_i % 2 == 1:
                    k.bank_i += 1
                b0 = next_bank(k)
                b1 = next_bank(k)
                pair = k.psall[:cs, b0.idx * 512:b0.idx * 512 + 1024]
                P.wait("pe", [ld_a, sc_vals[h]] + b0.free + b1.free + sb_ready[h])
                for vh, bnk in enumerate((b0, b1)):
                    for kc in range(4):
                        P.op("pe", "matmul", bnk.ap[:cs, :], sl["qT"][:, 4 * h + kc, :cs],
                             Sb[:, h, kc, vh * 512:(vh + 1) * 512], start=(kc == 0), stop=False, inc=None)
                    pv = P.op("pe", "matmul", bnk.ap[:cs, :], AT[oi][h][:cs, :cs],
                              sl["v"][:cs, h * DV + vh * 512:h * DV + (vh + 1) * 512], start=False, stop=True,
                              inc=(k.S_pe if vh == 1 else None))
                o_reads[h] = [pv]
                col = oi * 4 + h
                sq = P.op("act", "activation", out=sqj[:cs], in_=pair, func=AF.Square,
                          accum_out=k.ss[:cs, col:col + 1], deps=[pv])
                sdv = P.op("act", "activation", out=k.sd[:cs, col:col + 1], in_=k.ss[:cs, col:col + 1],
                           func=AF.Sqrt, scale=1.0 / DV, bias=k.epsb[:cs], deps=[sq])
                rv = P.op("dve", "reciprocal", out=k.rstd[:cs, col:col + 1], in_=k.sd[:cs, col:col + 1], deps=[sdv])
                ov = P.op("dve", "scalar_tensor_tensor", out=og[oi][:cs, h * DV:(h + 1) * DV], in0=pair,
                          scalar=k.rstd[:cs, col:col + 1], in1=sl["sr"][:cs, h * DV:(h + 1) * DV],
                          op0=ALU.mult, op1=ALU.mult, deps=[rv, gv] + og_free[oi])
                b0.free = [ov]
                b1.free = [ov]
                tr_list.append(ov)
                reads.append(ov)
        last_pe_read = None
        sfv = None
        for h in range(H):
            for kc in range(4):
                for vh in range(2):
                    bank = next_bank(k)
                    P.wait("pe", [ld_a] + bank.free)
                    if own:
                        pv = P.op("pe", "matmul", bank.ap[:, :],
                                  sl["kh"][:cs, (4 * h + kc) * 128:(4 * h + kc + 1) * 128],
                                  sl["v"][:cs, h * DV + vh * 512:h * DV + (vh + 1) * 512], start=True, stop=True,
                                  inc=k.S_pe)
                    else:
                        for a_ in range(2):
                            pv = P.op("pe", "matmul", bank.ap[:, :],
                                      sl["kh2"][:, a_, (4 * h + kc) * 128:(4 * h + kc + 1) * 128],
                                      sl["v2"][:, a_, h * DV + vh * 512:h * DV + (vh + 1) * 512],
                                      start=(a_ == 0), stop=(a_ == 1), inc=(k.S_pe if a_ == 1 else None))
                    sfv = P.op("dve", "scalar_tensor_tensor", out=Sf[:, h, kc, vh * 512:(vh + 1) * 512],
                               in0=Sf[:, h, kc, vh * 512:(vh + 1) * 512], scalar=decay3[:, 4 * h + kc, c:c + 1],
                               in1=bank.ap[:, :], op0=ALU.mult, op1=ALU.add, deps=[pv] + sf_last[h])
                    bank.free = [sfv]
                    if c >= n_pre - 1 and c < NCT - 1:
                        cv = P.op("act", "activation", out=Sb[:, h, kc, vh * 512:(vh + 1) * 512],
                                  in_=Sf[:, h, kc, vh * 512:(vh + 1) * 512], func=AF.Copy,
                                  deps=[sfv] + o_reads[h])
                        sb_ready[h] = [cv]
                    last_pe_read = pv
            sf_last[h] = [sfv]
        reads.append(last_pe_read)
        if own:
            oslot = ogT[n_own % 2]
            ev = None
            pv = None
            for h in range(H):
                bank = next_bank(k)
                pb = bank.ap.bitcast(BF16)
                P.wait("pe", [tr_list[h]] + bank.free)
                for kk in range(8):
                    pv = P.op("pe", "transpose", out=pb[:, kk * cs:(kk + 1) * cs],
                              in_=og[oi][:cs, h * DV + kk * 128:h * DV + (kk + 1) * 128], identity=k.identb[:cs, :cs],
                              inc=(k.S_pe if kk == 7 else None))
                ev = P.op("act", "activation", out=oslot.ap[:, h * 8:(h + 1) * 8, :cs],
                          in_=pb[:, :8 * cs].rearrange("p (a t) -> p a t", t=cs), func=AF.Copy,
                          deps=[pv] + oslot.free)
                bank.free = [ev]
            og_free[oi] = [pv]
            at_free[oi] = list(o_reads[H - 1])
            sv = P.op("sp", "dma_start",
                      out=g["ogT"][:, tl0:tl0 + cs].rearrange("(j p) t -> p j t", p=128),
                      in_=oslot.ap[:, :, :cs], inc=k.s_st[n_own % 2], deps=[ev])
            oslot.free = [sv]
            st_vals.append(sv)
            n_own += 1
        sl["free"] = reads
    k.dram_ready["ogT"] = st_vals[-2:]
    fin = st_vals[-2:] + [last_pe_read] + sf_last[H - 1]
    k.Wslot[0].free = list(fin)
    k.Wslot[1].free = list(fin)
    k.Aslot.free = list(fin)
    k.xr_free = list(fin)


def st_gla(k):
    g = k.g
    st_gla_inproj(k, g["xp"], TP, 0, 0, own=False)
    st_gla_inproj(k, g["xo"], T, TP, NCHP, own=True)
    st_gla_scan(k)
    st_resid_gemm(k, "w_out0", g["ogT"], KC, g["w_out0"], g["xo"], g["h"], a_deps=k.dram_ready["ogT"])


ALL_STAGES = ["gla", "ffn0", "conv", "ffn1", "final"]


def shared_inputs(inputs):
    f = np.float32
    sh = {}
    sh["w_in0"] = np.ascontiguousarray(inputs["gla_w_in"][0], dtype=f)
    sh["w_a2"] = np.ascontiguousarray(inputs["gla_w_a2"][0], dtype=f)
    sh["b_a_t"] = np.ascontiguousarray(np.asarray(inputs["gla_b_a"][0], dtype=f).reshape(16, 128).T)
    sh["hnorm"] = np.ascontiguousarray(inputs["gla_head_norm"][0], dtype=f)
    sh["w_out0"] = np.ascontiguousarray(inputs["gla_w_out"][0], dtype=f)
    sh["cw_in"] = np.ascontiguousarray(inputs["conv_w_in"][0], dtype=f)
    sh["cw_t"] = np.ascontiguousarray(
        np.asarray(inputs["conv_w"][0], dtype=f).reshape(3, KC, 128).transpose(2, 0, 1).reshape(128, 3 * KC))
    sh["cw_out"] = np.ascontiguousarray(inputs["conv_w_out"][0], dtype=f)
    for l in range(2):
        sh[f"wg{l}"] = np.ascontiguousarray(inputs["ffn_w_gate"][l], dtype=f)
        sh[f"wu{l}"] = np.ascontiguousarray(inputs["ffn_w_up"][l], dtype=f)
        sh[f"wd{l}"] = np.ascontiguousarray(inputs["ffn_w_down"][l], dtype=f)
    sh["nmix"] = np.ascontiguousarray(inputs["norm_mix"], dtype=f)
    sh["nffn"] = np.ascontiguousarray(inputs["norm_ffn"], dtype=f)
    sh["nfin"] = np.ascontiguousarray(inputs["norm_final"], dtype=f)
    sh["ident"] = np.eye(128, dtype=f)
    sh["tril"] = np.triu(np.ones((CS, CS), dtype=f))
    return sh


def core_tokens(x_b, meta):
    f = np.float32
    seq = np.concatenate([np.zeros((CH - meta.shape[0], D), f), np.asarray(meta, f), np.asarray(x_b, f)], axis=0)
    a = {"xo": np.ascontiguousarray(seq[0:T]), "xp": np.zeros((TP, D), f)}
    b = {"xo": np.ascontiguousarray(seq[TP:TP + T]), "xp": np.ascontiguousarray(seq[0:TP])}
    return a, b


_NC_CACHE = {}


def kernel(**inputs):
    x = np.asarray(inputs["x"])
    B = x.shape[0]
    sh = shared_inputs(inputs)
    in_maps = []
    for b in range(B):
        a, c = core_tokens(x[b], inputs["meta"])
        for t in (a, c):
            m = dict(sh)
            m.update(t)
            in_maps.append(m)
    if "nc" not in _NC_CACHE:
        _NC_CACHE["nc"] = build(ALL_STAGES)
    nc = _NC_CACHE["nc"]
    res = run_bass_kernel_spmd(nc, in_maps, core_ids=list(range(2 * B)))
    out = np.empty((B, 2 * NOUT, D), np.float32)
    for b in range(B):
        for hf in range(2):
            out[b, hf * NOUT:(hf + 1) * NOUT] = res.results[2 * b + hf]["out"]
    return out
```
